# Optimizing a Trainium2 kernel written in Bass

```python
import math
import jax, jax.numpy as jnp
from jax import lax
import numpy as np

D_MODEL = 1024
BATCH = 8
SEQ = 2048
DEPTH = 2
DEC_BATCH = 32
DEC_SEQ = 8
PAST_LEN = 8192
PAGE_SIZE = 128

N_MEM = 256
MEM_HEADS = 4
MEM_HEAD_DIM = D_MODEL // MEM_HEADS
MIX_WIDTH = D_MODEL
GDN_HEADS = 4
GDN_HEAD_DIM = MIX_WIDTH // (2 * GDN_HEADS)
GDN_WIDTH = GDN_HEADS * GDN_HEAD_DIM
GDN_CONV_CH = 3 * GDN_WIDTH
CONV_WIDTH = 4
GDN_CHUNK = 64
GDN_IN = 4 * GDN_WIDTH + 2 * GDN_HEADS
RWKV_HEAD_DIM = 64
RWKV_WIDTH = MIX_WIDTH - GDN_WIDTH
RWKV_HEADS = RWKV_WIDTH // RWKV_HEAD_DIM
DECAY_LORA = 64
ICLR_LORA = 64
GATE_LORA = 128
RWKV_IN = 3 * RWKV_WIDTH + DECAY_LORA + ICLR_LORA + GATE_LORA
EVEN_IN = GDN_IN + RWKV_IN
SB_HEADS = 16
SB_HEAD_DIM = MIX_WIDTH // SB_HEADS
Q_BLOCK = 128
SB_BIAS_INIT = -8.0
D_FF = ((8 * D_MODEL + 3 * 256 - 1) // (3 * 256)) * 256
N_EVEN = (DEPTH + 1) // 2
N_ODD = DEPTH // 2
EPS = 1e-6
RWKV_GN_EPS = 64e-5

kernel_name = 'hybrid_gdn_rwkv7_stickbreak_step'


def rmsnorm(x, g):
    xf = x.astype(jnp.float32)
    y = xf * lax.rsqrt(jnp.mean(xf * xf, axis=-1, keepdims=True) + EPS)
    return (y * g.astype(jnp.float32)).astype(x.dtype)


def l2norm(x):
    xf = x.astype(jnp.float32)
    return xf * lax.rsqrt(jnp.sum(xf * xf, axis=-1, keepdims=True) + EPS)


def causal_conv(u, buf, w):
    t = u.shape[1]
    ext = jnp.concatenate([buf.astype(u.dtype), u], axis=1)
    out = ext[:, 0:t] * w[0]
    for j in range(1, CONV_WIDTH):
        out = out + ext[:, j:j + t] * w[j]
    return out, ext[:, t:]


def gated_delta_chunked(q, k, v, beta, g, s0):
    b, t, h, dk = k.shape
    dv = v.shape[-1]
    c = GDN_CHUNK
    n = -(-t // c)
    pad = n * c - t

    def to_chunks(a):
        a = jnp.pad(a, [(0, 0), (0, pad)] + [(0, 0)] * (a.ndim - 2))
        a = a.reshape((b, n, c) + a.shape[2:])
        return jnp.swapaxes(jnp.moveaxis(a, 1, 0), 2, 3)

    qc, kc, vc, bc, gc = (to_chunks(a) for a in (q, k, v, beta, g))
    incl = jnp.tril(jnp.ones((c, c), dtype=bool))
    strict = jnp.tril(jnp.ones((c, c), dtype=bool), -1)
    eye = jnp.eye(c, dtype=jnp.float32)

    def chunk_step(s, inp):
        qb, kb, vb, bb, gb = inp
        gcum = jnp.cumsum(gb, axis=-1)
        gam = jnp.exp(gcum)
        decay = jnp.exp(jnp.where(incl, gcum[..., :, None] - gcum[..., None, :], -jnp.inf))
        kkt = jnp.einsum('bhik,bhjk->bhij', kb, kb)
        lmat = jnp.where(strict, bb[..., :, None] * kkt * decay, 0.0)
        rhs = bb[..., None] * (vb - gam[..., None] * jnp.einsum('bhck,bhkv->bhcv', kb, s))
        u = lax.linalg.triangular_solve(lmat + eye, rhs, left_side=True, lower=True, unit_diagonal=True)
        qkt = jnp.einsum('bhik,bhjk->bhij', qb, kb) * decay
        o = gam[..., None] * jnp.einsum('bhck,bhkv->bhcv', qb, s) + jnp.einsum('bhij,bhjv->bhiv', qkt, u)
        glast = gcum[..., -1:]
        s_new = jnp.exp(glast)[..., None] * s + jnp.einsum('bhck,bhcv->bhkv', kb * jnp.exp(glast - gcum)[..., None], u)
        return s_new, o

    s_fin, oc = lax.scan(chunk_step, s0, (qc, kc, vc, bc, gc))
    o = jnp.moveaxis(jnp.swapaxes(oc, 2, 3), 0, 1).reshape(b, n * c, h, dv)[:, :t]
    return o, s_fin


def rwkv7_scan(r, w, k, v, kk, a, s0):
    def step(s, inp):
        rt, wt, kt, vt, kkt, at = inp
        sa = jnp.einsum('bhvk,bhk->bhv', s, -kkt)
        s = s * wt[:, :, None, :] + sa[..., None] * (kkt * at)[:, :, None, :] + vt[..., None] * kt[:, :, None, :]
        return s, jnp.einsum('bhvk,bhk->bhv', s, rt)

    xs = tuple(jnp.moveaxis(u, 1, 0) for u in (r, w, k, v, kk, a))
    s_fin, ys = lax.scan(step, s0, xs)
    return jnp.moveaxis(ys, 0, 1), s_fin


def even_mixer(h, w_in, w_out, conv_w, a_log, dt_bias, gdn_norm, mu, w0, w2, a0, a2, g2, k_k, k_a, r_k, gn_g, gn_b,
               conv_buf, s_gdn, s_rwkv, shift_buf):
    b, t, _ = h.shape
    f32 = jnp.float32
    hn = (RWKV_HEADS, RWKV_HEAD_DIM)
    proj = h @ w_in
    gp, rp = proj[..., :GDN_IN], proj[..., GDN_IN:]
    qkv, conv_new = causal_conv(gp[..., :GDN_CONV_CH], conv_buf, conv_w)
    qkv = jax.nn.silu(qkv).reshape(b, t, 3, GDN_HEADS, GDN_HEAD_DIM)
    q = l2norm(qkv[:, :, 0]) * (GDN_HEAD_DIM ** -0.5)
    k = l2norm(qkv[:, :, 1])
    v = qkv[:, :, 2].astype(f32)
    z = gp[..., GDN_CONV_CH:4 * GDN_WIDTH].reshape(b, t, GDN_HEADS, GDN_HEAD_DIM).astype(f32)
    beta = jax.nn.sigmoid(gp[..., 4 * GDN_WIDTH:4 * GDN_WIDTH + GDN_HEADS].astype(f32))
    g = -jnp.exp(a_log.astype(f32)) * jax.nn.softplus(gp[..., 4 * GDN_WIDTH + GDN_HEADS:].astype(f32) + dt_bias.astype(f32))
    o_a, s_gdn_new = gated_delta_chunked(q, k, v, beta, g, s_gdn.astype(f32))
    o_a = (rmsnorm(o_a, gdn_norm) * jax.nn.silu(z)).reshape(b, t, GDN_WIDTH).astype(h.dtype)
    prev = jnp.concatenate([shift_buf[:, None].astype(rp.dtype), rp[:, :-1]], axis=1)
    xr = rp + (prev - rp) * mu
    shift_new = rp[:, -1]
    wd = RWKV_WIDTH
    off = 3 * wd
    r = xr[..., :wd]
    kr = xr[..., wd:2 * wd]
    vr = xr[..., 2 * wd:3 * wd]
    pw = xr[..., off:off + DECAY_LORA]
    pa = xr[..., off + DECAY_LORA:off + DECAY_LORA + ICLR_LORA]
    pg = xr[..., off + DECAY_LORA + ICLR_LORA:]
    w_raw = (w0 + jnp.tanh(pw) @ w2).astype(f32)
    decay = jnp.exp(-jnp.exp(-jax.nn.softplus(-w_raw) - 0.5))
    a = jax.nn.sigmoid((a0 + pa @ a2).astype(f32))
    gate = (jax.nn.sigmoid(pg) @ g2).astype(f32)
    r, kr, vr, decay, a = (u.astype(f32).reshape(b, t, RWKV_HEADS, RWKV_HEAD_DIM) for u in (r, kr, vr, decay, a))
    kk = l2norm(kr * k_k.astype(f32).reshape(hn))
    kr = kr * (1.0 + (a - 1.0) * k_a.astype(f32).reshape(hn))
    y, s_rwkv_new = rwkv7_scan(r, decay, kr, vr, kk, a, s_rwkv.astype(f32))
    mean = jnp.mean(y, axis=-1, keepdims=True)
    var = jnp.mean(jnp.square(y - mean), axis=-1, keepdims=True)
    y = (y - mean) * lax.rsqrt(var + RWKV_GN_EPS) * gn_g.astype(f32).reshape(hn) + gn_b.astype(f32).reshape(hn)
    y = y + jnp.sum(r * kr * r_k.astype(f32).reshape(hn), axis=-1, keepdims=True) * vr
    o_b = (y.reshape(b, t, wd) * gate).astype(h.dtype)
    out = jnp.concatenate([o_a, o_b], axis=-1) @ w_out
    return out, conv_new, s_gdn_new, s_rwkv_new, shift_new


def sb_block(qb, k, v, q_pos, bias):
    k_pos = jnp.arange(k.shape[1])
    z = jnp.einsum('bqhd,bkhd->bhqk', qb, k).astype(jnp.float32) * (SB_HEAD_DIM ** -0.5)
    z = z + bias.astype(jnp.float32)[None, :, None, None]
    mask = k_pos[None, :] < q_pos[:, None]
    log_rem = jnp.where(mask, jax.nn.log_sigmoid(-z), 0.0)
    rev = lax.cumsum(log_rem, axis=3, reverse=True)
    suffix = jnp.concatenate([rev[..., 1:], jnp.zeros_like(rev[..., :1])], axis=-1)
    att = jnp.where(mask, jnp.exp(jax.nn.log_sigmoid(z) + suffix), 0.0)
    return jnp.einsum('bhqk,bkhd->bqhd', att.astype(v.dtype), v)


def stick_breaking(q, k, v, q_start, bias):
    t = q.shape[1]
    blk = Q_BLOCK if t % Q_BLOCK == 0 else t
    outs = []
    for i in range(t // blk):
        s = i * blk
        kend = q_start + s + blk
        outs.append(sb_block(q[:, s:s + blk], k[:, :kend], v[:, :kend], q_start + s + jnp.arange(blk), bias))
    return jnp.concatenate(outs, axis=1)


def odd_mixer(h, w_in, w_out, bias, k_past, v_past):
    b, t, _ = h.shape
    qkv = (h @ w_in).reshape(b, t, 3, SB_HEADS, SB_HEAD_DIM)
    q, k, v = qkv[:, :, 0], qkv[:, :, 1], qkv[:, :, 2]
    q_start = k_past.shape[1]
    k_all = jnp.concatenate([k_past.astype(k.dtype), k], axis=1)
    v_all = jnp.concatenate([v_past.astype(v.dtype), v], axis=1)
    y = stick_breaking(q, k_all, v_all, q_start, bias)
    return y.reshape(b, t, MIX_WIDTH) @ w_out, k, v


def mem_kv(mem, g, w_k, w_v):
    b, m, _ = mem.shape
    mn = rmsnorm(mem, g)
    return ((mn @ w_k).reshape(b, m, MEM_HEADS, MEM_HEAD_DIM), (mn @ w_v).reshape(b, m, MEM_HEADS, MEM_HEAD_DIM))


def mem_attend(h, w_q, w_o, mk, mv):
    b, t, _ = h.shape
    q = (h @ w_q).reshape(b, t, MEM_HEADS, MEM_HEAD_DIM)
    s = jnp.einsum('bqhd,bkhd->bhqk', q, mk.astype(q.dtype)).astype(jnp.float32) * (MEM_HEAD_DIM ** -0.5)
    p = jax.nn.softmax(s, axis=-1)
    o = jnp.einsum('bhqk,bkhd->bqhd', p.astype(h.dtype), mv.astype(h.dtype))
    return o.reshape(b, t, D_MODEL) @ w_o


def swiglu(h, w_in, w_out):
    gu = h @ w_in
    return (jax.nn.silu(gu[..., :D_FF]) * gu[..., D_FF:]) @ w_out


def gather_pages(pool, page_table):
    rows = pool[page_table]
    return rows.reshape((page_table.shape[0], -1) + pool.shape[2:])


def run_trunk(x, mem_k, mem_v, conv_buf, s_gdn, s_rwkv, shift, sb_past, weights):
    (norm_mix, norm_mem, norm_ffn, norm_final, ev_w_in, ev_w_out, gdn_conv_w, gdn_a_log, gdn_dt_bias, gdn_norm,
     rwkv_mu, rwkv_w0, rwkv_w2, rwkv_a0, rwkv_a2, rwkv_g2, rwkv_k_k, rwkv_k_a, rwkv_r_k, rwkv_gn_g, rwkv_gn_b,
     sb_w_in, sb_w_out, sb_bias, mem_w_q, mem_w_o, ffn_w_in, ffn_w_out) = weights
    conv_new, gdn_new, rwkv_new, shift_new, k_new, v_new = [], [], [], [], [], []
    for layer in range(DEPTH):
        i = layer // 2
        h = rmsnorm(x, norm_mix[layer])
        if layer % 2 == 0:
            mix, cb, sg, sr, sh = even_mixer(
                h, ev_w_in[i], ev_w_out[i], gdn_conv_w[i], gdn_a_log[i], gdn_dt_bias[i], gdn_norm[i],
                rwkv_mu[i], rwkv_w0[i], rwkv_w2[i], rwkv_a0[i], rwkv_a2[i], rwkv_g2[i], rwkv_k_k[i], rwkv_k_a[i],
                rwkv_r_k[i], rwkv_gn_g[i], rwkv_gn_b[i], conv_buf[i], s_gdn[i], s_rwkv[i], shift[i])
            conv_new.append(cb)
            gdn_new.append(sg)
            rwkv_new.append(sr)
            shift_new.append(sh)
        else:
            k_past, v_past = sb_past(i)
            mix, kn, vn = odd_mixer(h, sb_w_in[i], sb_w_out[i], sb_bias[i], k_past, v_past)
            k_new.append(kn)
            v_new.append(vn)
        x = x + mix
        x = x + mem_attend(rmsnorm(x, norm_mem[layer]), mem_w_q[layer], mem_w_o[layer], mem_k[layer], mem_v[layer])
        x = x + swiglu(rmsnorm(x, norm_ffn[layer]), ffn_w_in[layer], ffn_w_out[layer])
    return (rmsnorm(x, norm_final), jnp.stack(conv_new), jnp.stack(gdn_new), jnp.stack(rwkv_new),
            jnp.stack(shift_new), jnp.stack(k_new), jnp.stack(v_new))


def setup_inputs(seed: int = 0) -> dict:
    key = jax.random.key(seed)
    ks = iter(jax.random.split(key, 64))
    f32 = jnp.float32

    def nrm(shape, scale):
        return jax.random.normal(next(ks), shape, f32) * scale

    def unif(shape, lo, hi):
        return jax.random.uniform(next(ks), shape, f32, lo, hi)

    def gain(shape):
        return 1.0 + nrm(shape, 0.01)

    n_pages = PAST_LEN // PAGE_SIZE
    n_used = DEC_BATCH * n_pages
    n_pool = n_used + (n_used + 3) // 4
    page_table = jax.random.permutation(next(ks), n_pool)[:n_used].reshape(DEC_BATCH, n_pages).astype(jnp.int32)
    dt = jnp.exp(unif((N_EVEN, GDN_HEADS), math.log(1e-3), math.log(1e-1)))
    return {
        'x_prompt': nrm((BATCH, SEQ, D_MODEL), 1.0),
        'x_sample': nrm((DEC_BATCH, DEC_SEQ, D_MODEL), 1.0),
        'mem_prompt': nrm((BATCH, N_MEM, D_MODEL), 1.0),
        'state_gdn': nrm((N_EVEN, DEC_BATCH, GDN_HEADS, GDN_HEAD_DIM, GDN_HEAD_DIM), 0.1),
        'state_gdn_conv': nrm((N_EVEN, DEC_BATCH, CONV_WIDTH - 1, GDN_CONV_CH), 1.0),
        'state_rwkv': nrm((N_EVEN, DEC_BATCH, RWKV_HEADS, RWKV_HEAD_DIM, RWKV_HEAD_DIM), 0.3),
        'state_rwkv_shift': nrm((N_EVEN, DEC_BATCH, RWKV_IN), 1.0),
        'cache_sb_k': nrm((N_ODD, n_pool, PAGE_SIZE, SB_HEADS, SB_HEAD_DIM), 1.0),
        'cache_sb_v': nrm((N_ODD, n_pool, PAGE_SIZE, SB_HEADS, SB_HEAD_DIM), 1.0),
        'cache_mem_k': nrm((DEPTH, DEC_BATCH, N_MEM, MEM_HEADS, MEM_HEAD_DIM), 1.0),
        'cache_mem_v': nrm((DEPTH, DEC_BATCH, N_MEM, MEM_HEADS, MEM_HEAD_DIM), 1.0),
        'page_table': page_table,
        'norm_mix': gain((DEPTH, D_MODEL)),
        'norm_mem': gain((DEPTH, D_MODEL)),
        'norm_memtok': gain((DEPTH, D_MODEL)),
        'norm_ffn': gain((DEPTH, D_MODEL)),
        'norm_final': gain((D_MODEL,)),
        'ev_w_in': nrm((N_EVEN, D_MODEL, EVEN_IN), D_MODEL ** -0.5),
        'ev_w_out': nrm((N_EVEN, MIX_WIDTH, D_MODEL), MIX_WIDTH ** -0.5),
        'gdn_conv_w': nrm((N_EVEN, CONV_WIDTH, GDN_CONV_CH), CONV_WIDTH ** -0.5),
        'gdn_a_log': jnp.log(unif((N_EVEN, GDN_HEADS), 1.0, 16.0)),
        'gdn_dt_bias': dt + jnp.log(-jnp.expm1(-dt)),
        'gdn_norm': gain((N_EVEN, GDN_HEAD_DIM)),
        'rwkv_mu': unif((N_EVEN, RWKV_IN), 0.0, 1.0),
        'rwkv_w0': unif((N_EVEN, RWKV_WIDTH), -6.0, -1.0),
        'rwkv_w2': nrm((N_EVEN, DECAY_LORA, RWKV_WIDTH), 0.1 * DECAY_LORA ** -0.5),
        'rwkv_a0': nrm((N_EVEN, RWKV_WIDTH), 0.1),
        'rwkv_a2': nrm((N_EVEN, ICLR_LORA, RWKV_WIDTH), 0.5 * ICLR_LORA ** -0.5),
        'rwkv_g2': nrm((N_EVEN, GATE_LORA, RWKV_WIDTH), GATE_LORA ** -0.5),
        'rwkv_k_k': 0.85 + nrm((N_EVEN, RWKV_WIDTH), 0.02),
        'rwkv_k_a': 1.0 + nrm((N_EVEN, RWKV_WIDTH), 0.02),
        'rwkv_r_k': nrm((N_EVEN, RWKV_WIDTH), 0.1),
        'rwkv_gn_g': gain((N_EVEN, RWKV_WIDTH)),
        'rwkv_gn_b': nrm((N_EVEN, RWKV_WIDTH), 0.01),
        'sb_w_in': nrm((N_ODD, D_MODEL, 3 * MIX_WIDTH), D_MODEL ** -0.5),
        'sb_w_out': nrm((N_ODD, MIX_WIDTH, D_MODEL), MIX_WIDTH ** -0.5),
        'sb_bias': SB_BIAS_INIT + nrm((N_ODD, SB_HEADS), 0.5),
        'mem_w_q': nrm((DEPTH, D_MODEL, D_MODEL), D_MODEL ** -0.5),
        'mem_w_k': nrm((DEPTH, D_MODEL, D_MODEL), D_MODEL ** -0.5),
        'mem_w_v': nrm((DEPTH, D_MODEL, D_MODEL), D_MODEL ** -0.5),
        'mem_w_o': nrm((DEPTH, D_MODEL, D_MODEL), D_MODEL ** -0.5),
        'ffn_w_in': nrm((DEPTH, D_MODEL, 2 * D_FF), D_MODEL ** -0.5),
        'ffn_w_out': nrm((DEPTH, D_FF, D_MODEL), D_FF ** -0.5),
    }


def reference(x_prompt, x_sample, mem_prompt, state_gdn, state_gdn_conv, state_rwkv, state_rwkv_shift,
              cache_sb_k, cache_sb_v, cache_mem_k, cache_mem_v, page_table,
              norm_mix, norm_mem, norm_memtok, norm_ffn, norm_final, ev_w_in, ev_w_out,
              gdn_conv_w, gdn_a_log, gdn_dt_bias, gdn_norm, rwkv_mu, rwkv_w0, rwkv_w2, rwkv_a0, rwkv_a2, rwkv_g2,
              rwkv_k_k, rwkv_k_a, rwkv_r_k, rwkv_gn_g, rwkv_gn_b, sb_w_in, sb_w_out, sb_bias,
              mem_w_q, mem_w_k, mem_w_v, mem_w_o, ffn_w_in, ffn_w_out):
    weights = (norm_mix, norm_mem, norm_ffn, norm_final, ev_w_in, ev_w_out, gdn_conv_w, gdn_a_log, gdn_dt_bias,
               gdn_norm, rwkv_mu, rwkv_w0, rwkv_w2, rwkv_a0, rwkv_a2, rwkv_g2, rwkv_k_k, rwkv_k_a, rwkv_r_k,
               rwkv_gn_g, rwkv_gn_b, sb_w_in, sb_w_out, sb_bias, mem_w_q, mem_w_o, ffn_w_in, ffn_w_out)
    f32 = jnp.float32
    bp = x_prompt.shape[0]
    mk_list, mv_list = [], []
    for layer in range(DEPTH):
        mk, mv = mem_kv(mem_prompt, norm_memtok[layer], mem_w_k[layer], mem_w_v[layer])
        mk_list.append(mk)
        mv_list.append(mv)
    mem_k_p = jnp.stack(mk_list)
    mem_v_p = jnp.stack(mv_list)
    empty = jnp.zeros((bp, 0, SB_HEADS, SB_HEAD_DIM), x_prompt.dtype)
    y_p, conv_p, gdn_p, rwkv_p, shift_p, sbk_p, sbv_p = run_trunk(
        x_prompt, mem_k_p, mem_v_p,
        jnp.zeros((N_EVEN, bp, CONV_WIDTH - 1, GDN_CONV_CH), x_prompt.dtype),
        jnp.zeros((N_EVEN, bp, GDN_HEADS, GDN_HEAD_DIM, GDN_HEAD_DIM), f32),
        jnp.zeros((N_EVEN, bp, RWKV_HEADS, RWKV_HEAD_DIM, RWKV_HEAD_DIM), f32),
        jnp.zeros((N_EVEN, bp, RWKV_IN), x_prompt.dtype),
        lambda i: (empty, empty), weights)
    y_s, conv_s, gdn_s, rwkv_s, shift_s, sbk_s, sbv_s = run_trunk(
        x_sample, cache_mem_k, cache_mem_v, state_gdn_conv, state_gdn, state_rwkv, state_rwkv_shift,
        lambda i: (gather_pages(cache_sb_k[i], page_table), gather_pages(cache_sb_v[i], page_table)), weights)
    return (y_p, y_s, gdn_p, gdn_s, conv_p, conv_s, rwkv_p, rwkv_s, shift_p, shift_s,
            sbk_p, sbk_s, sbv_p, sbv_s, mem_k_p, mem_v_p)
```

```python
import contextlib
import numpy as np
import concourse.bass as bass
import concourse.mybir as mybir
from concourse.bass_utils import run_bass_kernel_spmd

F32 = mybir.dt.float32
BF16 = mybir.dt.bfloat16
I32 = mybir.dt.int32
AF = mybir.ActivationFunctionType
ALU = mybir.AluOpType
AX = mybir.AxisListType

D = 1024
KC = 8
NP = 2048
NS = 32
NT = NP + NS
DFF = 2816
EPS = 1e-6

ENGS = ("pe", "act", "dve", "pool", "sp")
SEM_WRAP = 30000


class DSem:
    def __init__(self, name):
        self.name = name
        self.count = 0
        self.handle = None
        self.last = None


class Op:
    __slots__ = ("eng", "fn", "deps", "sig", "dsem", "dcount")

    def __init__(self, eng, fn):
        self.eng = eng
        self.fn = fn
        self.deps = []
        self.sig = None
        self.dsem = None
        self.dcount = None


class Sched:
    def __init__(self, nc):
        self.nc = nc
        self.streams = {e: [] for e in ENGS}
        self.last_w = {}
        self.readers = {}
        self.dsems = {}

    def dsem(self, name):
        if name not in self.dsems:
            self.dsems[name] = DSem(name)
        return self.dsems[name]

    def _dep(self, o, p):
        if p is None or p is o:
            return
        if p.dsem is not None:
            o.deps.append((p, p.dsem.count))
        elif not (p.eng == "pe" and o.eng == "pe"):
            o.deps.append((p, None))

    def op(self, eng, fn, reads=(), writes=(), dsem=None):
        o = Op(eng, fn)
        for k in reads:
            self._dep(o, self.last_w.get(k))
        for k in writes:
            self._dep(o, self.last_w.get(k))
            for r in self.readers.get(k, ()):
                self._dep(o, r)
        if dsem is not None:
            if isinstance(dsem, str):
                dsem = self.dsem(dsem)
            o.dsem = dsem
            dsem.count += 1
            o.dcount = dsem.count
            dsem.last = o
        for k in reads:
            self.readers.setdefault(k, []).append(o)
        for k in writes:
            self.last_w[k] = o
            self.readers[k] = []
        self.streams[eng].append(o)
        return o

    def barrier(self):
        lasts = [s[-1] for s in self.streams.values() if s]
        dl = [d.last for d in self.dsems.values() if d.last is not None]
        nc = self.nc
        nops = {"pe": lambda: nc.tensor.nop(), "act": lambda: nc.scalar.nop(), "dve": lambda: nc.vector.nop(),
                "pool": lambda: nc.gpsimd.nop(), "sp": lambda: nc.sync.nop()}
        for e in ENGS:
            o = Op(e, nops[e])
            for p in lasts:
                if p.dsem is None and not (p.eng == "pe" and e == "pe"):
                    o.deps.append((p, None))
            for p in dl:
                o.deps.append((p, p.dsem.count))
            self.streams[e].append(o)
        self.last_w = {}
        self.readers = {}

    def emit(self, stack):
        nc = self.nc
        for e in ENGS:
            for o in self.streams[e]:
                for (p, dc) in o.deps:
                    if p.dsem is None:
                        p.sig = 0
        nsig = {}
        for e in ENGS:
            c = 0
            for o in self.streams[e]:
                if o.dsem is None and o.sig is not None:
                    c += 1
                    o.sig = c
            nsig[e] = c
        esems = {}
        for e in ENGS:
            n = max(1, (nsig[e] + SEM_WRAP - 1) // SEM_WRAP)
            esems[e] = [stack.enter_context(nc.semaphore(f"s_{e}{i}")) for i in range(n)]
        for ds in self.dsems.values():
            ds.handle = stack.enter_context(nc.semaphore(f"d_{ds.name}"))
        engobj = {"pe": nc.tensor, "act": nc.scalar, "dve": nc.vector, "pool": nc.gpsimd, "sp": nc.sync}

        def emit_stream(e):
            eo = engobj[e]
            waited_e = {}
            waited_d = {}
            for o in self.streams[e]:
                need_e = {}
                need_d = {}
                for (p, dc) in o.deps:
                    if p.dsem is not None:
                        if need_d.get(p.dsem.name, 0) < dc:
                            need_d[p.dsem.name] = dc
                    else:
                        if need_e.get(p.eng, 0) < p.sig:
                            need_e[p.eng] = p.sig
                for pe_, sig in need_e.items():
                    if waited_e.get(pe_, 0) >= sig:
                        continue
                    si = (sig - 1) // SEM_WRAP
                    eo.wait_ge(esems[pe_][si], (sig - 1) % SEM_WRAP + 1)
                    waited_e[pe_] = sig
                for dn, dc in need_d.items():
                    if waited_d.get(dn, 0) >= dc:
                        continue
                    eo.wait_ge(self.dsems[dn].handle, dc * 16)
                    waited_d[dn] = dc
                ins = o.fn()
                if o.dsem is not None:
                    ins.then_inc(o.dsem.handle, 16)
                elif o.sig is not None:
                    si = (o.sig - 1) // SEM_WRAP
                    ins.then_inc(esems[e][si], 1)
            if e == "sp":
                for ds in self.dsems.values():
                    if ds.count and waited_d.get(ds.name, 0) < ds.count:
                        eo.wait_ge(ds.handle, ds.count * 16)

        block = stack.enter_context(nc.Block())

        @block.sync
        def _(eng):
            emit_stream("sp")

        @block.scalar
        def _(eng):
            emit_stream("act")

        @block.vector
        def _(eng):
            emit_stream("dve")

        @block.gpsimd
        def _(eng):
            emit_stream("pool")

        @block.tensor
        def _(eng):
            emit_stream("pe")


def w_slabs(w, col_ranges):
    K = w.shape[0]
    kc = K // 128
    out = np.zeros((len(col_ranges), 128, kc, 512), np.float32)
    wr = w.reshape(kc, 128, w.shape[1]).transpose(1, 0, 2)
    for i, (c0, c1) in enumerate(col_ranges):
        out[i, :, :, :c1 - c0] = wr[:, :, c0:c1]
    return out


def col_param(v, nch):
    return np.ascontiguousarray(np.asarray(v, np.float32).reshape(nch, 128).T)


TILES = [(0, 512), (512, 512), (1024, 512), (1536, 512), (2048, 32)]


class KB:
    def __init__(self, nc, cfg):
        self.nc = nc
        self.cfg = cfg
        self.S = Sched(nc)
        self.stack = contextlib.ExitStack()
        self.dr = {}
        self.ps_i = 0
        self.ps_n = 8
        self.w_i = 0
        self.uid = 0

    def din(self, name, shape, dtype=F32):
        self.dr[name] = self.nc.dram_tensor(name, list(shape), dtype, kind="ExternalInput").ap()
        return self.dr[name]

    def dout(self, name, shape, dtype=F32):
        self.dr[name] = self.nc.dram_tensor(name, list(shape), dtype, kind="ExternalOutput").ap()
        return self.dr[name]

    def sb(self, name, shape, dtype, stack=None):
        return (stack or self.stack).enter_context(self.nc.sbuf_tensor(name, list(shape), dtype))

    def psum(self):
        i = self.ps_i % self.ps_n
        self.ps_i = (i + 1) % self.ps_n
        return i, self.PS[i]

    def mm(self, out, lhsT, rhs, start, stop, reads, writes, skip=False):
        nc = self.nc
        if skip:
            return self.S.op("pe", lambda: nc.tensor.matmul(out, lhsT, rhs, start=start, stop=stop, skip_group_check=True), reads, writes)
        return self.S.op("pe", lambda: nc.tensor.matmul(out, lhsT, rhs, start=start, stop=stop), reads, writes)

    def act(self, out, in_, func, reads, writes, bias=None, scale=None, accum_out=None):
        nc = self.nc
        kw = {}
        if bias is not None:
            kw["bias"] = bias
        if scale is not None:
            kw["scale"] = scale
        if accum_out is not None:
            kw["accum_out"] = accum_out
        return self.S.op("act", lambda: nc.scalar.activation(out, in_, func, **kw), reads, writes)

    def tt(self, out, in0, in1, op, reads, writes, eng="dve"):
        nc = self.nc
        e = nc.vector if eng == "dve" else nc.gpsimd
        return self.S.op(eng, lambda: e.tensor_tensor(out, in0, in1, op), reads, writes)

    def ts(self, out, in0, s1, s2, op0, op1, reads, writes, eng="dve"):
        nc = self.nc
        e = nc.vector if eng == "dve" else nc.gpsimd
        if op1 is None:
            return self.S.op(eng, lambda: e.tensor_scalar(out, in0, s1, None, op0), reads, writes)
        return self.S.op(eng, lambda: e.tensor_scalar(out, in0, s1, s2, op0, op1), reads, writes)

    def stt(self, out, in0, scalar, in1, op0, op1, reads, writes):
        nc = self.nc
        return self.S.op("dve", lambda: nc.vector.scalar_tensor_tensor(out, in0, scalar, in1, op0, op1), reads, writes)

    def copy(self, out, in_, reads, writes, eng="dve"):
        nc = self.nc
        if eng == "act":
            return self.S.op("act", lambda: nc.scalar.activation(out, in_, AF.Identity), reads, writes)
        e = nc.vector if eng == "dve" else nc.gpsimd
        return self.S.op(eng, lambda: e.tensor_copy(out, in_), reads, writes)

    def memset(self, ap, val, writes, eng="pool"):
        nc = self.nc
        e = nc.vector if eng == "dve" else nc.gpsimd
        return self.S.op(eng, lambda: e.memset(ap, val), (), writes)

    def dma(self, out, in_, reads, writes, dsem, eng="sp", **kw):
        nc = self.nc
        e = nc.sync if eng == "sp" else nc.gpsimd
        return self.S.op(eng, lambda: e.dma_start(out=out, in_=in_, **kw), reads, writes, dsem=dsem)

    def load_slab(self, dram_slab, kc=8):
        slot = self.w_i
        self.w_i = (self.w_i + 1) % len(self.WS)
        w = self.WS[slot]
        self.dma(w[:, 0:kc, :], dram_slab, (), [("w", slot)], f"w{slot}", eng="pool", max_dma_last_dim=4096)
        return slot

    def proj_fm(self, slot, ncols, H, tiles, consume, kc=8):
        w = self.WS[slot]
        for ti, (hkeys, hfn, n) in enumerate(tiles):
            for j in range(ncols // 128):
                pi, ps = self.psum()
                for k in range(kc):
                    self.mm(ps[:, 0:n], w[:, k, j * 128:(j + 1) * 128], hfn(k), k == 0, k == kc - 1,
                            [("w", slot)] + list(hkeys), [("ps", pi)])
                consume(j, ti, ps[:, 0:n], ("ps", pi))

    def arena_reset(self):
        self.ar_off = 0

    def arena(self, shape, dtype):
        esz = 2 if dtype == BF16 else 4
        n = 1
        for d in shape[1:]:
            n *= d
        nbytes = (n * esz + 31) // 32 * 32
        off = self.ar_off
        assert off + nbytes <= self.ARENA_BYTES, f"arena overflow {off + nbytes} > {self.ARENA_BYTES}"
        self.ar_off += nbytes
        v = self.ARENA[:, off // 4: (off + nbytes) // 4]
        if dtype == BF16:
            v = v.bitcast(BF16)
        elif dtype == I32:
            v = v.bitcast(I32)
        v = v[:, 0:n]
        if len(shape) == 3:
            v = v.rearrange("p (a b) -> p a b", a=shape[1])
        elif len(shape) == 4:
            v = v.rearrange("p (a b c) -> p a b c", a=shape[1], b=shape[2])
        return v

    def key(self, base):
        self.uid += 1
        return (base, self.uid)

    def rmsnorm(self, src_fn, src_keys, gcol, out_fn, out_keys, n, tmp):
        pi, ps = self.psum()
        sq, r1, r2 = tmp["sq"], tmp["r1"], tmp["r2"]
        sqk = tmp.get("sqkeys") or [("sq", tmp["id"], k) for k in range(KC)]
        for k in range(KC):
            self.act(sq[:, k, 0:n], src_fn(k), AF.Square, [src_keys[k]], [sqk[k]])
        for k in range(KC):
            self.mm(ps[:, 0:n], self.ONES[:, :], sq[:, k, 0:n], k == 0, k == KC - 1,
                    [sqk[k], ("ones",)], [("ps", pi)])
        self.act(r1[:, 0:n], ps[:, 0:n], AF.Ln, [("ps", pi)], [("r1", tmp["id"])], bias=self.EPSC[:, 0:1], scale=1.0 / D)
        self.act(r2[:, 0:n], r1[:, 0:n], AF.Exp, [("r1", tmp["id"])], [("r2", tmp["id"])], scale=-0.5)
        for k in range(KC):
            self.stt(out_fn(k), src_fn(k), gcol[:, k:k + 1], r2[:, 0:n], ALU.mult, ALU.mult,
                     [src_keys[k], ("r2", tmp["id"]), ("par",)], [out_keys[k]])

    def norm_tmp(self, sq=None, sqkeys=None):
        self.uid += 1
        return {"sq": sq if sq is not None else self.arena([128, 8, 512], BF16), "sqkeys": sqkeys,
                "r1": self.arena([128, 512], F32), "r2": self.arena([128, 512], F32), "id": self.uid}

    def mem_phase(self, L):
        dr = self.dr
        self.arena_reset()
        tmp = self.norm_tmp()
        HG = self.arena([128, 8, 512], BF16)
        QT = self.arena([128, 8, 512], BF16)
        OT = self.arena([128, 8, 512], BF16)
        PT = self.arena([128, 2, 512], BF16)
        RD1 = self.arena([128, 512], F32)
        RD2 = self.arena([128, 512], F32)
        MKT = self.arena([128, 8, 256], BF16)
        MV = self.arena([128, 2, 1024], BF16)
        SMK = [self.arena([128, 8, 256], BF16) for _ in range(2)]
        SMV = [self.arena([128, 2, 1024], BF16) for _ in range(2)]
        MEMX = self.arena([128, 8, 256], F32)
        MN = self.arena([128, 8, 256], BF16)
        STG = [self.arena([128, 512], F32) for _ in range(2)]
        stg_i = [0]

        def stage_out(ps_ap, pskey, n, dst, bf_dst, bf_key):
            i = stg_i[0]
            stg_i[0] ^= 1
            self.copy(STG[i][:, 0:n], ps_ap, [pskey], [("stg", i)], eng="dve")
            self.copy(bf_dst, STG[i][:, 0:n], [("stg", i)], [bf_key], eng="pool")
            self.dma(dst, STG[i][:, 0:n], [("stg", i)], [], f"stg{i}")

        self.dma(MEMX[:, :, :], dr["memT"].rearrange("(kc p) m -> p kc m", p=128), [], [("memx",)], "memx")
        self.rmsnorm(lambda k: MEMX[:, k, :], [("memx",)] * 8, self.G["memtok"][L],
                     lambda k: MN[:, k, :], [("mn", k) for k in range(8)], 256, tmp)
        mnkeys = [("mn", k) for k in range(8)]
        sub = self.cfg.get("sub", 3)
        for s in range(2):
            if sub == 10:
                continue
            slot = self.load_slab(dr["w_mk"][L, s])

            def consume(j, ti, ps, pskey, s=s):
                oc = s * 4 + j
                stage_out(ps, pskey, 256, dr["o_memkT"][L, oc * 128:(oc + 1) * 128, :], MKT[:, oc, :], ("mkt", oc))
            self.proj_fm(slot, 512, None, [(mnkeys, lambda k: MN[:, k, :], 256)], consume)
        for s in range(2):
            if sub in (10, 11):
                continue
            slot = self.load_slab(dr["w_mv"][L, s])
            w = self.WS[slot]
            for mb in range(2):
                pi, ps = self.psum()
                for k in range(8):
                    self.mm(ps[:, :], MN[:, k, mb * 128:(mb + 1) * 128], w[:, k, :], k == 0, k == 7,
                            [("w", slot), ("mn", k)], [("ps", pi)])
                stage_out(ps[:, :], ("ps", pi), 512, dr["o_memv"][L, mb * 128:(mb + 1) * 128, s * 512:(s + 1) * 512],
                          MV[:, mb, s * 512:(s + 1) * 512], ("mv", mb, s))

        def attend(kt, kv, kvkeys, c0, n):
            for hd in range(4):
                for mb in range(2):
                    pi, ps = self.psum()
                    for dc in range(2):
                        self.mm(ps[:, 0:n], kt[:, hd * 2 + dc, mb * 128:(mb + 1) * 128], QT[:, hd * 2 + dc, c0:c0 + n],
                                dc == 0, dc == 1, kvkeys + [("qt", hd * 2 + dc)], [("ps", pi)])
                    self.act(PT[:, mb, 0:n], ps[:, 0:n], AF.Exp, [("ps", pi)], [("pt", mb)], scale=1.0 / 16.0)
                pi, ps = self.psum()
                for mb in range(2):
                    self.mm(ps[:, 0:n], self.ONES[:, :], PT[:, mb, 0:n], mb == 0, mb == 1, [("pt", mb)], [("ps", pi)])
                self.act(RD1[:, 0:n], ps[:, 0:n], AF.Ln, [("ps", pi)], [("rd1",)])
                self.act(RD2[:, 0:n], RD1[:, 0:n], AF.Exp, [("rd1",)], [("rd2",)], scale=-1.0)
                for dc in range(2):
                    pi, ps = self.psum()
                    for mb in range(2):
                        self.mm(ps[:, 0:n], kv[:, mb, hd * 256 + dc * 128: hd * 256 + (dc + 1) * 128], PT[:, mb, 0:n],
                                mb == 0, mb == 1, kvkeys + [("pt", mb)], [("ps", pi)])
                    self.tt(OT[:, hd * 2 + dc, c0:c0 + n], ps[:, 0:n], RD2[:, 0:n], ALU.mult,
                            [("ps", pi), ("rd2",)], [("ot", hd * 2 + dc)])

        pkv = [("mkt", oc) for oc in range(8)] + [("mv", mb, s) for mb in range(2) for s in range(2)]
        for g, (t0, n) in enumerate(TILES):
            if sub in (1, 10, 11) or (sub == 2 and g == len(TILES) - 1):
                continue
            xk = [("X", k, g) for k in range(8)]
            hk = [("hg", k) for k in range(8)]
            self.rmsnorm(lambda k: self.X[:, k, t0:t0 + n], xk, self.G["mem"][L],
                         lambda k: HG[:, k, 0:n], hk, n, tmp)
            for s in range(2):
                slot = self.load_slab(dr["w_mq"][L, s])

                def consume(j, ti, ps, pskey, s=s):
                    oc = s * 4 + j
                    self.copy(QT[:, oc, 0:n], ps, [pskey], [("qt", oc)], eng="act")
                self.proj_fm(slot, 512, None, [(hk, lambda k: HG[:, k, 0:n], n)], consume)
            if g < len(TILES) - 1:
                attend(MKT, MV, pkv, 0, n)
            else:
                for sq in range(4):
                    b = sq % 2
                    self.dma(SMK[b][:, :, :], dr["cmkT"][L, sq].rearrange("(kc p) m -> p kc m", p=128),
                             [], [("smk", b)], f"smk{b}", eng="pool", max_dma_last_dim=1024)
                    self.dma(SMV[b][:, :, :], dr["cmv"][L, sq].rearrange("(mb p) f -> p mb f", p=128),
                             [], [("smv", b)], f"smv{b}", eng="pool", max_dma_last_dim=4096)
                    attend(SMK[b], SMV[b], [("smk", b), ("smv", b)], sq * 8, 8)
            ok = [("ot", k) for k in range(8)]
            for s in range(2):
                slot = self.load_slab(dr["w_mo"][L, s])

                def consume(j, ti, ps, pskey, s=s):
                    oc = s * 4 + j
                    self.tt(self.X[:, oc, t0:t0 + n], self.X[:, oc, t0:t0 + n], ps, ALU.add,
                            [pskey, ("X", oc, g)], [("X", oc, g)])
                self.proj_fm(slot, 512, None, [(ok, lambda k: OT[:, k, 0:n], n)], consume)
        self.S.barrier()

    def ffn_phase(self, L):
        dr = self.dr
        self.arena_reset()
        tmp = self.norm_tmp()
        HA = self.arena([128, 8, NT], BF16)
        ACTB = [self.arena([128, 4, 512], BF16) for _ in range(2)]
        SG = [self.arena([128, 512], F32) for _ in range(2)]
        for g, (t0, n) in enumerate(TILES):
            self.rmsnorm(lambda k: self.X[:, k, t0:t0 + n], [("X", k, g) for k in range(8)], self.G["ffn"][L],
                         lambda k: HA[:, k, t0:t0 + n], [("ha", k, g) for k in range(8)], n, tmp)
        cnt = 0
        for hg in range(6):
            nch = 4 if hg < 5 else 2
            sg_ = self.load_slab(dr["w_ffi"][L, hg])
            su_ = self.load_slab(dr["w_ffi"][L, 6 + hg])
            so_ = self.load_slab(dr["w_ffo"][L, hg].rearrange("p a (b c) -> p (a b) c", c=512))
            wg, wu = self.WS[sg_], self.WS[su_]
            wo = self.WS[so_].rearrange("p (a b) c -> p a (b c)", b=2)
            for g, (t0, n) in enumerate(TILES):
                ab = ACTB[cnt % 2]
                abk = cnt % 2
                cnt += 1
                hk = [("ha", k, g) for k in range(8)]
                for j in range(nch):
                    pg, psg = self.psum()
                    for k in range(8):
                        self.mm(psg[:, 0:n], wg[:, k, j * 128:(j + 1) * 128], HA[:, k, t0:t0 + n], k == 0, k == 7,
                                [("w", sg_), hk[k]], [("ps", pg)])
                    pu, psu = self.psum()
                    for k in range(8):
                        self.mm(psu[:, 0:n], wu[:, k, j * 128:(j + 1) * 128], HA[:, k, t0:t0 + n], k == 0, k == 7,
                                [("w", su_), hk[k]], [("ps", pu)])
                    si = j % 2
                    self.act(SG[si][:, 0:n], psg[:, 0:n], AF.Silu, [("ps", pg)], [("sg", si)])
                    self.tt(ab[:, j, 0:n], SG[si][:, 0:n], psu[:, 0:n], ALU.mult, [("sg", si), ("ps", pu)], [("ab", abk, j)])
                for oc in range(8):
                    po, pso = self.psum()
                    for j in range(nch):
                        self.mm(pso[:, 0:n], wo[:, j, oc * 128:(oc + 1) * 128], ab[:, j, 0:n], j == 0, j == nch - 1,
                                [("w", so_), ("ab", abk, j)], [("ps", po)])
                    self.tt(self.X[:, oc, t0:t0 + n], self.X[:, oc, t0:t0 + n], pso[:, 0:n], ALU.add,
                            [("ps", po), ("X", oc, g)], [("X", oc, g)])
        self.S.barrier()

    def final_phase(self):
        dr = self.dr
        self.arena_reset()
        tmp = self.norm_tmp()
        YS = [self.arena([128, 8, 512], F32) for _ in range(2)]
        for g, (t0, n) in enumerate(TILES):
            b = g % 2
            self.rmsnorm(lambda k: self.X[:, k, t0:t0 + n], [("X", k, g) for k in range(8)], self.G["final"],
                         lambda k: YS[b][:, k, 0:n], [("ys", b, k) for k in range(8)], n, tmp)
            self.dma(dr["o_yT"][:, t0:t0 + n].rearrange("(kc p) t -> p kc t", p=128), YS[b][:, :, 0:n],
                     [("ys", b, k) for k in range(8)], [], f"ys{b}")
        self.S.barrier()


ARENA_BYTES = 110592


def set_np(n):
    global NP, NT, TILES
    NP = n
    NT = NP + NS
    TILES = [(i * 512, 512) for i in range(NP // 512)] + [(NP, NS)]


def build(cfg):
    set_np(cfg.get("np", 2048))
    nc = bass.Bass("TRN2", target_bir_lowering=False)
    kb = KB(nc, cfg)
    S = kb.S
    din, dout = kb.din, kb.dout
    din("xT", [D, NT])
    din("memT", [D, 256])
    din("cmkT", [2, 4, D, 256])
    din("cmv", [2, 4, 256, D])
    din("gcols", [8, 128, 8])
    din("gfinal", [128, 8])
    din("w_mq", [2, 2, 128, 8, 512])
    din("w_mk", [2, 2, 128, 8, 512])
    din("w_mv", [2, 2, 128, 8, 512])
    din("w_mo", [2, 2, 128, 8, 512])
    din("w_ffi", [2, 12, 128, 8, 512])
    din("w_ffo", [2, 6, 128, 4, 1024])
    din("cstb", [128, 2320])
    din("cstf", [128, 21])
    din("w_sbi", [6, 128, 8, 512])
    din("w_sbo", [2, 128, 8, 512])
    if cfg.get("odd", True):
        din("poolk", [2560 * 128, 1024])
        din("poolv", [2560 * 128, 1024])
        din("pt", [256], I32)
    din("cstg", [128, 4 * 128])
    din("cstm", [128, 8 * 128])
    din("w_bg", [128, 8, 8])
    din("dtb", [128, 8])
    din("convw", [128, 12, 4])
    din("gnrm", [128, 1])
    din("w_gdn", [4, 128, 8, 512])
    din("w_evo", [2, 128, 8, 512])
    din("rwp", [128, 42])
    din("w2a", [128, 4, 128])
    din("g2", [128, 4, 128])
    din("w_rwl", [128, 8, 512])
    din("w_rwp", [4, 128, 8, 512])
    din("shiftT", [1792, 4])
    din("rwkv_s0T", [4, 4, 128, 64])
    dout("o_shiftT", [1792, 1])
    dout("o_shiftTs", [1792, 4])
    dout("o_rwkvT_p", [4, 128, 64])
    dout("o_rwkvT_s", [4, 4, 128, 64])
    din("convT", [4, 1536, 3])
    din("gdn_s0", [4, 4, 128, 128])
    dout("o_convT", [1536, 3])
    dout("o_convTs", [4, 1536, 3])
    dout("o_gdn_p", [4, 128, 128])
    dout("o_gdn_s", [4, 4, 128, 128])
    dout("o_sbkT", [D, NT])
    dout("o_sbv", [NT, D])
    dout("o_yT", [D, NT])
    dout("o_memkT", [2, D, 256])
    dout("o_memv", [2, 256, D])
    dr = kb.dr
    with kb.stack:
        st = kb.stack
        kb.X = kb.sb("X", [128, 8, NT], F32)
        kb.ONES = kb.sb("ONES", [128, 128], BF16)
        CB = kb.sb("CB", [128, 2320], BF16)
        CF = kb.sb("CF", [128, 21], F32)
        CG = kb.sb("CG", [128, 4 * 128], F32)
        CM = kb.sb("CM", [128, 8 * 128], BF16)
        kb.IDF, kb.TRIF, kb.ONESF, kb.NEGF = CG[:, 0:128], CG[:, 128:256], CG[:, 256:384], CG[:, 384:512]
        kb.MSU, kb.MIU, kb.MSL = CM[:, 0:128], CM[:, 128:256], CM[:, 256:384]
        kb.XSU, kb.XIU, kb.XIUN, kb.XSLN, kb.BLK64 = CM[:, 384:512], CM[:, 512:640], CM[:, 640:768], CM[:, 768:896], CM[:, 896:1024]
        kb.IDENT, kb.TRIU, kb.TRIL, kb.NEGM = CB[:, 0:128], CB[:, 128:256], CB[:, 256:384], CB[:, 384:1280]
        kb.BLKM, kb.SEL, kb.MASK8 = CB[:, 1280:2304], CB[:, 2304:2312], CB[:, 2312:2320]
        kb.PIDX, kb.ONEC, kb.EPSC, kb.BIASQ, kb.BIASH, kb.GNEPS = CF[:, 0:1], CF[:, 1:2], CF[:, 2:3], CF[:, 3:4], CF[:, 4:20], CF[:, 20:21]
        GC = kb.sb("GC", [128, 9, 8], F32)
        kb.WS = [kb.sb(f"WS{i}", [128, 8, 512], BF16) for i in range(3)]
        kb.PS = [st.enter_context(nc.psum_tensor(f"PS{i}", [128, 512], F32)) for i in range(8)]
        kb.ARENA = kb.sb("ARENA", [128, ARENA_BYTES // 4], F32)
        kb.ARENA_BYTES = ARENA_BYTES
        kb.G = {"mix": [GC[:, 0, :], GC[:, 1, :]], "mem": [GC[:, 2, :], GC[:, 3, :]],
                "memtok": [GC[:, 4, :], GC[:, 5, :]], "ffn": [GC[:, 6, :], GC[:, 7, :]], "final": GC[:, 8, :]}
        kb.memset(kb.ONES[:, :], 1.0, [("ones",)])
        kb.dma(CB[:, :], dr["cstb"], [], [("const",)], "const", eng="pool", max_dma_last_dim=4096)
        kb.dma(CF[:, :], dr["cstf"], [], [("constf",), ("par",)], "constf")
        kb.dma(CG[:, :], dr["cstg"], [], [("constf",)], "constf")
        kb.dma(CM[:, :], dr["cstm"], [], [("const",)], "const", eng="pool", max_dma_last_dim=4096)
        for i in range(8):
            kb.dma(GC[:, i, :], dr["gcols"][i], [], [("par",)], "par")
        kb.dma(GC[:, 8, :], dr["gfinal"], [], [("par",)], "par")
        for k in range(8):
            kb.dma(kb.X[:, k, :], dr["xT"][k * 128:(k + 1) * 128, :], [], [("X", k, g) for g in range(len(TILES))], "xin")
        S.barrier()
        dbg = cfg.get("dbg", "all")
        if dbg == "A":
            for k in range(8):
                kb.dma(dr["o_yT"][k * 128:(k + 1) * 128, :], kb.X[:, k, :], [("X", k, g) for g in range(len(TILES))], [], "yo")
        elif dbg == "B":
            kb.final_phase()
        elif dbg == "C":
            kb.mem_phase(0)
            kb.final_phase()
        elif dbg == "D":
            kb.ffn_phase(0)
            kb.final_phase()
        elif dbg == "E":
            kb.even_phase()
            kb.final_phase()
        elif dbg == "O":
            kb.odd_phase()
            kb.final_phase()
        else:
            for L in range(2):
                if L == 0 and cfg.get("even", True):
                    kb.even_phase()
                if L == 1 and cfg.get("odd", True):
                    kb.odd_phase()
                kb.mem_phase(L)
                kb.ffn_phase(L)
            kb.final_phase()
        S.emit(st)
    return nc


def prep_inputs(inp, c):
    f = np.float32
    m = {}
    xs = inp["x_sample"][4 * c:4 * c + 4].reshape(NS, D)
    m["xT"] = np.ascontiguousarray(np.concatenate([inp["x_prompt"][c][:NP], xs], axis=0).T.astype(f))
    m["memT"] = np.ascontiguousarray(inp["mem_prompt"][c].T)
    m["cmkT"] = np.ascontiguousarray(inp["cache_mem_k"][:, 4 * c:4 * c + 4].reshape(2, 4, 256, D).transpose(0, 1, 3, 2))
    m["cmv"] = np.ascontiguousarray(inp["cache_mem_v"][:, 4 * c:4 * c + 4].reshape(2, 4, 256, D))
    m["convT"] = np.ascontiguousarray(inp["state_gdn_conv"][0, 4 * c:4 * c + 4].transpose(0, 2, 1))
    m["gdn_s0"] = np.ascontiguousarray(inp["state_gdn"][0, 4 * c:4 * c + 4])
    m["shiftT"] = np.ascontiguousarray(inp["state_rwkv_shift"][0, 4 * c:4 * c + 4].T)
    sr = inp["state_rwkv"][0, 4 * c:4 * c + 4]
    m["rwkv_s0T"] = np.ascontiguousarray(sr.transpose(0, 1, 3, 2).reshape(4, 4, 128, 64))
    if "page_table" in inp:
        m["pt"] = np.ascontiguousarray(inp["page_table"][4 * c:4 * c + 4].reshape(256).astype(np.int32))
    return m


def prep_pools(inp):
    pk = inp["cache_sb_k"][0].reshape(2560, 128, 8, 128)
    pk = np.ascontiguousarray(pk.transpose(0, 3, 2, 1)).reshape(2560 * 128, 1024)
    pv = np.ascontiguousarray(inp["cache_sb_v"][0]).reshape(2560 * 128, 1024)
    return pk, pv


def consts(sb_bias):
    cb = np.zeros((128, 2320), np.float32)
    k = np.arange(128)
    cb[:, 0:128] = np.eye(128)
    cb[:, 128:256] = (k[:, None] > k[None, :])
    cb[:, 256:384] = (k[:, None] <= k[None, :])
    c = np.arange(896)
    cb[:, 384:1280] = np.where(k[:, None] < c[None, :] - 384, 0.0, -1600.0)
    hq = k // 8
    hp = np.arange(1024) // 64
    cb[:, 1280:2304] = (hq[:, None] == hp[None, :])
    cb[:, 2304:2312] = ((k % 8)[:, None] == np.arange(8)[None, :])
    cb[:, 2312:2320] = (np.arange(8)[None, :] < (k % 8)[:, None])
    cf = np.zeros((128, 21), np.float32)
    cf[:, 20] = 64e-5
    cf[:, 0] = k
    cf[:, 1] = 1.0
    cf[:, 2] = EPS
    cf[:, 3] = np.repeat(sb_bias, 8)
    cf[:, 4:20] = np.broadcast_to(sb_bias[None, :], (128, 16))
    return cb, cf


def consts2():
    k = np.arange(128)
    cg = np.zeros((128, 512), np.float32)
    cg[:, 0:128] = np.eye(128)
    cg[:, 128:256] = (k[:, None] <= k[None, :])
    cg[:, 256:384] = 1.0
    cg[:, 384:512] = -1.0
    cm = np.zeros((128, 1024), np.float32)
    r, c = k[:, None], k[None, :]
    NEG = -10000.0
    cm[:, 0:128] = np.where(c > r, 0.0, NEG)
    cm[:, 128:256] = np.where(c >= r, 0.0, NEG)
    cm[:, 256:384] = np.where(r > c, 0.0, NEG)
    cm[:, 384:512] = np.where(c > r, -1.0, 0.0)
    cm[:, 512:640] = np.where(c >= r, 1.0, 0.0)
    cm[:, 640:768] = np.where(c >= r, -1.0, 0.0)
    cm[:, 768:896] = np.where(r > c, -1.0, 0.0)
    cm[:, 896:1024] = ((r // 64) == (c // 64))
    return cg, cm


def prep_shared(inp):
    m = {}
    m["cstb"], m["cstf"] = consts(np.asarray(inp["sb_bias"][0], np.float32))
    m["cstg"], m["cstm"] = consts2()
    wi = inp["ev_w_in"][0]
    m["w_bg"] = np.ascontiguousarray(wi[:, 2048:2056].reshape(8, 128, 8).transpose(1, 0, 2))
    dtb = np.zeros((128, 8), np.float32)
    dtb[:, 0:4] = inp["gdn_dt_bias"][0][None, :]
    dtb[:, 4:8] = inp["gdn_a_log"][0][None, :]
    m["dtb"] = dtb
    cw = inp["gdn_conv_w"][0]
    m["convw"] = np.ascontiguousarray(cw.reshape(4, 12, 128).transpose(2, 1, 0))
    m["gnrm"] = np.ascontiguousarray(inp["gdn_norm"][0].reshape(128, 1))
    gcols = []
    for h in range(4):
        cols = np.concatenate([np.arange(j * 512 + h * 128, j * 512 + (h + 1) * 128) for j in range(4)])
        gcols.append(wi[:, cols])
    m["w_gdn"] = np.stack([w_slabs(g, [(0, 512)])[0] for g in gcols])
    m["w_evo"] = w_slabs(inp["ev_w_out"][0], [(0, 512), (512, 1024)])
    RB = 2056
    wr = wi[:, RB:RB + 1792]
    m["w_rwl"] = w_slabs(wr, [(1536, 1792)])[0]
    pc = []
    for p in range(4):
        cols = np.concatenate([np.arange(j * 512 + p * 128, j * 512 + (p + 1) * 128) for j in range(3)])
        pc.append(w_slabs(wr[:, cols], [(0, 384)])[0])
    m["w_rwp"] = np.stack(pc)
    rwp = np.zeros((128, 42), np.float32)
    mu = inp["rwkv_mu"][0]
    for p in range(4):
        for j in range(3):
            rwp[:, p * 3 + j] = mu[j * 512 + p * 128: j * 512 + (p + 1) * 128]
    rwp[:, 12] = mu[1536:1664]
    rwp[:, 13] = mu[1664:1792]
    for wi_, key in enumerate(["rwkv_w0", "rwkv_a0", "rwkv_k_k", "rwkv_k_a", "rwkv_r_k", "rwkv_gn_g", "rwkv_gn_b"]):
        rwp[:, 14 + wi_ * 4: 18 + wi_ * 4] = col_param(inp[key][0], 4)
    m["rwp"] = rwp
    w2a = np.zeros((128, 4, 128), np.float32)
    w2a[0:64] = inp["rwkv_w2"][0].reshape(64, 4, 128)
    w2a[64:128] = inp["rwkv_a2"][0].reshape(64, 4, 128)
    m["w2a"] = w2a
    m["g2"] = np.ascontiguousarray(inp["rwkv_g2"][0].reshape(128, 4, 128))
    m["w_sbi"] = w_slabs(inp["sb_w_in"][0], [(i * 512, (i + 1) * 512) for i in range(6)])
    m["w_sbo"] = w_slabs(inp["sb_w_out"][0], [(0, 512), (512, 1024)])
    g = [inp["norm_mix"][0], inp["norm_mix"][1], inp["norm_mem"][0], inp["norm_mem"][1],
         inp["norm_memtok"][0], inp["norm_memtok"][1], inp["norm_ffn"][0], inp["norm_ffn"][1]]
    m["gcols"] = np.stack([col_param(v, 8) for v in g])
    m["gfinal"] = col_param(inp["norm_final"], 8)
    r2 = [(0, 512), (512, 1024)]
    for nm, key in (("w_mq", "mem_w_q"), ("w_mk", "mem_w_k"), ("w_mv", "mem_w_v"), ("w_mo", "mem_w_o")):
        m[nm] = np.stack([w_slabs(inp[key][L], r2) for L in range(2)])
    rg = [(i * 512, min((i + 1) * 512, DFF)) for i in range(6)]
    rr = rg + [(DFF + a, DFF + b) for a, b in rg]
    m["w_ffi"] = np.stack([w_slabs(inp["ffn_w_in"][L], rr) for L in range(2)])
    wo = np.zeros((2, 6, 128, 4, 1024), np.float32)
    for L in range(2):
        w = inp["ffn_w_out"][L].reshape(22, 128, D)
        for hg in range(6):
            nch = 4 if hg < 5 else 2
            wo[L, hg, :, :nch, :] = w[hg * 4:hg * 4 + nch].transpose(1, 0, 2)
    m["w_ffo"] = wo
    return m


def odd_phase(self):
    dr = self.dr
    S = self.S
    nc = self.nc
    self.arena_reset()
    self.ps_n = 4
    HG = self.arena([128, 8, 512], BF16)
    QT = self.arena([128, 8, 512], BF16)
    OTA = self.arena([128, 8, 512], BF16)
    tmp = self.norm_tmp(sq=QT, sqkeys=[("qt", k) for k in range(8)])
    STG = [self.arena([128, 512], F32) for _ in range(2)]
    mark = self.ar_off
    KT = self.arena([128, 8, NP], BF16)
    nkb = NP // 128
    V = self.arena([128, nkb, 1024], BF16)
    EB = [self.arena([128, 512], BF16) for _ in range(2)]
    SPB = [self.arena([128, 512], BF16) for _ in range(2)]
    TB = [self.arena([128, 512], F32) for _ in range(2)]
    ATT = [self.arena([128, 512], BF16) for _ in range(2)]
    stg_i = [0]

    def stage_out(ps_ap, pskey, n, dst, bf_dst, bf_key, npart=128):
        i = stg_i[0]
        stg_i[0] ^= 1
        self.copy(STG[i][0:npart, 0:n], ps_ap, [pskey], [("stg", i)], eng="dve")
        self.copy(bf_dst, STG[i][0:npart, 0:n], [("stg", i)], [bf_key], eng="pool")
        self.dma(dst, STG[i][0:npart, 0:n], [("stg", i)], [], f"stg{i}")

    rot = self.psum
    blk = 0
    for g, (t0, n) in enumerate(TILES):
        prompt = g < len(TILES) - 1
        if not prompt:
            S.barrier()
            self.ar_off = mark
        xk = [("X", k, g) for k in range(8)]
        hk = [("hg", k) for k in range(8)]
        self.rmsnorm(lambda k: self.X[:, k, t0:t0 + n], xk, self.G["mix"][1], lambda k: HG[:, k, 0:n], hk, n, tmp)
        for s in range(2):
            slot = self.load_slab(dr["w_sbi"][s])

            def consume(j, ti, ps, pskey, s=s):
                self.copy(QT[:, s * 4 + j, 0:n], ps, [pskey], [("qt", s * 4 + j)], eng="act")
            self.proj_fm(slot, 512, None, [(hk, lambda k: HG[:, k, 0:n], n)], consume)
        if not prompt:
            KN = self.arena([128, 8, 32], BF16)
            VN = self.arena([8, 4, 1024], BF16) if False else self.arena([128, 4, 1024], BF16)
        for s in range(2):
            slot = self.load_slab(dr["w_sbi"][2 + s])

            def consume(j, ti, ps, pskey, s=s):
                oc = s * 4 + j
                dst = KT[:, oc, t0:t0 + n] if prompt else KN[:, oc, 0:n]
                stage_out(ps, pskey, n, dr["o_sbkT"][oc * 128:(oc + 1) * 128, t0:t0 + n], dst, ("kt", oc, g))
            self.proj_fm(slot, 512, None, [(hk, lambda k: HG[:, k, 0:n], n)], consume)
        for s in range(2):
            slot = self.load_slab(dr["w_sbi"][4 + s])
            w = self.WS[slot]
            if prompt:
                for tb in range(4):
                    pi, ps = self.psum()
                    for k in range(8):
                        self.mm(ps[:, :], HG[:, k, tb * 128:(tb + 1) * 128], w[:, k, :], k == 0, k == 7,
                                [("w", slot), ("hg", k)], [("ps", pi)])
                    kb = g * 4 + tb
                    stage_out(ps[:, :], ("ps", pi), 512, dr["o_sbv"][t0 + tb * 128:t0 + (tb + 1) * 128, s * 512:(s + 1) * 512],
                              V[:, kb, s * 512:(s + 1) * 512], ("v", kb, s))
            else:
                for sq in range(4):
                    pi, ps = self.psum()
                    for k in range(8):
                        self.mm(ps[0:8, :], HG[:, k, sq * 8:(sq + 1) * 8], w[:, k, :], k == 0, k == 7,
                                [("w", slot), ("hg", k)], [("ps", pi)])
                    stage_out(ps[0:8, :], ("ps", pi), 512, dr["o_sbv"][t0 + sq * 8:t0 + (sq + 1) * 8, s * 512:(s + 1) * 512],
                              VN[0:8, sq, s * 512:(s + 1) * 512], ("vn", sq, s), npart=8)
        if prompt:
            nkv = (g + 1) * 4
            for h in range(16):
                c, po = h // 2, (h % 2) * 64
                ch = h % 2
                pc, psc = 4 + ch * 2, self.PS[4 + ch * 2]
                pob, pso = 5 + ch * 2, self.PS[5 + ch * 2]
                oap = pso[po:po + 64, :]
                for bi, kb in enumerate(range(nkv - 1, -1, -1)):
                    b = blk % 2
                    blk += 1
                    pz, psz = rot()
                    diag = kb >= g * 4
                    self.mm(psz[:, :], KT[po:po + 64, c, kb * 128:(kb + 1) * 128], QT[po:po + 64, c, 0:512], True, not diag,
                            [("kt", c, kb // 4), ("qt", c)], [("ps", pz)])
                    if diag:
                        off = 384 - 128 * (kb - g * 4)
                        self.mm(psz[:, :], self.IDENT[:, :], self.NEGM[:, off:off + 512], False, True, [("const",)], [("ps", pz)])
                    self.act(EB[b][:, :], psz[:, :], AF.Exp, [("ps", pz)], [("eb", b)], bias=self.BIASH[:, h:h + 1], scale=0.125)
                    self.act(SPB[b][:, :], EB[b][:, :], AF.Ln, [("eb", b)], [("spb", b)], bias=self.ONEC[:, 0:1])
                    self.mm(psc[:, :], self.TRIU[:, :], SPB[b][:, :], bi == 0, True, [("spb", b), ("const",)], [("ps", pc)], skip=True)
                    self.tt(TB[b][:, :], SPB[b][:, :], psc[:, :], ALU.add, [("spb", b), ("ps", pc)], [("tb", b)])
                    self.mm(psc[:, :], self.TRIL[:, :], SPB[b][:, :], False, True, [("spb", b), ("const",)], [("ps", pc)], skip=True)
                    self.act(TB[b][:, :], TB[b][:, :], AF.Exp, [("tb", b)], [("tb", b)], scale=-1.0)
                    self.tt(ATT[b][:, :], EB[b][:, :], TB[b][:, :], ALU.mult, [("eb", b), ("tb", b)], [("att", b)], eng="pool")
                    self.mm(oap, V[:, kb, h * 64:(h + 1) * 64], ATT[b][:, :], bi == 0, kb == 0,
                            [("v", kb, h // 8), ("att", b)], [("ps", pob)])
                self.copy(OTA[po:po + 64, c, 0:512], oap, [("ps", pob)], [("ota", c, ch)], eng="act")
        else:
            odd_sample(self, HG, QT, OTA, KN, VN, rot)
        for s in range(2):
            slot = self.load_slab(dr["w_sbo"][s])

            def consume(j, ti, ps, pskey, s=s):
                oc = s * 4 + j
                self.tt(self.X[:, oc, t0:t0 + n], self.X[:, oc, t0:t0 + n], ps, ALU.add, [pskey, ("X", oc, g)], [("X", oc, g)])
            rk = [("ota", k, 0) for k in range(8)] + [("ota", k, 1) for k in range(8)]
            self.proj_fm(slot, 512, None, [(rk, lambda k: OTA[:, k, 0:n], n)], consume)
    self.ps_n = 8
    S.barrier()


KB.odd_phase = odd_phase


def odd_sample(self, HG, QT, OTA, KN, VN, rot):
    dr = self.dr
    nc = self.nc
    NPG = 64
    QB = self.arena([128, 8, 128], BF16)
    E = [self.arena([128, 512], BF16) for _ in range(2)]
    SP = [self.arena([128, 512], BF16) for _ in range(2)]
    CS = [self.arena([128, 512], F32) for _ in range(2)]
    WT = [self.arena([128, 512], F32) for _ in range(2)]
    AT = [self.arena([128, 512], BF16) for _ in range(2)]
    ATTT = [self.arena([128, 4, 128], BF16) for _ in range(2)]
    KTP = [self.arena([128, 8, 128], BF16) for _ in range(8)]
    VP = [self.arena([128, 1024], BF16) for _ in range(8)]
    AM = self.arena([128, 1024], BF16)
    CAR = self.arena([128, 4], F32)
    IDX = self.arena([128, 256], I32)
    PTB = self.arena([128, 256], I32)
    self.dma(PTB[:, :], dr["pt"].partition_broadcast(128), [], [("ptb",)], "ptb")
    self.ts(IDX[:, :], PTB[:, :], 128.0, self.PIDX[:, 0:1], ALU.mult, ALU.add, [("ptb",), ("const",)], [("idx",)])
    self.memset(QB[:, :, :], 0.0, [("qb",)])
    pg_i = 0
    cnt = 0
    for sq in range(4):
        for c in range(8):
            for hh in range(2):
                h = 2 * c + hh
                self.copy(QB[hh * 64:(hh + 1) * 64, c, h * 8:(h + 1) * 8], QT[hh * 64:(hh + 1) * 64, c, sq * 8:(sq + 1) * 8],
                          [("qt", c)], [("qb",)], eng="pool")
        self.memset(CAR[:, 0:1], 0.0, [("car",)], eng="dve")
        pfa, psfa = 4, self.PS[4]
        pfb, psfb = 5, self.PS[5]
        chunks = [("new", None)] + [("pg", pgp) for pgp in range(15, -1, -1)]
        first_v = True
        for ci, (kind, pgp) in enumerate(chunks):
            b = cnt % 2
            cnt += 1
            nk = 8 if kind == "new" else 512
            pz, psz = rot()
            slots = []
            if kind == "new":
                for c in range(8):
                    self.mm(psz[:, 0:8], QB[:, c, :], KN[:, c, sq * 8:(sq + 1) * 8], c == 0, c == 7,
                            [("qb",), ("kt", c, len(TILES) - 1)], [("ps", pz)])
            else:
                for jj in range(4):
                    j = pgp * 4 + jj
                    sl = pg_i % 8
                    pg_i += 1
                    slots.append(sl)
                    col = sq * NPG + j
                    self.S.op("pool", (lambda sl=sl, col=col: nc.gpsimd.indirect_dma_start(
                        out=KTP[sl].rearrange("p a b -> p (a b)"), out_offset=None, in_=dr["poolk"][:, :],
                        in_offset=bass.IndirectOffsetOnAxis(ap=IDX[:, col:col + 1], axis=0))),
                        [("idx",)], [("ktp", sl)], dsem=f"ktp{sl}")
                    self.S.op("pool", (lambda sl=sl, col=col: nc.gpsimd.indirect_dma_start(
                        out=VP[sl][:, :], out_offset=None, in_=dr["poolv"][:, :],
                        in_offset=bass.IndirectOffsetOnAxis(ap=IDX[:, col:col + 1], axis=0))),
                        [("idx",)], [("vp", sl)], dsem=f"vp{sl}")
                    for c in range(8):
                        self.mm(psz[:, jj * 128:(jj + 1) * 128], QB[:, c, :], KTP[sl][:, c, :], c == 0, c == 7,
                                [("qb",), ("ktp", sl)], [("ps", pz)])
            self.act(E[b][:, 0:nk], psz[:, 0:nk], AF.Exp, [("ps", pz)], [("e", b)], bias=self.BIASQ[:, 0:1], scale=0.125)
            if kind == "new":
                self.tt(E[b][:, 0:8], E[b][:, 0:8], self.MASK8[:, :], ALU.mult, [("e", b), ("const",)], [("e", b)])
            self.act(SP[b][:, 0:nk], E[b][:, 0:nk], AF.Ln, [("e", b)], [("sp", b)], bias=self.ONEC[:, 0:1])
            self.S.op("dve", (lambda b=b, nk=nk: nc.vector.tensor_tensor_scan(
                CS[b][:, 0:nk], self.ONEC[:, 0:1].to_broadcast([128, nk]), SP[b][:, 0:nk], 0.0, ALU.mult, ALU.add)),
                [("sp", b), ("const",)], [("cs", b)])
            self.tt(CAR[:, 1:2], CAR[:, 0:1], CS[b][:, nk - 1:nk], ALU.add, [("car",), ("cs", b)], [("car1",)])
            self.ts(CAR[:, 2:3], CAR[:, 1:2], -1.0, None, ALU.mult, None, [("car1",)], [("car2",)])
            self.act(WT[b][:, 0:nk], CS[b][:, 0:nk], AF.Exp, [("cs", b), ("car2",)], [("wt", b)], bias=CAR[:, 2:3])
            self.copy(CAR[:, 0:1], CAR[:, 1:2], [("car1",), ("car2",)], [("car",)])
            self.tt(AT[b][:, 0:nk], E[b][:, 0:nk], WT[b][:, 0:nk], ALU.mult, [("e", b), ("wt", b)], [("at", b)], eng="pool")
            pt_, pst = rot()
            pstb = pst[:].bitcast(BF16)
            nblk = 1 if kind == "new" else 4
            for jj in range(nblk):
                kk = 8 if kind == "new" else 128
                self.S.op("pe", (lambda jj=jj, kk=kk, b=b, pstb=pstb: nc.tensor.transpose(
                    pstb[0:kk, jj * 128:(jj + 1) * 128], AT[b][:, jj * 128:jj * 128 + kk], self.IDENT[:, :])),
                    [("at", b), ("const",)], [("ps", pt_)])
            kk = 8 if kind == "new" else 128
            self.copy(ATTT[b][0:kk, 0:nblk, :], pstb[0:kk, 0:nblk * 128].rearrange("p (a b) -> p a b", b=128),
                      [("ps", pt_)], [("attt", b)], eng="dve")
            for jj in range(nblk):
                last = (ci == len(chunks) - 1) and (jj == nblk - 1)
                if kind == "new":
                    rv = lambda half: VN[0:8, sq, half * 512:(half + 1) * 512]
                    rk = [("vn", sq, 0), ("vn", sq, 1)]
                else:
                    sl = slots[jj]
                    rv = lambda half, sl=sl: VP[sl][:, half * 512:(half + 1) * 512]
                    rk = [("vp", sl)]
                self.mm(psfa[:, :], ATTT[b][0:kk, jj, :], rv(0), first_v, last, [("attt", b)] + rk, [("ps", pfa)])
                self.mm(psfb[:, :], ATTT[b][0:kk, jj, :], rv(1), first_v, last, [("attt", b)] + rk, [("ps", pfb)])
                first_v = False
        self.tt(AM[:, 0:512], psfa[:, :], self.BLKM[:, 0:512], ALU.mult, [("ps", pfa), ("const",)], [("am", 0)])
        self.tt(AM[:, 512:1024], psfb[:, :], self.BLKM[:, 512:1024], ALU.mult, [("ps", pfb), ("const",)], [("am", 1)])
        po_, pso = rot()
        for c in range(8):
            self.mm(pso[:, c * 8:(c + 1) * 8], AM[:, c * 128:(c + 1) * 128], self.SEL[:, :], True, True,
                    [("am", c // 4), ("const",)], [("ps", po_)])
        self.copy(OTA[:, :, sq * 8:(sq + 1) * 8], pso[:, 0:64].rearrange("p (a b) -> p a b", b=8), [("ps", po_)],
                  [("ota", k, 0) for k in range(8)] + [("ota", k, 1) for k in range(8)], eng="act")


def neumann_TT(self, Mt, Nt, TT, nch, C, mk, nk, tk, tag):
    nc = self.nc
    for c in range(nch):
        self.tt(TT[0:C, c, 0:C], Mt[0:C, c, 0:C], self.IDF[0:C, 0:C], ALU.add, [mk], [tk], eng="pool")
    p = 1
    while 2 * p < C:
        last = 4 * p >= C
        pn, psn = self.psum()
        pm, psm = self.psum()
        for c in range(nch):
            self.mm(psn[0:C, c * C:(c + 1) * C], Mt[0:C, c, 0:C], Nt[0:C, c, 0:C], True, True, [mk, nk], [("ps", pn)])
            if not last:
                self.mm(psm[0:C, c * C:(c + 1) * C], Nt[0:C, c, 0:C], Mt[0:C, c, 0:C], True, True, [mk, nk], [("ps", pm)])
        self.copy(Nt[0:C, 0:nch, 0:C], psn[0:C, 0:nch * C].rearrange("p (a b) -> p a b", b=C), [("ps", pn)], [nk], eng="act")
        if not last:
            self.copy(Mt[0:C, 0:nch, 0:C], psm[0:C, 0:nch * C].rearrange("p (a b) -> p a b", b=C), [("ps", pm)], [mk], eng="dve")
        pt_, pst = self.psum()
        for c in range(nch):
            self.mm(pst[0:C, c * C:(c + 1) * C], Nt[0:C, c, 0:C], TT[0:C, c, 0:C], True, True, [nk, tk], [("ps", pt_)])
        self.tt(TT[0:C, 0:nch, 0:C], TT[0:C, 0:nch, 0:C], pst[0:C, 0:nch * C].rearrange("p (a b) -> p a b", b=C), ALU.add,
                [("ps", pt_), tk], [tk])
        p *= 2


KB.neumann_TT = neumann_TT


def transpose_f32(self, out_ps, in_ap, kpart, reads, pkey):
    nc = self.nc
    return self.S.op("pe", lambda: nc.tensor.transpose(out_ps, in_ap, self.IDF[0:kpart, 0:kpart]), reads, [pkey])


KB.transpose_f32 = transpose_f32


def even_phase(self):
    dr = self.dr
    S = self.S
    nc = self.nc
    self.arena_reset()
    A = self.arena
    HG = A([128, 8, 512], BF16)
    OA = A([128, 8, 512], BF16)
    tmp = self.norm_tmp(sq=OA, sqkeys=[("oa", k) for k in range(8)])
    SG = A([128, 4, 128], F32)
    HISTC = A([128, 12, 3], F32)
    PR = A([128, 4, 64], F32)
    HISTR = A([128, 16], F32)
    BGW = A([128, 8, 8], BF16)
    DTB = A([128, 8], F32)
    CONVW = A([128, 12, 4], F32)
    GNRM = A([128, 1], F32)
    mark = self.ar_off
    self.dma(BGW[:, :, :], dr["w_bg"], [], [("bgw",)], "bgw", eng="pool", max_dma_last_dim=4096)
    self.dma(DTB[:, :], dr["dtb"], [], [("dtb",)], "evp")
    self.dma(CONVW[:, :, :], dr["convw"], [], [("convw",)], "evp")
    self.dma(GNRM[:, :], dr["gnrm"], [], [("gnrm",)], "evp")
    self.act(DTB[:, 4:8], DTB[:, 4:8], AF.Exp, [("dtb",)], [("dtb",)])
    self.ts(DTB[:, 4:8], DTB[:, 4:8], -1.0, None, ALU.mult, None, [("dtb",)], [("dtb",)])
    self.memset(SG[:, :, :], 0.0, [("sg", h) for h in range(4)])
    self.memset(HISTC[:, :, :], 0.0, [("histc",)])
    ntl = len(TILES)
    for g, (t0, n) in enumerate(TILES):
        prompt = g < ntl - 1
        nch, C = (4, 128) if prompt else (4, 8)
        nseg, ntok = (1, 512) if prompt else (4, 8)
        S.barrier()
        self.ar_off = mark
        xk = [("X", k, g) for k in range(8)]
        hk = [("hg", k) for k in range(8)]
        self.rmsnorm(lambda k: self.X[:, k, t0:t0 + n], xk, self.G["mix"][0], lambda k: HG[:, k, 0:n], hk, n, tmp)
        BG = A([128, 4, 8], F32)
        LB = A([128, 4, 4], F32)
        GG = A([128, 4, 4], F32)
        GCL = A([128, 4, 8], F32)
        SM = A([128, 4, 16], F32)
        pi, ps = self.psum()
        for c in range(nch):
            for k in range(8):
                self.mm(ps[0:C, c * 8:(c + 1) * 8], HG[:, k, c * C:(c + 1) * C], BGW[:, k, :], k == 0, k == 7,
                        [("hg", k), ("bgw",)], [("ps", pi)])
        psv = ps[0:C, 0:nch * 8].rearrange("p (a b) -> p a b", b=8)
        self.copy(BG[0:C, 0:nch, :], psv, [("ps", pi)], [("bg",)], eng="dve")
        self.act(LB[0:C, 0:nch, :], BG[0:C, 0:nch, 0:4], AF.Exp, [("bg",)], [("lb",)], scale=-1.0)
        self.act(LB[0:C, 0:nch, :], LB[0:C, 0:nch, :], AF.Ln, [("lb",)], [("lb",)], bias=self.ONEC[0:C, 0:1])
        self.ts(LB[0:C, 0:nch, :], LB[0:C, 0:nch, :], -1.0, None, ALU.mult, None, [("lb",)], [("lb",)])
        for c in range(nch):
            self.tt(GG[0:C, c, :], BG[0:C, c, 4:8], DTB[0:C, 0:4], ALU.add, [("bg",), ("dtb",)], [("gg",)])
        self.act(GG[0:C, 0:nch, :], GG[0:C, 0:nch, :], AF.Exp, [("gg",)], [("gg",)])
        self.act(GG[0:C, 0:nch, :], GG[0:C, 0:nch, :], AF.Ln, [("gg",)], [("gg",)], bias=self.ONEC[0:C, 0:1])
        for c in range(nch):
            self.tt(GG[0:C, c, :], GG[0:C, c, :], DTB[0:C, 4:8], ALU.mult, [("gg",), ("dtb",)], [("gg",)])
        pi, ps = self.psum()
        for c in range(nch):
            self.mm(ps[0:C, c * 8:c * 8 + 4], self.TRIF[0:C, 0:C], GG[0:C, c, :], True, True, [("gg",)], [("ps", pi)])
            self.mm(ps[0:C, c * 8 + 4:c * 8 + 8], self.ONESF[0:C, 0:C], GG[0:C, c, :], True, True, [("gg",)], [("ps", pi)])
        self.copy(GCL[0:C, 0:nch, :], ps[0:C, 0:nch * 8].rearrange("p (a b) -> p a b", b=8), [("ps", pi)], [("gcl",)], eng="dve")
        self.act(SM[0:C, 0:nch, 0:4], LB[0:C, 0:nch, :], AF.Exp, [("lb",)], [("sm",)])
        self.tt(SM[0:C, 0:nch, 12:16], GCL[0:C, 0:nch, 0:4], LB[0:C, 0:nch, :], ALU.add, [("gcl",), ("lb",), ("sm",)], [("sm",)])
        self.act(SM[0:C, 0:nch, 4:8], SM[0:C, 0:nch, 12:16], AF.Exp, [("sm",)], [("sm",)])
        self.tt(SM[0:C, 0:nch, 12:16], GCL[0:C, 0:nch, 4:8], GCL[0:C, 0:nch, 0:4], ALU.subtract, [("gcl",), ("sm",)], [("sm",)])
        self.act(SM[0:C, 0:nch, 8:12], SM[0:C, 0:nch, 12:16], AF.Exp, [("sm",)], [("sm",)])
        mark_g = self.ar_off
        if cfg_get(self, "gdn", True):
            for h in range(4):
                gdn_head(self, h, g, t0, n, nch, C, nseg, ntok, HG, OA, SG, HISTC, CONVW, GNRM, LB, GG, SM, tmp)
        else:
            for h in range(4):
                self.memset(OA[:, h, 0:n], 0.0, [("oa", h)], eng="pool")
        if cfg_get(self, "rwkv", True):
            rwkv_part(self, g, t0, n, nch, C, nseg, ntok, HG, OA, PR, HISTR, tmp, mark2=mark_g)
        else:
            for h in range(4, 8):
                self.memset(OA[:, h, 0:n], 0.0, [("oa", h)], eng="pool")
        ok = [("oa", k) for k in range(8)]
        for s in range(2):
            slot = self.load_slab(dr["w_evo"][s])

            def consume(j, ti, ps, pskey, s=s):
                oc = s * 4 + j
                self.tt(self.X[:, oc, t0:t0 + n], self.X[:, oc, t0:t0 + n], ps, ALU.add, [pskey, ("X", oc, g)], [("X", oc, g)])
            self.proj_fm(slot, 512, None, [(ok, lambda k: OA[:, k, 0:n], n)], consume)
    S.barrier()


KB.even_phase = even_phase


def cfg_get(self, k, d):
    return self.cfg.get(k, d)


def gdn_head(self, h, g, t0, n, nch, C, nseg, ntok, HG, OA, SG, HISTC, CONVW, GNRM, LB, GG, SM, tmp):
    dr = self.dr
    nc = self.nc
    A = self.arena
    ntl = len(TILES)
    prompt = g < ntl - 1
    first_alloc = not hasattr(self, "_gdnbuf") or self._gdnbuf[0] != g
    if first_alloc:
        b = {}
        b["U"] = [A([128, nseg, 3 + ntok], F32) for _ in range(3)]
        b["CV"] = [A([128, 512], F32) for _ in range(3)]
        b["ZS"] = A([128, 512], F32)
        b["SQ"] = A([128, 512], BF16)
        b["KBG"] = A([128, 4, 128], F32)
        b["KE"] = A([128, 4, 128], F32)
        b["VT"] = A([128, 4, 128], F32)
        b["YG"] = A([128, 4, 128], F32)
        b["YGB"] = A([128, 4, 128], F32)
        b["EX"] = [A([128, 4, 128], F32) for _ in range(3)]
        b["GAMB"] = A([128, 512], F32)
        b["MT"] = A([128, 4, 128], F32)
        b["NT"] = A([128, 4, 128], F32)
        b["AQ"] = A([128, 4, 128], F32)
        b["TT"] = A([128, 4, 128], F32)
        b["TBT"] = A([128, 4, 128], F32)
        b["GNT"] = A([128, 512], F32)
        b["QG"] = A([128, 512], F32)
        b["US"] = A([128, 128], F32)
        b["OT"] = A([128, 512], F32)
        b["SS"] = A([128, 128], F32)
        self._gdnbuf = (g, b)
    b = self._gdnbuf[1]
    U, CV, ZS, SQ = b["U"], b["CV"], b["ZS"], b["SQ"]
    KBG, KE, VT, YG, YGB, EX, GAMB = b["KBG"], b["KE"], b["VT"], b["YG"], b["YGB"], b["EX"], b["GAMB"]
    MT, NT_, AQ, TT, TBT, GNT, QG, US, OT, SS = b["MT"], b["NT"], b["AQ"], b["TT"], b["TBT"], b["GNT"], b["QG"], b["US"], b["OT"], b["SS"]
    r1, r2 = tmp["r1"], tmp["r2"]
    for j in range(3):
        if prompt:
            self.copy(U[j][:, 0, 0:3], HISTC[:, j * 4 + h, :], [("histc",)], [("u", j)], eng="pool")
        else:
            self.dma(U[j][:, :, 0:3], dr["convT"][:, j * 512 + h * 128: j * 512 + (h + 1) * 128, :].rearrange("s p t -> p s t"),
                     [], [("u", j)], "uh")
    slot = self.load_slab(dr["w_gdn"][h])
    hk = [("hg", k) for k in range(8)]

    def consume(j, ti, ps, pskey):
        if j < 3:
            self.copy(U[j][:, :, 3:3 + ntok], ps.rearrange("p (s t) -> p s t", s=nseg), [pskey], [("u", j)], eng="act")
        else:
            self.act(ZS[:, 0:n], ps, AF.Silu, [pskey], [("zs",)])
    self.proj_fm(slot, 512, None, [(hk, lambda k: HG[:, k, 0:n], n)], consume)
    for j in range(3):
        if prompt:
            self.copy(HISTC[:, j * 4 + h, :], U[j][:, 0, ntok:ntok + 3], [("u", j)], [("histc",)], eng="pool")
            if g == ntl - 2:
                self.dma(dr["o_convT"][j * 512 + h * 128: j * 512 + (h + 1) * 128, :], U[j][:, 0, ntok:ntok + 3], [("u", j)], [], "oc")
        else:
            self.dma(dr["o_convTs"][:, j * 512 + h * 128: j * 512 + (h + 1) * 128, :].rearrange("s p t -> p s t"),
                     U[j][:, :, ntok:ntok + 3], [("u", j)], [], "oc")
    for j in range(3):
        cv = CV[j][:, 0:n].rearrange("p (s t) -> p s t", s=nseg)
        wc = CONVW[:, j * 4 + h, :]
        self.ts(cv, U[j][:, :, 0:ntok], wc[:, 0:1], None, ALU.mult, None, [("u", j), ("convw",)], [("cv", j)])
        for tp in range(1, 4):
            self.stt(cv, U[j][:, :, tp:tp + ntok], wc[:, tp:tp + 1], cv, ALU.mult, ALU.add, [("u", j), ("cv", j), ("convw",)], [("cv", j)])
        self.act(CV[j][:, 0:n], CV[j][:, 0:n], AF.Silu, [("cv", j)], [("cv", j)])
    for j in range(2):
        self.act(SQ[:, 0:n], CV[j][:, 0:n], AF.Square, [("cv", j)], [("sq",)])
        pi, ps = self.psum()
        self.mm(ps[:, 0:n], self.ONES[:, :], SQ[:, 0:n], True, True, [("sq",)], [("ps", pi)])
        self.act(r1[:, 0:n], ps[:, 0:n], AF.Ln, [("ps", pi)], [("r1", tmp["id"])], bias=self.EPSC[:, 0:1])
        self.act(r2[:, 0:n], r1[:, 0:n], AF.Exp, [("r1", tmp["id"])], [("r2", tmp["id"])], scale=-0.5)
        if j == 0:
            self.stt(CV[0][:, 0:n], CV[0][:, 0:n], 128.0 ** -0.5, r2[:, 0:n], ALU.mult, ALU.mult, [("cv", 0), ("r2", tmp["id"])], [("cv", 0)])
        else:
            self.tt(CV[1][:, 0:n], CV[1][:, 0:n], r2[:, 0:n], ALU.mult, [("cv", 1), ("r2", tmp["id"])], [("cv", 1)])
    QN, KN, VV = CV[0], CV[1], CV[2]
    pa, psa = self.psum()
    pb, psb = self.psum()
    for c in range(nch):
        self.transpose_f32(psa[0:C, c * 128:(c + 1) * 128], KN[:, c * C:(c + 1) * C], 128, [("cv", 1)], ("ps", pa))
        self.transpose_f32(psb[0:C, c * 128:(c + 1) * 128], VV[:, c * C:(c + 1) * C], 128, [("cv", 2)], ("ps", pb))
    for c in range(nch):
        self.ts(KBG[0:C, c, :], psa[0:C, c * 128:(c + 1) * 128], SM[0:C, c, 4 + h:5 + h], None, ALU.mult, None,
                [("ps", pa), ("sm",)], [("kbg",)])
        self.ts(KE[0:C, c, :], psa[0:C, c * 128:(c + 1) * 128], SM[0:C, c, 8 + h:9 + h], None, ALU.mult, None,
                [("ps", pa), ("sm",)], [("ke",)])
    self.copy(VT[0:C, 0:nch, :], psb[0:C, 0:nch * 128].rearrange("p (a b) -> p a b", b=128), [("ps", pb)], [("vt",)], eng="act")
    for c in range(nch):
        self.ts(YG[0:C, c, 0:C], self.TRIF[0:C, 0:C], GG[0:C, c, h:h + 1], None, ALU.mult, None, [("gg",)], [("yg",)], eng="pool")
        self.stt(YGB[0:C, c, 0:C], self.IDF[0:C, 0:C], LB[0:C, c, h:h + 1], YG[0:C, c, 0:C], ALU.mult, ALU.add,
                 [("lb",), ("yg",)], [("ygb",)])
    specs = [
        ("ones", "ygb", "yg", "neg", self.MSU),
        ("ygb", "ones", "neg", "yg", self.MSL),
        ("ones", "yg", "yg", "neg", self.MIU),
    ]
    for e, (l1, r1_, l2, r2_, msk) in enumerate(specs):
        pi, ps = self.psum()
        for c in range(nch):
            def opnd(nm):
                if nm == "ones":
                    return self.ONESF[0:C, 0:C]
                if nm == "neg":
                    return self.NEGF[0:C, 0:C]
                if nm == "yg":
                    return YG[0:C, c, 0:C]
                return YGB[0:C, c, 0:C]
            o = ps[0:C, c * C:(c + 1) * C]
            self.mm(o, opnd(l1), opnd(r1_), True, False, [("yg",), ("ygb",)], [("ps", pi)])
            self.mm(o, opnd(l2), opnd(r2_), False, False, [("yg",), ("ygb",)], [("ps", pi)])
            self.mm(o, self.IDENT[0:C, 0:C], msk[0:C, 0:C], False, True, [], [("ps", pi)])
        self.act(EX[e][0:C, 0:nch, 0:C], ps[0:C, 0:nch * C].rearrange("p (a b) -> p a b", b=C), AF.Exp, [("ps", pi)], [("ex", e)])
    pi, ps = self.psum()
    for c in range(nch):
        self.mm(ps[:, c * C:(c + 1) * C], self.ONESF[0:C, :], YG[0:C, c, 0:C], True, True, [("yg",)], [("ps", pi)])
    self.act(GAMB[:, 0:n], ps[:, 0:n], AF.Exp, [("ps", pi)], [("gamb",)])
    pk_, psk = self.psum()
    pq_, psq = self.psum()
    for c in range(nch):
        self.mm(psk[0:C, c * C:(c + 1) * C], KN[:, c * C:(c + 1) * C], KN[:, c * C:(c + 1) * C], True, True, [("cv", 1)], [("ps", pk_)])
        self.mm(psq[0:C, c * C:(c + 1) * C], KN[:, c * C:(c + 1) * C], QN[:, c * C:(c + 1) * C], True, True, [("cv", 0), ("cv", 1)], [("ps", pq_)])
    v3 = lambda p_: p_[0:C, 0:nch * C].rearrange("p (a b) -> p a b", b=C)
    self.stt(MT[0:C, 0:nch, 0:C], v3(psk), -1.0, EX[0][0:C, 0:nch, 0:C], ALU.mult, ALU.mult, [("ps", pk_), ("ex", 0)], [("mt",)])
    self.stt(NT_[0:C, 0:nch, 0:C], v3(psk), -1.0, EX[1][0:C, 0:nch, 0:C], ALU.mult, ALU.mult, [("ps", pk_), ("ex", 1)], [("nt",)])
    self.tt(AQ[0:C, 0:nch, 0:C], v3(psq), EX[2][0:C, 0:nch, 0:C], ALU.mult, [("ps", pq_), ("ex", 2)], [("aq",)])
    self.neumann_TT(MT, NT_, TT, nch, C, ("mt",), ("nt",), ("tt",), "g")
    for c in range(nch):
        self.ts(TBT[0:C, c, 0:C], TT[0:C, c, 0:C], SM[0:C, c, h:h + 1], None, ALU.mult, None, [("tt",), ("sm",)], [("tbt",)], eng="pool")
    pi, ps = self.psum()
    for c in range(nch):
        self.mm(ps[:, c * C:(c + 1) * C], KBG[0:C, c, :], TT[0:C, c, 0:C], True, True, [("kbg",), ("tt",)], [("ps", pi)])
    self.act(GNT[:, 0:n], ps[:, 0:n], AF.Identity, [("ps", pi)], [("gnt",)], scale=-1.0)
    self.tt(QG[:, 0:n], QN[:, 0:n], GAMB[:, 0:n], ALU.mult, [("cv", 0), ("gamb",)], [("qg",)], eng="pool")
    for c in range(nch):
        cs = slice(c * C, (c + 1) * C)
        if prompt:
            Sst, skey = SG[:, h, :], ("sg", h)
        else:
            Sst, skey = SS[:, :], ("ss",)
            self.dma(SS[:, :], dr["gdn_s0"][c, h], [], [("ss",)], "ss")
        pu, psu = self.psum()
        self.mm(psu[0:C, 0:128], TBT[0:C, c, 0:C], VT[0:C, c, :], True, False, [("tbt",), ("vt",)], [("ps", pu)])
        self.mm(psu[0:C, 0:128], GNT[:, cs], Sst, False, True, [("gnt",), skey], [("ps", pu)])
        self.copy(US[0:C, :], psu[0:C, 0:128], [("ps", pu)], [("us",)], eng="act")
        po, pso = self.psum()
        self.mm(pso[:, 0:C], Sst, QG[:, cs], True, False, [skey, ("qg",)], [("ps", po)])
        self.mm(pso[:, 0:C], US[0:C, :], AQ[0:C, c, 0:C], False, True, [("us",), ("aq",)], [("ps", po)])
        pc_, psc = self.psum()
        self.mm(psc[:, 0:128], KE[0:C, c, :], US[0:C, :], True, True, [("ke",), ("us",)], [("ps", pc_)])
        self.stt(Sst, Sst, GAMB[:, (c + 1) * C - 1:(c + 1) * C], psc[:, 0:128], ALU.mult, ALU.add,
                 [skey, ("gamb",), ("ps", pc_)], [skey])
        self.copy(OT[:, cs], pso[:, 0:C], [("ps", po)], [("ot",)], eng="act")
        if not prompt:
            self.dma(dr["o_gdn_s"][c, h], SS[:, :], [("ss",)], [], "sso")
    if prompt and g == ntl - 2:
        self.dma(dr["o_gdn_p"][h], SG[:, h, :], [("sg", h)], [], "sso")
    self.act(SQ[:, 0:n], OT[:, 0:n], AF.Square, [("ot",)], [("sq",)])
    pi, ps = self.psum()
    self.mm(ps[:, 0:n], self.ONES[:, :], SQ[:, 0:n], True, True, [("sq",)], [("ps", pi)])
    self.act(r1[:, 0:n], ps[:, 0:n], AF.Ln, [("ps", pi)], [("r1", tmp["id"])], bias=self.EPSC[:, 0:1], scale=1.0 / 128.0)
    self.act(r2[:, 0:n], r1[:, 0:n], AF.Exp, [("r1", tmp["id"])], [("r2", tmp["id"])], scale=-0.5)
    self.stt(OT[:, 0:n], OT[:, 0:n], GNRM[:, 0:1], r2[:, 0:n], ALU.mult, ALU.mult, [("ot",), ("r2", tmp["id"]), ("gnrm",)], [("ot",)])
    self.tt(OA[:, h, 0:n], OT[:, 0:n], ZS[:, 0:n], ALU.mult, [("ot",), ("zs",)], [("oa", h)])


def rwkv_part(self, g, t0, n, nch, C, nseg, ntok, HG, OA, PR, HISTR, tmp, mark2):
    dr = self.dr
    nc = self.nc
    A = self.arena
    S = self.S
    ntl = len(TILES)
    prompt = g < ntl - 1
    S.barrier()
    self.ar_off = mark2
    r1, r2 = tmp["r1"], tmp["r2"]
    r1k, r2k = ("r1", tmp["id"]), ("r2", tmp["id"])
    RWP = A([128, 42], F32)
    W2A = A([128, 4, 128], BF16)
    G2 = A([128, 4, 128], BF16)
    self.dma(RWP[:, :], dr["rwp"], [], [("rwp",)], "rwp")
    self.dma(W2A[:, :, :], dr["w2a"], [], [("w2a",)], "w2a", eng="pool", max_dma_last_dim=4096)
    self.dma(G2[:, :, :], dr["g2"], [], [("w2a",)], "w2a", eng="pool", max_dma_last_dim=4096)
    MU = lambda ci: RWP[:, ci:ci + 1]
    PC = lambda which, p: RWP[:, 14 + which * 4 + p: 15 + which * 4 + p]
    RL = [A([128, nseg, 1 + ntok], F32) for _ in range(3)]
    XR = [A([128, 512], F32) for _ in range(3)]
    DT = A([128, 512], F32)
    LWA = A([128, 512], BF16)
    LG = A([128, 512], BF16)
    hk = [("hg", k) for k in range(8)]
    v3 = lambda ap: ap.rearrange("p (s t) -> p s t", s=nseg)

    def load_hist(buf, bkey, ci, feat0):
        if prompt:
            if g == 0:
                self.memset(buf[:, 0, 0:1], 0.0, [bkey], eng="pool")
            else:
                self.copy(buf[:, 0, 0:1], HISTR[:, ci:ci + 1], [("histr",)], [bkey], eng="pool")
        else:
            self.dma(buf[:, :, 0:1], dr["shiftT"][feat0:feat0 + 128, :].rearrange("p (s o) -> p s o", o=1), [], [bkey], "uh",
                     allow_slow_non_contiguous=True)

    def save_hist(buf, bkey, ci, feat0):
        if prompt:
            self.copy(HISTR[:, ci:ci + 1], buf[:, 0, ntok:ntok + 1], [bkey], [("histr",)], eng="pool")
            if g == ntl - 2:
                self.dma(dr["o_shiftT"][feat0:feat0 + 128, :], buf[:, 0, ntok:ntok + 1], [bkey], [], "oc", allow_slow_non_contiguous=True)
        else:
            self.dma(dr["o_shiftTs"][feat0:feat0 + 128, :].rearrange("p (s o) -> p s o", o=1), buf[:, :, ntok:ntok + 1], [bkey], [], "oc",
                     allow_slow_non_contiguous=True)

    def shift_mix(buf, bkey, ci, out, okey):
        self.tt(v3(DT[:, 0:n]), buf[:, :, 0:ntok], buf[:, :, 1:1 + ntok], ALU.subtract, [bkey], [("dt",)])
        self.stt(v3(out), v3(DT[:, 0:n]), MU(ci), buf[:, :, 1:1 + ntok], ALU.mult, ALU.add, [("dt",), bkey, ("rwp",)], [okey])

    for j in range(2):
        load_hist(RL[j], ("rl", j), 12 + j, 1536 + j * 128)
    slot = self.load_slab(dr["w_rwl"])

    def consume(j, ti, ps, pskey):
        self.copy(RL[j][:, :, 1:1 + ntok], v3(ps), [pskey], [("rl", j)], eng="act")
    self.proj_fm(slot, 256, None, [(hk, lambda k: HG[:, k, 0:n], n)], consume)
    for j in range(2):
        save_hist(RL[j], ("rl", j), 12 + j, 1536 + j * 128)
        shift_mix(RL[j], ("rl", j), 12 + j, XR[j][:, 0:n], ("xr", j))
    self.act(LWA[0:64, 0:n], XR[0][0:64, 0:n], AF.Tanh, [("xr", 0)], [("lwa",)])
    self.copy(LWA[64:128, 0:n], XR[0][64:128, 0:n], [("xr", 0)], [("lwa",)], eng="pool")
    self.act(LG[:, 0:n], XR[1][:, 0:n], AF.Sigmoid, [("xr", 1)], [("lg",)])
    LW = A([128, 512], F32)
    AA = A([128, 512], F32)
    GT_ = A([128, 512], F32)
    KK = A([128, 512], F32)
    BB = A([128, 512], F32)
    CW = A([128, 512], F32)
    EW = [A([128, 512], F32) for _ in range(2)]
    KKW, RW, KI, BI, KEF, NBE = [A([128, 512], F32) for _ in range(6)]
    SQ = A([128, 512], BF16)
    VTr, KET, NBET, KKWT = [A([128, 4, 128], F32) for _ in range(4)]
    MT, NT_, TT, AKKN, ARKT, NARBT, TAT = [A([128, 4, 128], F32) for _ in range(7)]
    GTT = A([128, 512], F32)
    US = [A([128, 64], F32) for _ in range(2)]
    YT = A([128, 512], F32)
    PS_ = A([128, 64], F32)
    for p in range(4):
        feats = [p * 128, 512 + p * 128, 1024 + p * 128]
        for j in range(3):
            load_hist(RL[j], ("rl", j), p * 3 + j, feats[j])
        slot = self.load_slab(dr["w_rwp"][p])

        def consume(j, ti, ps, pskey):
            self.copy(RL[j][:, :, 1:1 + ntok], v3(ps), [pskey], [("rl", j)], eng="act")
        self.proj_fm(slot, 384, None, [(hk, lambda k: HG[:, k, 0:n], n)], consume)
        for j in range(3):
            save_hist(RL[j], ("rl", j), p * 3 + j, feats[j])
            shift_mix(RL[j], ("rl", j), p * 3 + j, XR[j][:, 0:n], ("xr", j))
        R_, K_, V_ = XR[0], XR[1], XR[2]
        pw_, psw = self.psum()
        self.mm(psw[:, 0:n], W2A[0:64, p, :], LWA[0:64, 0:n], True, True, [("w2a",), ("lwa",)], [("ps", pw_)])
        pa_, psa = self.psum()
        self.mm(psa[:, 0:n], W2A[64:128, p, :], LWA[64:128, 0:n], True, True, [("w2a",), ("lwa",)], [("ps", pa_)])
        pg_, psg = self.psum()
        self.mm(psg[:, 0:n], G2[:, p, :], LG[:, 0:n], True, True, [("w2a",), ("lg",)], [("ps", pg_)])
        self.act(LW[:, 0:n], psw[:, 0:n], AF.Sigmoid, [("ps", pw_), ("rwp",)], [("lw",)], bias=PC(0, p))
        self.ts(LW[:, 0:n], LW[:, 0:n], -float(np.exp(-0.5)), None, ALU.mult, None, [("lw",)], [("lw",)])
        self.act(AA[:, 0:n], psa[:, 0:n], AF.Sigmoid, [("ps", pa_), ("rwp",)], [("aa",)], bias=PC(1, p))
        self.copy(GT_[:, 0:n], psg[:, 0:n], [("ps", pg_)], [("gt",)], eng="act")
        self.ts(KK[:, 0:n], K_[:, 0:n], PC(2, p), None, ALU.mult, None, [("xr", 1), ("rwp",)], [("kk",)])
        self.act(SQ[:, 0:n], KK[:, 0:n], AF.Square, [("kk",)], [("sq",)])
        pi, ps = self.psum()
        self.mm(ps[:, 0:n], self.BLK64[:, :], SQ[:, 0:n], True, True, [("sq",)], [("ps", pi)])
        self.act(r1[:, 0:n], ps[:, 0:n], AF.Ln, [("ps", pi)], [r1k], bias=self.EPSC[:, 0:1])
        self.act(r2[:, 0:n], r1[:, 0:n], AF.Exp, [r1k], [r2k], scale=-0.5)
        self.tt(KK[:, 0:n], KK[:, 0:n], r2[:, 0:n], ALU.mult, [("kk",), r2k], [("kk",)])
        self.ts(DT[:, 0:n], AA[:, 0:n], -1.0, PC(3, p), ALU.add, ALU.mult, [("aa",), ("rwp",)], [("dt",)])
        self.stt(K_[:, 0:n], DT[:, 0:n], 1.0, K_[:, 0:n], ALU.add, ALU.mult, [("dt",), ("xr", 1)], [("xr", 1)])
        self.tt(BB[:, 0:n], KK[:, 0:n], AA[:, 0:n], ALU.mult, [("kk",), ("aa",)], [("bb",)])
        for c in range(nch):
            cs = slice(c * C, (c + 1) * C)
            self.S.op("dve", (lambda cs=cs: nc.vector.tensor_tensor_scan(
                CW[:, cs], self.ONEC[:, 0:1].to_broadcast([128, C]), LW[:, cs], 0.0, ALU.mult, ALU.add)),
                [("lw",)], [("cw",)])
        self.act(EW[0][:, 0:n], CW[:, 0:n], AF.Exp, [("cw",)], [("ew", 0)])
        self.tt(RW[:, 0:n], R_[:, 0:n], EW[0][:, 0:n], ALU.mult, [("xr", 0), ("ew", 0)], [("rw",)])
        self.tt(DT[:, 0:n], CW[:, 0:n], LW[:, 0:n], ALU.subtract, [("cw",), ("lw",)], [("dt",)])
        self.act(EW[1][:, 0:n], DT[:, 0:n], AF.Exp, [("dt",)], [("ew", 1)])
        self.tt(KKW[:, 0:n], KK[:, 0:n], EW[1][:, 0:n], ALU.mult, [("kk",), ("ew", 1)], [("kkw",)])
        self.act(EW[1][:, 0:n], CW[:, 0:n], AF.Exp, [("cw",), ("kkw",)], [("ew", 1)], scale=-1.0)
        self.tt(KI[:, 0:n], K_[:, 0:n], EW[1][:, 0:n], ALU.mult, [("xr", 1), ("ew", 1)], [("ki",)])
        self.tt(BI[:, 0:n], BB[:, 0:n], EW[1][:, 0:n], ALU.mult, [("bb",), ("ew", 1)], [("bi",)])
        for c in range(nch):
            cs = slice(c * C, (c + 1) * C)
            self.act(DT[:, cs], CW[:, cs], AF.Exp, [("cw",), ("dt",)], [("dt",)], bias=CW[:, (c + 1) * C - 1:(c + 1) * C], scale=-1.0)
        self.tt(KEF[:, 0:n], K_[:, 0:n], DT[:, 0:n], ALU.mult, [("xr", 1), ("dt",)], [("kef",)])
        self.stt(NBE[:, 0:n], BB[:, 0:n], -1.0, DT[:, 0:n], ALU.mult, ALU.mult, [("bb",), ("dt",)], [("nbe",)])
        for src, skey, dst, dkey in ((V_, ("xr", 2), VTr, ("vtr",)), (KEF, ("kef",), KET, ("ket",)),
                                     (NBE, ("nbe",), NBET, ("nbet",)), (KKW, ("kkw",), KKWT, ("kkwt",))):
            pi, ps = self.psum()
            for c in range(nch):
                self.transpose_f32(ps[0:C, c * 128:(c + 1) * 128], src[:, c * C:(c + 1) * C], 128, [skey], ("ps", pi))
            self.copy(dst[0:C, 0:nch, :], ps[0:C, 0:nch * 128].rearrange("p (a b) -> p a b", b=128), [("ps", pi)], [dkey], eng="act")
        pv3 = lambda p_: p_[0:C, 0:nch * C].rearrange("p (a b) -> p a b", b=C)
        for hh in range(2):
            bp = hh * 64
            hs = slice(bp, bp + 64)
            specs = [(KI, ("ki",), RW, ("rw",), ARKT, ("arkt",), self.XIU), (BI, ("bi",), KKW, ("kkw",), MT, ("mt",), self.XSU),
                     (BI, ("bi",), RW, ("rw",), NARBT, ("narbt",), self.XIUN), (KKW, ("kkw",), BI, ("bi",), NT_, ("nt",), self.XSLN),
                     (KKW, ("kkw",), KI, ("ki",), AKKN, ("akkn",), self.XSLN)]
            for (l, lk, r, rk, dst, dk, msk) in specs:
                pi, ps = self.psum()
                for c in range(nch):
                    cs = slice(c * C, (c + 1) * C)
                    self.mm(ps[0:C, c * C:(c + 1) * C], l[hs, cs], r[hs, cs], True, True, [lk, rk], [("ps", pi)])
                for c in range(nch):
                    self.tt(dst[0:C, c, 0:C], ps[0:C, c * C:(c + 1) * C], msk[0:C, 0:C], ALU.mult, [("ps", pi)], [dk])
            self.neumann_TT(MT, NT_, TT, nch, C, ("mt",), ("nt",), ("tt",), "r")
            pi, ps = self.psum()
            for c in range(nch):
                self.mm(ps[0:C, c * C:(c + 1) * C], AKKN[0:C, c, 0:C], TT[0:C, c, 0:C], True, True, [("akkn",), ("tt",)], [("ps", pi)])
            self.act(TAT[0:C, 0:nch, 0:C], pv3(ps), AF.Identity, [("ps", pi)], [("tat",)], scale=-1.0)
            pi, ps = self.psum()
            for c in range(nch):
                self.mm(ps[hs, c * C:(c + 1) * C], KKWT[0:C, c, hs], TT[0:C, c, 0:C], True, True, [("kkwt",), ("tt",)], [("ps", pi)])
            self.copy(GTT[hs, 0:n], ps[hs, 0:n], [("ps", pi)], [("gtt", hh)], eng="act")
            for c in range(nch):
                cs = slice(c * C, (c + 1) * C)
                if prompt:
                    P_, pkey = PR[hs, p, :], ("pr", p, hh)
                    if g == 0 and c == 0:
                        self.memset(PR[hs, p, :], 0.0, [pkey], eng="pool")
                else:
                    P_, pkey = PS_[hs, :], ("pss", hh)
                    self.dma(PS_[hs, :], dr["rwkv_s0T"][c, p, hs, :], [], [pkey], f"pss{hh}")
                u = US[hh]
                pu, psu = self.psum()
                self.mm(psu[0:C, 0:64], TAT[0:C, c, 0:C], VTr[0:C, c, hs], True, False, [("tat",), ("vtr",)], [("ps", pu)])
                self.mm(psu[0:C, 0:64], GTT[hs, cs], P_, False, True, [("gtt", hh), pkey], [("ps", pu)])
                self.copy(u[0:C, :], psu[0:C, 0:64], [("ps", pu)], [("us", hh)], eng="act")
                py, psy = self.psum()
                self.mm(psy[hs, 0:C], P_, RW[hs, cs], True, False, [pkey, ("rw",)], [("ps", py)])
                self.mm(psy[hs, 0:C], VTr[0:C, c, hs], ARKT[0:C, c, 0:C], False, False, [("vtr",), ("arkt",)], [("ps", py)])
                self.mm(psy[hs, 0:C], u[0:C, :], NARBT[0:C, c, 0:C], False, True, [("us", hh), ("narbt",)], [("ps", py)])
                pc_, psc = self.psum()
                self.mm(psc[hs, 0:64], KET[0:C, c, hs], VTr[0:C, c, hs], True, False, [("ket",), ("vtr",)], [("ps", pc_)])
                self.mm(psc[hs, 0:64], NBET[0:C, c, hs], u[0:C, :], False, True, [("nbet",), ("us", hh)], [("ps", pc_)])
                self.stt(P_, P_, EW[0][hs, (c + 1) * C - 1:(c + 1) * C], psc[hs, 0:64], ALU.mult, ALU.add,
                         [pkey, ("ew", 0), ("ps", pc_)], [pkey])
                self.copy(YT[hs, cs], psy[hs, 0:C], [("ps", py)], [("yt", hh)], eng="act")
                if not prompt:
                    self.dma(dr["o_rwkvT_s"][c, p, hs, :], PS_[hs, :], [pkey], [], f"pso{hh}")
            if prompt and g == ntl - 2:
                self.dma(dr["o_rwkvT_p"][p, hs, :], PR[hs, p, :], [("pr", p, hh)], [], "sso")
        ytk = [("yt", 0), ("yt", 1)]
        self.copy(SQ[:, 0:n], YT[:, 0:n], ytk, [("sq",)], eng="pool")
        pi, ps = self.psum()
        self.mm(ps[:, 0:n], self.BLK64[:, :], SQ[:, 0:n], True, True, [("sq",)], [("ps", pi)])
        self.stt(YT[:, 0:n], ps[:, 0:n], -1.0 / 64.0, YT[:, 0:n], ALU.mult, ALU.add, [("ps", pi)] + ytk, ytk)
        self.act(SQ[:, 0:n], YT[:, 0:n], AF.Square, ytk, [("sq",)])
        pi, ps = self.psum()
        self.mm(ps[:, 0:n], self.BLK64[:, :], SQ[:, 0:n], True, True, [("sq",)], [("ps", pi)])
        self.act(r1[:, 0:n], ps[:, 0:n], AF.Ln, [("ps", pi)], [r1k], bias=self.GNEPS[:, 0:1], scale=1.0 / 64.0)
        self.act(r2[:, 0:n], r1[:, 0:n], AF.Exp, [r1k], [r2k], scale=-0.5)
        self.stt(YT[:, 0:n], YT[:, 0:n], PC(5, p), r2[:, 0:n], ALU.mult, ALU.mult, ytk + [r2k, ("rwp",)], ytk)
        self.stt(DT[:, 0:n], R_[:, 0:n], PC(4, p), K_[:, 0:n], ALU.mult, ALU.mult, [("xr", 0), ("xr", 1), ("rwp",)], [("dt",)])
        self.copy(SQ[:, 0:n], DT[:, 0:n], [("dt",)], [("sq",)], eng="pool")
        pi, ps = self.psum()
        self.mm(ps[:, 0:n], self.BLK64[:, :], SQ[:, 0:n], True, True, [("sq",)], [("ps", pi)])
        self.tt(DT[:, 0:n], ps[:, 0:n], V_[:, 0:n], ALU.mult, [("ps", pi), ("xr", 2)], [("dt",)])
        self.stt(YT[:, 0:n], YT[:, 0:n], PC(6, p), DT[:, 0:n], ALU.add, ALU.add, ytk + [("dt",), ("rwp",)], ytk)
        self.tt(OA[:, 4 + p, 0:n], YT[:, 0:n], GT_[:, 0:n], ALU.mult, ytk + [("gt",)], [("oa", 4 + p)])


_NC_CACHE = {}


def kernel(**inp):
    inp = {k: np.asarray(v) for k, v in inp.items()}
    ncores = 8
    if "nc" not in _NC_CACHE:
        _NC_CACHE["nc"] = build({})
    nc = _NC_CACHE["nc"]
    set_np(2048)
    shared = prep_shared(inp)
    pk, pv = prep_pools(inp)
    shared["poolk"], shared["poolv"] = pk, pv
    in_maps = []
    for c in range(ncores):
        m = prep_inputs(inp, c)
        m.update(shared)
        in_maps.append(m)
    res = run_bass_kernel_spmd(nc, in_maps, core_ids=list(range(ncores)))
    R = res.results
    f = np.float32

    def st(fn):
        return np.stack([fn(R[c]) for c in range(ncores)]).astype(f)

    def cat(fn):
        return np.concatenate([fn(R[c]) for c in range(ncores)], axis=0).astype(f)
    y_p = st(lambda r: r["o_yT"][:, :NP].T)
    y_s = cat(lambda r: r["o_yT"][:, NP:].T.reshape(4, 8, D))
    gdn_p = st(lambda r: r["o_gdn_p"])[None]
    gdn_s = cat(lambda r: r["o_gdn_s"])[None]
    conv_p = st(lambda r: r["o_convT"].T)[None]
    conv_s = cat(lambda r: r["o_convTs"].transpose(0, 2, 1))[None]
    rw_p = st(lambda r: r["o_rwkvT_p"].reshape(4, 2, 64, 64).transpose(0, 1, 3, 2).reshape(8, 64, 64))[None]
    rw_s = cat(lambda r: r["o_rwkvT_s"].reshape(4, 4, 2, 64, 64).transpose(0, 1, 2, 4, 3).reshape(4, 8, 64, 64))[None]
    sh_p = st(lambda r: r["o_shiftT"][:, 0])[None]
    sh_s = cat(lambda r: r["o_shiftTs"].T)[None]
    sbk_p = st(lambda r: r["o_sbkT"][:, :NP].T.reshape(NP, 16, 64))[None]
    sbk_s = cat(lambda r: r["o_sbkT"][:, NP:].T.reshape(4, 8, 16, 64))[None]
    sbv_p = st(lambda r: r["o_sbv"][:NP].reshape(NP, 16, 64))[None]
    sbv_s = cat(lambda r: r["o_sbv"][NP:].reshape(4, 8, 16, 64))[None]
    mk_p = np.stack([R[c]["o_memkT"].transpose(0, 2, 1).reshape(2, 256, 4, 256) for c in range(ncores)], axis=1).astype(f)
    mv_p = np.stack([R[c]["o_memv"].reshape(2, 256, 4, 256) for c in range(ncores)], axis=1).astype(f)
    return (y_p, y_s, gdn_p, gdn_s, conv_p, conv_s, rw_p, rw_s, sh_p, sh_s, sbk_p, sbk_s, sbv_p, sbv_s, mk_p, mv_p)
```

```python
import contextlib
import numpy as np
import concourse.bass as bass
import concourse.mybir as mybir
from concourse.bass_utils import run_bass_kernel_spmd

F32 = mybir.dt.float32
BF16 = mybir.dt.bfloat16
I32 = mybir.dt.int32
AF = mybir.ActivationFunctionType
ALU = mybir.AluOpType
AX = mybir.AxisListType

D = 1024
KC = 8
NP = 2048
NS = 32
NT = NP + NS
DFF = 2816
EPS = 1e-6

ENGS = ("pe", "act", "dve", "pool", "sp")
SEM_WRAP = 30000


class DSem:
    def __init__(self, name):
        self.name = name
        self.count = 0
        self.handle = None
        self.last = None


class Op:
    __slots__ = ("eng", "fn", "deps", "sig", "dsem", "dcount")

    def __init__(self, eng, fn):
        self.eng = eng
        self.fn = fn
        self.deps = []
        self.sig = None
        self.dsem = None
        self.dcount = None


class Sched:
    def __init__(self, nc):
        self.nc = nc
        self.streams = {e: [] for e in ENGS}
        self.last_w = {}
        self.readers = {}
        self.dsems = {}

    def dsem(self, name):
        if name not in self.dsems:
            self.dsems[name] = DSem(name)
        return self.dsems[name]

    def _dep(self, o, p):
        if p is None or p is o:
            return
        if p.dsem is not None:
            o.deps.append((p, p.dsem.count))
        elif not (p.eng == "pe" and o.eng == "pe"):
            o.deps.append((p, None))

    def op(self, eng, fn, reads=(), writes=(), dsem=None):
        o = Op(eng, fn)
        for k in reads:
            self._dep(o, self.last_w.get(k))
        for k in writes:
            self._dep(o, self.last_w.get(k))
            for r in self.readers.get(k, ()):
                self._dep(o, r)
        if dsem is not None:
            if isinstance(dsem, str):
                dsem = self.dsem(dsem)
            o.dsem = dsem
            dsem.count += 1
            o.dcount = dsem.count
            dsem.last = o
        for k in reads:
            self.readers.setdefault(k, []).append(o)
        for k in writes:
            self.last_w[k] = o
            self.readers[k] = []
        self.streams[eng].append(o)
        return o

    def barrier(self):
        lasts = [s[-1] for s in self.streams.values() if s]
        dl = [d.last for d in self.dsems.values() if d.last is not None]
        nc = self.nc
        nops = {"pe": lambda: nc.tensor.nop(), "act": lambda: nc.scalar.nop(), "dve": lambda: nc.vector.nop(),
                "pool": lambda: nc.gpsimd.nop(), "sp": lambda: nc.sync.nop()}
        for e in ENGS:
            o = Op(e, nops[e])
            for p in lasts:
                if p.dsem is None and not (p.eng == "pe" and e == "pe"):
                    o.deps.append((p, None))
            for p in dl:
                o.deps.append((p, p.dsem.count))
            self.streams[e].append(o)
        self.last_w = {}
        self.readers = {}

    def emit(self, stack):
        nc = self.nc
        for e in ENGS:
            for o in self.streams[e]:
                for (p, dc) in o.deps:
                    if p.dsem is None:
                        p.sig = 0
        nsig = {}
        for e in ENGS:
            c = 0
            for o in self.streams[e]:
                if o.dsem is None and o.sig is not None:
                    c += 1
                    o.sig = c
            nsig[e] = c
        esems = {}
        for e in ENGS:
            n = max(1, (nsig[e] + SEM_WRAP - 1) // SEM_WRAP)
            esems[e] = [stack.enter_context(nc.semaphore(f"s_{e}{i}")) for i in range(n)]
        for ds in self.dsems.values():
            ds.handle = stack.enter_context(nc.semaphore(f"d_{ds.name}"))
        engobj = {"pe": nc.tensor, "act": nc.scalar, "dve": nc.vector, "pool": nc.gpsimd, "sp": nc.sync}

        def emit_stream(e):
            eo = engobj[e]
            waited_e = {}
            waited_d = {}
            for o in self.streams[e]:
                need_e = {}
                need_d = {}
                for (p, dc) in o.deps:
                    if p.dsem is not None:
                        if need_d.get(p.dsem.name, 0) < dc:
                            need_d[p.dsem.name] = dc
                    else:
                        if need_e.get(p.eng, 0) < p.sig:
                            need_e[p.eng] = p.sig
                for pe_, sig in need_e.items():
                    if waited_e.get(pe_, 0) >= sig:
                        continue
                    si = (sig - 1) // SEM_WRAP
                    eo.wait_ge(esems[pe_][si], (sig - 1) % SEM_WRAP + 1)
                    waited_e[pe_] = sig
                for dn, dc in need_d.items():
                    if waited_d.get(dn, 0) >= dc:
                        continue
                    eo.wait_ge(self.dsems[dn].handle, dc * 16)
                    waited_d[dn] = dc
                ins = o.fn()
                if o.dsem is not None:
                    ins.then_inc(o.dsem.handle, 16)
                elif o.sig is not None:
                    si = (o.sig - 1) // SEM_WRAP
                    ins.then_inc(esems[e][si], 1)
            if e == "sp":
                for ds in self.dsems.values():
                    if ds.count and waited_d.get(ds.name, 0) < ds.count:
                        eo.wait_ge(ds.handle, ds.count * 16)

        block = stack.enter_context(nc.Block())

        @block.sync
        def _(eng):
            emit_stream("sp")

        @block.scalar
        def _(eng):
            emit_stream("act")

        @block.vector
        def _(eng):
            emit_stream("dve")

        @block.gpsimd
        def _(eng):
            emit_stream("pool")

        @block.tensor
        def _(eng):
            emit_stream("pe")


def w_slabs(w, col_ranges):
    K = w.shape[0]
    kc = K // 128
    out = np.zeros((len(col_ranges), 128, kc, 512), np.float32)
    wr = w.reshape(kc, 128, w.shape[1]).transpose(1, 0, 2)
    for i, (c0, c1) in enumerate(col_ranges):
        out[i, :, :, :c1 - c0] = wr[:, :, c0:c1]
    return out


def col_param(v, nch):
    return np.ascontiguousarray(np.asarray(v, np.float32).reshape(nch, 128).T)


TILES = [(0, 512), (512, 512), (1024, 512), (1536, 512), (2048, 32)]


class KB:
    def __init__(self, nc, cfg):
        self.nc = nc
        self.cfg = cfg
        self.S = Sched(nc)
        self.stack = contextlib.ExitStack()
        self.dr = {}
        self.ps_i = 0
        self.ps_n = 8
        self.w_i = 0
        self.uid = 0

    def din(self, name, shape, dtype=F32):
        self.dr[name] = self.nc.dram_tensor(name, list(shape), dtype, kind="ExternalInput").ap()
        return self.dr[name]

    def dout(self, name, shape, dtype=F32):
        self.dr[name] = self.nc.dram_tensor(name, list(shape), dtype, kind="ExternalOutput").ap()
        return self.dr[name]

    def sb(self, name, shape, dtype, stack=None):
        return (stack or self.stack).enter_context(self.nc.sbuf_tensor(name, list(shape), dtype))

    def psum(self):
        i = self.ps_i % self.ps_n
        self.ps_i = (i + 1) % self.ps_n
        return i, self.PS[i]

    def mm(self, out, lhsT, rhs, start, stop, reads, writes, skip=False):
        nc = self.nc
        if skip:
            return self.S.op("pe", lambda: nc.tensor.matmul(out, lhsT, rhs, start=start, stop=stop, skip_group_check=True), reads, writes)
        return self.S.op("pe", lambda: nc.tensor.matmul(out, lhsT, rhs, start=start, stop=stop), reads, writes)

    def act(self, out, in_, func, reads, writes, bias=None, scale=None, accum_out=None):
        nc = self.nc
        kw = {}
        if bias is not None:
            kw["bias"] = bias
        if scale is not None:
            kw["scale"] = scale
        if accum_out is not None:
            kw["accum_out"] = accum_out
        return self.S.op("act", lambda: nc.scalar.activation(out, in_, func, **kw), reads, writes)

    def tt(self, out, in0, in1, op, reads, writes, eng="dve"):
        nc = self.nc
        e = nc.vector if eng == "dve" else nc.gpsimd
        return self.S.op(eng, lambda: e.tensor_tensor(out, in0, in1, op), reads, writes)

    def ts(self, out, in0, s1, s2, op0, op1, reads, writes, eng="dve"):
        nc = self.nc
        e = nc.vector if eng == "dve" else nc.gpsimd
        if op1 is None:
            return self.S.op(eng, lambda: e.tensor_scalar(out, in0, s1, None, op0), reads, writes)
        return self.S.op(eng, lambda: e.tensor_scalar(out, in0, s1, s2, op0, op1), reads, writes)

    def stt(self, out, in0, scalar, in1, op0, op1, reads, writes):
        nc = self.nc
        return self.S.op("dve", lambda: nc.vector.scalar_tensor_tensor(out, in0, scalar, in1, op0, op1), reads, writes)

    def copy(self, out, in_, reads, writes, eng="dve"):
        nc = self.nc
        if eng == "act":
            return self.S.op("act", lambda: nc.scalar.activation(out, in_, AF.Identity), reads, writes)
        e = nc.vector if eng == "dve" else nc.gpsimd
        return self.S.op(eng, lambda: e.tensor_copy(out, in_), reads, writes)

    def memset(self, ap, val, writes, eng="pool"):
        nc = self.nc
        e = nc.vector if eng == "dve" else nc.gpsimd
        return self.S.op(eng, lambda: e.memset(ap, val), (), writes)

    def dma(self, out, in_, reads, writes, dsem, eng="sp", **kw):
        nc = self.nc
        e = nc.sync if eng == "sp" else nc.gpsimd
        return self.S.op(eng, lambda: e.dma_start(out=out, in_=in_, **kw), reads, writes, dsem=dsem)

    def load_slab(self, dram_slab, kc=8):
        slot = self.w_i
        self.w_i = (self.w_i + 1) % len(self.WS)
        w = self.WS[slot]
        self.dma(w[:, 0:kc, :], dram_slab, (), [("w", slot)], f"w{slot}", eng="pool", max_dma_last_dim=4096)
        return slot

    def proj_fm(self, slot, ncols, H, tiles, consume, kc=8):
        w = self.WS[slot]
        for ti, (hkeys, hfn, n) in enumerate(tiles):
            for j in range(ncols // 128):
                pi, ps = self.psum()
                for k in range(kc):
                    self.mm(ps[:, 0:n], w[:, k, j * 128:(j + 1) * 128], hfn(k), k == 0, k == kc - 1,
                            [("w", slot)] + list(hkeys), [("ps", pi)])
                consume(j, ti, ps[:, 0:n], ("ps", pi))

    def arena_reset(self):
        self.ar_off = 0

    def arena(self, shape, dtype):
        esz = 2 if dtype == BF16 else 4
        n = 1
        for d in shape[1:]:
            n *= d
        nbytes = (n * esz + 31) // 32 * 32
        off = self.ar_off
        assert off + nbytes <= self.ARENA_BYTES, f"arena overflow {off + nbytes} > {self.ARENA_BYTES}"
        self.ar_off += nbytes
        v = self.ARENA[:, off // 4: (off + nbytes) // 4]
        if dtype == BF16:
            v = v.bitcast(BF16)
        elif dtype == I32:
            v = v.bitcast(I32)
        v = v[:, 0:n]
        if len(shape) == 3:
            v = v.rearrange("p (a b) -> p a b", a=shape[1])
        elif len(shape) == 4:
            v = v.rearrange("p (a b c) -> p a b c", a=shape[1], b=shape[2])
        return v

    def key(self, base):
        self.uid += 1
        return (base, self.uid)

    def rmsnorm(self, src_fn, src_keys, gcol, out_fn, out_keys, n, tmp):
        pi, ps = self.psum()
        sq, r1, r2 = tmp["sq"], tmp["r1"], tmp["r2"]
        sqk = tmp.get("sqkeys") or [("sq", tmp["id"], k) for k in range(KC)]
        for k in range(KC):
            self.act(sq[:, k, 0:n], src_fn(k), AF.Square, [src_keys[k]], [sqk[k]])
        for k in range(KC):
            self.mm(ps[:, 0:n], self.ONES[:, :], sq[:, k, 0:n], k == 0, k == KC - 1,
                    [sqk[k], ("ones",)], [("ps", pi)])
        self.act(r1[:, 0:n], ps[:, 0:n], AF.Ln, [("ps", pi)], [("r1", tmp["id"])], bias=self.EPSC[:, 0:1], scale=1.0 / D)
        self.act(r2[:, 0:n], r1[:, 0:n], AF.Exp, [("r1", tmp["id"])], [("r2", tmp["id"])], scale=-0.5)
        for k in range(KC):
            self.stt(out_fn(k), src_fn(k), gcol[:, k:k + 1], r2[:, 0:n], ALU.mult, ALU.mult,
                     [src_keys[k], ("r2", tmp["id"]), ("par",)], [out_keys[k]])

    def norm_tmp(self, sq=None, sqkeys=None):
        self.uid += 1
        return {"sq": sq if sq is not None else self.arena([128, 8, 512], BF16), "sqkeys": sqkeys,
                "r1": self.arena([128, 512], F32), "r2": self.arena([128, 512], F32), "id": self.uid}

    def mem_phase(self, L):
        dr = self.dr
        self.arena_reset()
        tmp = self.norm_tmp()
        HG = self.arena([128, 8, 512], BF16)
        QT = self.arena([128, 8, 512], BF16)
        OT = self.arena([128, 8, 512], BF16)
        PT = self.arena([128, 2, 512], BF16)
        RD1 = self.arena([128, 512], F32)
        RD2 = self.arena([128, 512], F32)
        MKT = self.arena([128, 8, 256], BF16)
        MV = self.arena([128, 2, 1024], BF16)
        SMK = [self.arena([128, 8, 256], BF16) for _ in range(2)]
        SMV = [self.arena([128, 2, 1024], BF16) for _ in range(2)]
        MEMX = self.arena([128, 8, 256], F32)
        MN = self.arena([128, 8, 256], BF16)
        STG = [self.arena([128, 512], F32) for _ in range(2)]
        stg_i = [0]
        ws_base = self.WS
        self.WS = list(ws_base) + [self.arena([128, 8, 512], BF16) for _ in range(2)]
        self.w_i = 0

        def stage_out(ps_ap, pskey, n, dst, bf_dst, bf_key):
            i = stg_i[0]
            stg_i[0] ^= 1
            self.copy(STG[i][:, 0:n], ps_ap, [pskey], [("stg", i)], eng="dve")
            self.copy(bf_dst, STG[i][:, 0:n], [("stg", i)], [bf_key], eng="pool")
            self.dma(dst, STG[i][:, 0:n], [("stg", i)], [], f"stg{i}")

        self.dma(MEMX[:, :, :], dr["memT"].rearrange("(kc p) m -> p kc m", p=128), [], [("memx",)], "memx")
        self.rmsnorm(lambda k: MEMX[:, k, :], [("memx",)] * 8, self.G["memtok"][L],
                     lambda k: MN[:, k, :], [("mn", k) for k in range(8)], 256, tmp)
        mnkeys = [("mn", k) for k in range(8)]
        sub = self.cfg.get("sub", 3)
        for s in range(2):
            if sub == 10:
                continue
            slot = self.load_slab(dr["w_mk"][L, s])

            def consume(j, ti, ps, pskey, s=s):
                oc = s * 4 + j
                stage_out(ps, pskey, 256, dr["o_memkT"][L, oc * 128:(oc + 1) * 128, :], MKT[:, oc, :], ("mkt", oc))
            self.proj_fm(slot, 512, None, [(mnkeys, lambda k: MN[:, k, :], 256)], consume)
        for s in range(2):
            if sub in (10, 11):
                continue
            slot = self.load_slab(dr["w_mv"][L, s])
            w = self.WS[slot]
            for mb in range(2):
                pi, ps = self.psum()
                for k in range(8):
                    self.mm(ps[:, :], MN[:, k, mb * 128:(mb + 1) * 128], w[:, k, :], k == 0, k == 7,
                            [("w", slot), ("mn", k)], [("ps", pi)])
                stage_out(ps[:, :], ("ps", pi), 512, dr["o_memv"][L, mb * 128:(mb + 1) * 128, s * 512:(s + 1) * 512],
                          MV[:, mb, s * 512:(s + 1) * 512], ("mv", mb, s))

        def attend(kt, kv, kvkeys, c0, n):
            for hd in range(4):
                for mb in range(2):
                    pi, ps = self.psum()
                    for dc in range(2):
                        self.mm(ps[:, 0:n], kt[:, hd * 2 + dc, mb * 128:(mb + 1) * 128], QT[:, hd * 2 + dc, c0:c0 + n],
                                dc == 0, dc == 1, kvkeys + [("qt", hd * 2 + dc)], [("ps", pi)])
                    self.act(PT[:, mb, 0:n], ps[:, 0:n], AF.Exp, [("ps", pi)], [("pt", mb)], scale=1.0 / 16.0)
                pi, ps = self.psum()
                for mb in range(2):
                    self.mm(ps[:, 0:n], self.ONES[:, :], PT[:, mb, 0:n], mb == 0, mb == 1, [("pt", mb)], [("ps", pi)])
                self.act(RD1[:, 0:n], ps[:, 0:n], AF.Ln, [("ps", pi)], [("rd1",)])
                self.act(RD2[:, 0:n], RD1[:, 0:n], AF.Exp, [("rd1",)], [("rd2",)], scale=-1.0)
                for dc in range(2):
                    pi, ps = self.psum()
                    for mb in range(2):
                        self.mm(ps[:, 0:n], kv[:, mb, hd * 256 + dc * 128: hd * 256 + (dc + 1) * 128], PT[:, mb, 0:n],
                                mb == 0, mb == 1, kvkeys + [("pt", mb)], [("ps", pi)])
                    self.tt(OT[:, hd * 2 + dc, c0:c0 + n], ps[:, 0:n], RD2[:, 0:n], ALU.mult,
                            [("ps", pi), ("rd2",)], [("ot", hd * 2 + dc)])

        pkv = [("mkt", oc) for oc in range(8)] + [("mv", mb, s) for mb in range(2) for s in range(2)]
        for g, (t0, n) in enumerate(TILES):
            if sub in (1, 10, 11) or (sub == 2 and g == len(TILES) - 1):
                continue
            xk = [("X", k, g) for k in range(8)]
            hk = [("hg", k) for k in range(8)]
            self.rmsnorm(lambda k: self.X[:, k, t0:t0 + n], xk, self.G["mem"][L],
                         lambda k: HG[:, k, 0:n], hk, n, tmp)
            for s in range(2):
                slot = self.load_slab(dr["w_mq"][L, s])

                def consume(j, ti, ps, pskey, s=s):
                    oc = s * 4 + j
                    self.copy(QT[:, oc, 0:n], ps, [pskey], [("qt", oc)], eng="act")
                self.proj_fm(slot, 512, None, [(hk, lambda k: HG[:, k, 0:n], n)], consume)
            if g < len(TILES) - 1:
                attend(MKT, MV, pkv, 0, n)
            else:
                for sq in range(4):
                    b = sq % 2
                    self.dma(SMK[b][:, :, :], dr["cmkT"][L, sq].rearrange("(kc p) m -> p kc m", p=128),
                             [], [("smk", b)], f"smk{b}", eng="pool", max_dma_last_dim=1024)
                    self.dma(SMV[b][:, :, :], dr["cmv"][L, sq].rearrange("(mb p) f -> p mb f", p=128),
                             [], [("smv", b)], f"smv{b}", eng="pool", max_dma_last_dim=4096)
                    attend(SMK[b], SMV[b], [("smk", b), ("smv", b)], sq * 8, 8)
            ok = [("ot", k) for k in range(8)]
            for s in range(2):
                slot = self.load_slab(dr["w_mo"][L, s])

                def consume(j, ti, ps, pskey, s=s):
                    oc = s * 4 + j
                    self.tt(self.X[:, oc, t0:t0 + n], self.X[:, oc, t0:t0 + n], ps, ALU.add,
                            [pskey, ("X", oc, g)], [("X", oc, g)])
                self.proj_fm(slot, 512, None, [(ok, lambda k: OT[:, k, 0:n], n)], consume)
        self.S.barrier()
        self.WS = ws_base
        self.w_i = 0

    def ffn_phase(self, L):
        dr = self.dr
        self.arena_reset()
        tmp = self.norm_tmp()
        HA = self.arena([128, 8, NT], BF16)
        ACTB = [self.arena([128, 4, 512], BF16) for _ in range(2)]
        SG = [self.arena([128, 512], F32) for _ in range(2)]
        ws_base = self.WS
        self.WS = list(ws_base) + [self.arena([128, 8, 512], BF16) for _ in range(3)]
        self.w_i = 0
        for g, (t0, n) in enumerate(TILES):
            self.rmsnorm(lambda k: self.X[:, k, t0:t0 + n], [("X", k, g) for k in range(8)], self.G["ffn"][L],
                         lambda k: HA[:, k, t0:t0 + n], [("ha", k, g) for k in range(8)], n, tmp)
        cnt = 0
        for hg in range(6):
            nch = 4 if hg < 5 else 2
            sg_ = self.load_slab(dr["w_ffi"][L, hg])
            su_ = self.load_slab(dr["w_ffi"][L, 6 + hg])
            so_ = self.load_slab(dr["w_ffo"][L, hg].rearrange("p a (b c) -> p (a b) c", c=512))
            wg, wu = self.WS[sg_], self.WS[su_]
            wo = self.WS[so_].rearrange("p (a b) c -> p a (b c)", b=2)
            for g, (t0, n) in enumerate(TILES):
                ab = ACTB[cnt % 2]
                abk = cnt % 2
                cnt += 1
                hk = [("ha", k, g) for k in range(8)]
                for j in range(nch):
                    pg, psg = self.psum()
                    for k in range(8):
                        self.mm(psg[:, 0:n], wg[:, k, j * 128:(j + 1) * 128], HA[:, k, t0:t0 + n], k == 0, k == 7,
                                [("w", sg_), hk[k]], [("ps", pg)])
                    pu, psu = self.psum()
                    for k in range(8):
                        self.mm(psu[:, 0:n], wu[:, k, j * 128:(j + 1) * 128], HA[:, k, t0:t0 + n], k == 0, k == 7,
                                [("w", su_), hk[k]], [("ps", pu)])
                    si = j % 2
                    self.act(SG[si][:, 0:n], psg[:, 0:n], AF.Silu, [("ps", pg)], [("sg", si)])
                    self.tt(ab[:, j, 0:n], SG[si][:, 0:n], psu[:, 0:n], ALU.mult, [("sg", si), ("ps", pu)], [("ab", abk, j)])
                for oc in range(8):
                    po, pso = self.psum()
                    for j in range(nch):
                        self.mm(pso[:, 0:n], wo[:, j, oc * 128:(oc + 1) * 128], ab[:, j, 0:n], j == 0, j == nch - 1,
                                [("w", so_), ("ab", abk, j)], [("ps", po)])
                    self.tt(self.X[:, oc, t0:t0 + n], self.X[:, oc, t0:t0 + n], pso[:, 0:n], ALU.add,
                            [("ps", po), ("X", oc, g)], [("X", oc, g)])
        self.S.barrier()
        self.WS = ws_base
        self.w_i = 0

    def final_phase(self):
        dr = self.dr
        self.arena_reset()
        tmp = self.norm_tmp()
        YS = [self.arena([128, 8, 512], F32) for _ in range(2)]
        for g, (t0, n) in enumerate(TILES):
            b = g % 2
            self.rmsnorm(lambda k: self.X[:, k, t0:t0 + n], [("X", k, g) for k in range(8)], self.G["final"],
                         lambda k: YS[b][:, k, 0:n], [("ys", b, k) for k in range(8)], n, tmp)
            self.dma(dr["o_yT"][:, t0:t0 + n].rearrange("(kc p) t -> p kc t", p=128), YS[b][:, :, 0:n],
                     [("ys", b, k) for k in range(8)], [], f"ys{b}")
        self.S.barrier()


ARENA_BYTES = 110592


def set_np(n):
    global NP, NT, TILES
    NP = n
    NT = NP + NS
    TILES = [(i * 512, 512) for i in range(NP // 512)] + [(NP, NS)]


def build(cfg):
    set_np(cfg.get("np", 2048))
    nc = bass.Bass("TRN2", target_bir_lowering=False)
    kb = KB(nc, cfg)
    S = kb.S
    din, dout = kb.din, kb.dout
    din("xT", [D, NT])
    din("memT", [D, 256])
    din("cmkT", [2, 4, D, 256])
    din("cmv", [2, 4, 256, D])
    din("gcols", [8, 128, 8])
    din("gfinal", [128, 8])
    din("w_mq", [2, 2, 128, 8, 512])
    din("w_mk", [2, 2, 128, 8, 512])
    din("w_mv", [2, 2, 128, 8, 512])
    din("w_mo", [2, 2, 128, 8, 512])
    din("w_ffi", [2, 12, 128, 8, 512])
    din("w_ffo", [2, 6, 128, 4, 1024])
    din("cstb", [128, 2320])
    din("cstf", [128, 21])
    din("w_sbi", [6, 128, 8, 512])
    din("w_sbo", [2, 128, 8, 512])
    if cfg.get("odd", True):
        din("poolk", [2560 * 128, 1024])
        din("poolv", [2560 * 128, 1024])
        din("pt", [256], I32)
    din("cstg", [128, 4 * 128])
    din("cstm", [128, 8 * 128])
    din("w_bg", [128, 8, 8])
    din("dtb", [128, 8])
    din("convw", [128, 12, 4])
    din("gnrm", [128, 1])
    din("w_gdn", [4, 128, 8, 512])
    din("w_evo", [2, 128, 8, 512])
    din("rwp", [128, 42])
    din("w2a", [128, 4, 128])
    din("g2", [128, 4, 128])
    din("w_rwl", [128, 8, 512])
    din("w_rwp", [4, 128, 8, 512])
    din("shiftT", [1792, 4])
    din("rwkv_s0T", [4, 4, 128, 64])
    dout("o_shiftT", [1792, 1])
    dout("o_shiftTs", [1792, 4])
    dout("o_rwkvT_p", [4, 128, 64])
    dout("o_rwkvT_s", [4, 4, 128, 64])
    din("convT", [4, 1536, 3])
    din("gdn_s0", [4, 4, 128, 128])
    dout("o_convT", [1536, 3])
    dout("o_convTs", [4, 1536, 3])
    dout("o_gdn_p", [4, 128, 128])
    dout("o_gdn_s", [4, 4, 128, 128])
    dout("o_sbkT", [D, NT])
    dout("o_sbv", [NT, D])
    dout("o_yT", [D, NT])
    dout("o_memkT", [2, D, 256])
    dout("o_memv", [2, 256, D])
    dr = kb.dr
    with kb.stack:
        st = kb.stack
        kb.X = kb.sb("X", [128, 8, NT], F32)
        kb.ONES = kb.sb("ONES", [128, 128], BF16)
        CB = kb.sb("CB", [128, 2320], BF16)
        CF = kb.sb("CF", [128, 21], F32)
        CG = kb.sb("CG", [128, 4 * 128], F32)
        CM = kb.sb("CM", [128, 8 * 128], BF16)
        kb.IDF, kb.TRIF, kb.ONESF, kb.NEGF = CG[:, 0:128], CG[:, 128:256], CG[:, 256:384], CG[:, 384:512]
        kb.MSU, kb.MIU, kb.MSL = CM[:, 0:128], CM[:, 128:256], CM[:, 256:384]
        kb.XSU, kb.XIU, kb.XIUN, kb.XSLN, kb.BLK64 = CM[:, 384:512], CM[:, 512:640], CM[:, 640:768], CM[:, 768:896], CM[:, 896:1024]
        kb.IDENT, kb.TRIU, kb.TRIL, kb.NEGM = CB[:, 0:128], CB[:, 128:256], CB[:, 256:384], CB[:, 384:1280]
        kb.BLKM, kb.SEL, kb.MASK8 = CB[:, 1280:2304], CB[:, 2304:2312], CB[:, 2312:2320]
        kb.PIDX, kb.ONEC, kb.EPSC, kb.BIASQ, kb.BIASH, kb.GNEPS = CF[:, 0:1], CF[:, 1:2], CF[:, 2:3], CF[:, 3:4], CF[:, 4:20], CF[:, 20:21]
        GC = kb.sb("GC", [128, 9, 8], F32)
        kb.WS = [kb.sb(f"WS{i}", [128, 8, 512], BF16) for i in range(3)]
        kb.PS = [st.enter_context(nc.psum_tensor(f"PS{i}", [128, 512], F32)) for i in range(8)]
        kb.ARENA = kb.sb("ARENA", [128, ARENA_BYTES // 4], F32)
        kb.ARENA_BYTES = ARENA_BYTES
        kb.G = {"mix": [GC[:, 0, :], GC[:, 1, :]], "mem": [GC[:, 2, :], GC[:, 3, :]],
                "memtok": [GC[:, 4, :], GC[:, 5, :]], "ffn": [GC[:, 6, :], GC[:, 7, :]], "final": GC[:, 8, :]}
        kb.memset(kb.ONES[:, :], 1.0, [("ones",)])
        kb.dma(CB[:, :], dr["cstb"], [], [("const",)], "const", eng="pool", max_dma_last_dim=4096)
        kb.dma(CF[:, :], dr["cstf"], [], [("constf",), ("par",)], "constf")
        kb.dma(CG[:, :], dr["cstg"], [], [("constf",)], "constf")
        kb.dma(CM[:, :], dr["cstm"], [], [("const",)], "const", eng="pool", max_dma_last_dim=4096)
        for i in range(8):
            kb.dma(GC[:, i, :], dr["gcols"][i], [], [("par",)], "par")
        kb.dma(GC[:, 8, :], dr["gfinal"], [], [("par",)], "par")
        for k in range(8):
            kb.dma(kb.X[:, k, :], dr["xT"][k * 128:(k + 1) * 128, :], [], [("X", k, g) for g in range(len(TILES))], "xin")
        S.barrier()
        dbg = cfg.get("dbg", "all")
        if dbg == "A":
            for k in range(8):
                kb.dma(dr["o_yT"][k * 128:(k + 1) * 128, :], kb.X[:, k, :], [("X", k, g) for g in range(len(TILES))], [], "yo")
        elif dbg == "B":
            kb.final_phase()
        elif dbg == "C":
            kb.mem_phase(0)
            kb.final_phase()
        elif dbg == "D":
            kb.ffn_phase(0)
            kb.final_phase()
        elif dbg == "E":
            kb.even_phase()
            kb.final_phase()
        elif dbg == "O":
            kb.odd_phase()
            kb.final_phase()
        else:
            for L in range(2):
                if L == 0 and cfg.get("even", True):
                    kb.even_phase()
                if L == 1 and cfg.get("odd", True):
                    kb.odd_phase()
                kb.mem_phase(L)
                kb.ffn_phase(L)
            kb.final_phase()
        S.emit(st)
    return nc


def prep_inputs(inp, c):
    f = np.float32
    m = {}
    xs = inp["x_sample"][4 * c:4 * c + 4].reshape(NS, D)
    m["xT"] = np.ascontiguousarray(np.concatenate([inp["x_prompt"][c][:NP], xs], axis=0).T.astype(f))
    m["memT"] = np.ascontiguousarray(inp["mem_prompt"][c].T)
    m["cmkT"] = np.ascontiguousarray(inp["cache_mem_k"][:, 4 * c:4 * c + 4].reshape(2, 4, 256, D).transpose(0, 1, 3, 2))
    m["cmv"] = np.ascontiguousarray(inp["cache_mem_v"][:, 4 * c:4 * c + 4].reshape(2, 4, 256, D))
    m["convT"] = np.ascontiguousarray(inp["state_gdn_conv"][0, 4 * c:4 * c + 4].transpose(0, 2, 1))
    m["gdn_s0"] = np.ascontiguousarray(inp["state_gdn"][0, 4 * c:4 * c + 4])
    m["shiftT"] = np.ascontiguousarray(inp["state_rwkv_shift"][0, 4 * c:4 * c + 4].T)
    sr = inp["state_rwkv"][0, 4 * c:4 * c + 4]
    m["rwkv_s0T"] = np.ascontiguousarray(sr.transpose(0, 1, 3, 2).reshape(4, 4, 128, 64))
    if "page_table" in inp:
        m["pt"] = np.ascontiguousarray(inp["page_table"][4 * c:4 * c + 4].reshape(256).astype(np.int32))
    return m


def prep_pools(inp):
    pk = inp["cache_sb_k"][0].reshape(2560, 128, 8, 128)
    pk = np.ascontiguousarray(pk.transpose(0, 3, 2, 1)).reshape(2560 * 128, 1024)
    pv = np.ascontiguousarray(inp["cache_sb_v"][0]).reshape(2560 * 128, 1024)
    return pk, pv


def consts(sb_bias):
    cb = np.zeros((128, 2320), np.float32)
    k = np.arange(128)
    cb[:, 0:128] = np.eye(128)
    cb[:, 128:256] = (k[:, None] > k[None, :])
    cb[:, 256:384] = (k[:, None] <= k[None, :])
    c = np.arange(896)
    cb[:, 384:1280] = np.where(k[:, None] < c[None, :] - 384, 0.0, -1600.0)
    hq = k // 8
    hp = np.arange(1024) // 64
    cb[:, 1280:2304] = (hq[:, None] == hp[None, :])
    cb[:, 2304:2312] = ((k % 8)[:, None] == np.arange(8)[None, :])
    cb[:, 2312:2320] = (np.arange(8)[None, :] < (k % 8)[:, None])
    cf = np.zeros((128, 21), np.float32)
    cf[:, 20] = 64e-5
    cf[:, 0] = k
    cf[:, 1] = 1.0
    cf[:, 2] = EPS
    cf[:, 3] = np.repeat(sb_bias, 8)
    cf[:, 4:20] = np.broadcast_to(sb_bias[None, :], (128, 16))
    return cb, cf


def consts2():
    k = np.arange(128)
    cg = np.zeros((128, 512), np.float32)
    cg[:, 0:128] = np.eye(128)
    cg[:, 128:256] = (k[:, None] <= k[None, :])
    cg[:, 256:384] = 1.0
    cg[:, 384:512] = -1.0
    cm = np.zeros((128, 1024), np.float32)
    r, c = k[:, None], k[None, :]
    NEG = -10000.0
    cm[:, 0:128] = np.where(c > r, 0.0, NEG)
    cm[:, 128:256] = np.where(c >= r, 0.0, NEG)
    cm[:, 256:384] = np.where(r > c, 0.0, NEG)
    cm[:, 384:512] = np.where(c > r, -1.0, 0.0)
    cm[:, 512:640] = np.where(c >= r, 1.0, 0.0)
    cm[:, 640:768] = np.where(c >= r, -1.0, 0.0)
    cm[:, 768:896] = np.where(r > c, -1.0, 0.0)
    cm[:, 896:1024] = ((r // 64) == (c // 64))
    return cg, cm


def prep_shared(inp):
    m = {}
    m["cstb"], m["cstf"] = consts(np.asarray(inp["sb_bias"][0], np.float32))
    m["cstg"], m["cstm"] = consts2()
    wi = inp["ev_w_in"][0]
    m["w_bg"] = np.ascontiguousarray(wi[:, 2048:2056].reshape(8, 128, 8).transpose(1, 0, 2))
    dtb = np.zeros((128, 8), np.float32)
    dtb[:, 0:4] = inp["gdn_dt_bias"][0][None, :]
    dtb[:, 4:8] = inp["gdn_a_log"][0][None, :]
    m["dtb"] = dtb
    cw = inp["gdn_conv_w"][0]
    m["convw"] = np.ascontiguousarray(cw.reshape(4, 12, 128).transpose(2, 1, 0))
    m["gnrm"] = np.ascontiguousarray(inp["gdn_norm"][0].reshape(128, 1))
    gcols = []
    for h in range(4):
        cols = np.concatenate([np.arange(j * 512 + h * 128, j * 512 + (h + 1) * 128) for j in range(4)])
        gcols.append(wi[:, cols])
    m["w_gdn"] = np.stack([w_slabs(g, [(0, 512)])[0] for g in gcols])
    m["w_evo"] = w_slabs(inp["ev_w_out"][0], [(0, 512), (512, 1024)])
    RB = 2056
    wr = wi[:, RB:RB + 1792]
    m["w_rwl"] = w_slabs(wr, [(1536, 1792)])[0]
    pc = []
    for p in range(4):
        cols = np.concatenate([np.arange(j * 512 + p * 128, j * 512 + (p + 1) * 128) for j in range(3)])
        pc.append(w_slabs(wr[:, cols], [(0, 384)])[0])
    m["w_rwp"] = np.stack(pc)
    rwp = np.zeros((128, 42), np.float32)
    mu = inp["rwkv_mu"][0]
    for p in range(4):
        for j in range(3):
            rwp[:, p * 3 + j] = mu[j * 512 + p * 128: j * 512 + (p + 1) * 128]
    rwp[:, 12] = mu[1536:1664]
    rwp[:, 13] = mu[1664:1792]
    for wi_, key in enumerate(["rwkv_w0", "rwkv_a0", "rwkv_k_k", "rwkv_k_a", "rwkv_r_k", "rwkv_gn_g", "rwkv_gn_b"]):
        rwp[:, 14 + wi_ * 4: 18 + wi_ * 4] = col_param(inp[key][0], 4)
    m["rwp"] = rwp
    w2a = np.zeros((128, 4, 128), np.float32)
    w2a[0:64] = inp["rwkv_w2"][0].reshape(64, 4, 128)
    w2a[64:128] = inp["rwkv_a2"][0].reshape(64, 4, 128)
    m["w2a"] = w2a
    m["g2"] = np.ascontiguousarray(inp["rwkv_g2"][0].reshape(128, 4, 128))
    m["w_sbi"] = w_slabs(inp["sb_w_in"][0], [(i * 512, (i + 1) * 512) for i in range(6)])
    m["w_sbo"] = w_slabs(inp["sb_w_out"][0], [(0, 512), (512, 1024)])
    g = [inp["norm_mix"][0], inp["norm_mix"][1], inp["norm_mem"][0], inp["norm_mem"][1],
         inp["norm_memtok"][0], inp["norm_memtok"][1], inp["norm_ffn"][0], inp["norm_ffn"][1]]
    m["gcols"] = np.stack([col_param(v, 8) for v in g])
    m["gfinal"] = col_param(inp["norm_final"], 8)
    r2 = [(0, 512), (512, 1024)]
    for nm, key in (("w_mq", "mem_w_q"), ("w_mk", "mem_w_k"), ("w_mv", "mem_w_v"), ("w_mo", "mem_w_o")):
        m[nm] = np.stack([w_slabs(inp[key][L], r2) for L in range(2)])
    rg = [(i * 512, min((i + 1) * 512, DFF)) for i in range(6)]
    rr = rg + [(DFF + a, DFF + b) for a, b in rg]
    m["w_ffi"] = np.stack([w_slabs(inp["ffn_w_in"][L], rr) for L in range(2)])
    wo = np.zeros((2, 6, 128, 4, 1024), np.float32)
    for L in range(2):
        w = inp["ffn_w_out"][L].reshape(22, 128, D)
        for hg in range(6):
            nch = 4 if hg < 5 else 2
            wo[L, hg, :, :nch, :] = w[hg * 4:hg * 4 + nch].transpose(1, 0, 2)
    m["w_ffo"] = wo
    return m


def odd_phase(self):
    dr = self.dr
    S = self.S
    nc = self.nc
    self.arena_reset()
    self.ps_n = 4
    HG = self.arena([128, 8, 512], BF16)
    QT = self.arena([128, 8, 512], BF16)
    OTA = self.arena([128, 8, 512], BF16)
    tmp = self.norm_tmp(sq=QT, sqkeys=[("qt", k) for k in range(8)])
    STG = [self.arena([128, 512], F32) for _ in range(2)]
    mark = self.ar_off
    KT = self.arena([128, 8, NP], BF16)
    nkb = NP // 128
    V = self.arena([128, nkb, 1024], BF16)
    EB = [self.arena([128, 512], BF16) for _ in range(2)]
    SPB = [self.arena([128, 512], BF16) for _ in range(2)]
    TBh = [self.arena([128, 512], BF16) for _ in range(4)]
    HGT = [HG[:, k, :] for k in range(8)]
    ATT = [self.arena([128, 512], BF16) for _ in range(2)]
    stg_i = [0]

    def stage_out(ps_ap, pskey, n, dst, bf_dst, bf_key, npart=128):
        i = stg_i[0]
        stg_i[0] ^= 1
        self.copy(STG[i][0:npart, 0:n], ps_ap, [pskey], [("stg", i)], eng="dve")
        self.copy(bf_dst, STG[i][0:npart, 0:n], [("stg", i)], [bf_key], eng="pool")
        self.dma(dst, STG[i][0:npart, 0:n], [("stg", i)], [], f"stg{i}")

    rot = self.psum
    blk = 0
    for g, (t0, n) in enumerate(TILES):
        prompt = g < len(TILES) - 1
        if not prompt:
            S.barrier()
            self.ar_off = mark
        xk = [("X", k, g) for k in range(8)]
        hk = [("hg", k) for k in range(8)]
        self.rmsnorm(lambda k: self.X[:, k, t0:t0 + n], xk, self.G["mix"][1], lambda k: HG[:, k, 0:n], hk, n, tmp)
        for s in range(2):
            slot = self.load_slab(dr["w_sbi"][s])

            def consume(j, ti, ps, pskey, s=s):
                self.copy(QT[:, s * 4 + j, 0:n], ps, [pskey], [("qt", s * 4 + j)], eng="act")
            self.proj_fm(slot, 512, None, [(hk, lambda k: HG[:, k, 0:n], n)], consume)
        if not prompt:
            KN = self.arena([128, 8, 32], BF16)
            VN = self.arena([8, 4, 1024], BF16) if False else self.arena([128, 4, 1024], BF16)
        for s in range(2):
            slot = self.load_slab(dr["w_sbi"][2 + s])

            def consume(j, ti, ps, pskey, s=s):
                oc = s * 4 + j
                dst = KT[:, oc, t0:t0 + n] if prompt else KN[:, oc, 0:n]
                stage_out(ps, pskey, n, dr["o_sbkT"][oc * 128:(oc + 1) * 128, t0:t0 + n], dst, ("kt", oc, g))
            self.proj_fm(slot, 512, None, [(hk, lambda k: HG[:, k, 0:n], n)], consume)
        for s in range(2):
            slot = self.load_slab(dr["w_sbi"][4 + s])
            w = self.WS[slot]
            if prompt:
                for tb in range(4):
                    pi, ps = self.psum()
                    for k in range(8):
                        self.mm(ps[:, :], HG[:, k, tb * 128:(tb + 1) * 128], w[:, k, :], k == 0, k == 7,
                                [("w", slot), ("hg", k)], [("ps", pi)])
                    kb = g * 4 + tb
                    stage_out(ps[:, :], ("ps", pi), 512, dr["o_sbv"][t0 + tb * 128:t0 + (tb + 1) * 128, s * 512:(s + 1) * 512],
                              V[:, kb, s * 512:(s + 1) * 512], ("v", kb, s))
            else:
                for sq in range(4):
                    pi, ps = self.psum()
                    for k in range(8):
                        self.mm(ps[0:8, :], HG[:, k, sq * 8:(sq + 1) * 8], w[:, k, :], k == 0, k == 7,
                                [("w", slot), ("hg", k)], [("ps", pi)])
                    stage_out(ps[0:8, :], ("ps", pi), 512, dr["o_sbv"][t0 + sq * 8:t0 + (sq + 1) * 8, s * 512:(s + 1) * 512],
                              VN[0:8, sq, s * 512:(s + 1) * 512], ("vn", sq, s), npart=8)
        if prompt:
            nkv = (g + 1) * 4
            S.barrier()
            blocks = [(2 * c + hh, kb) for c in range(8) for kb in range(nkv - 1, -1, -1) for hh in range(2)]
            NB = len(blocks)
            DE = 6
            EBp = [HGT[i] for i in range(6)]
            SPp = [HGT[6], HGT[7]] + EB + SPB
            TBp = [TBh[0], TBh[1], TBh[2]]
            ATp = [TBh[3], ATT[0], ATT[1]]
            zb = {}

            def stA(i):
                h, kb = blocks[i]
                c, po = h // 2, (h % 2) * 64
                pz, psz = rot()
                diag = kb >= g * 4
                self.mm(psz[:, :], KT[po:po + 64, c, kb * 128:(kb + 1) * 128], QT[po:po + 64, c, 0:512], True, not diag,
                        [("kt", c, kb // 4), ("qt", c)], [("ps", pz)])
                if diag:
                    off = 384 - 128 * (kb - g * 4)
                    self.mm(psz[:, :], self.IDENT[:, :], self.NEGM[:, off:off + 512], False, True, [("const",)], [("ps", pz)])
                e, sp = i % DE, i % DE
                self.act(EBp[e][:, :], psz[:, :], AF.Exp, [("ps", pz)], [("eb", e)], bias=self.BIASH[:, h:h + 1], scale=0.125)
                self.act(SPp[sp][:, :], EBp[e][:, :], AF.Ln, [("eb", e)], [("spb", sp)], bias=self.ONEC[:, 0:1])

            def banks(h):
                ch = h % 2
                return 4 + ch * 2, self.PS[4 + ch * 2], 5 + ch * 2, self.PS[5 + ch * 2]

            def stB1(i):
                h, kb = blocks[i]
                pc, psc, pob, pso = banks(h)
                sp, tb = i % DE, i % 3
                self.mm(psc[:, :], self.TRIU[:, :], SPp[sp][:, :], kb == nkv - 1, True, [("spb", sp), ("const",)], [("ps", pc)], skip=True)
                self.tt(TBp[tb][:, :], SPp[sp][:, :], psc[:, :], ALU.add, [("spb", sp), ("ps", pc)], [("tb", tb)])

            def stB2(i):
                h, kb = blocks[i]
                pc, psc, pob, pso = banks(h)
                sp, tb, e, at = i % DE, i % 3, i % DE, i % 3
                self.mm(psc[:, :], self.TRIL[:, :], SPp[sp][:, :], False, True, [("spb", sp), ("const",)], [("ps", pc)], skip=True)
                self.act(TBp[tb][:, :], TBp[tb][:, :], AF.Exp, [("tb", tb)], [("tb", tb)], scale=-1.0)
                self.tt(ATp[at][:, :], EBp[e][:, :], TBp[tb][:, :], ALU.mult, [("eb", e), ("tb", tb)], [("att", at)])

            def stC(i):
                h, kb = blocks[i]
                c, po = h // 2, (h % 2) * 64
                pc, psc, pob, pso = banks(h)
                oap = pso[po:po + 64, :]
                at = i % 3
                self.mm(oap, V[:, kb, h * 64:(h + 1) * 64], ATp[at][:, :], kb == nkv - 1, kb == 0,
                        [("v", kb, h // 8), ("att", at)], [("ps", pob)])
                if kb == 0:
                    self.copy(OTA[po:po + 64, c, 0:512], oap, [("ps", pob)], [("ota", c, h % 2)], eng="act")
            for i in range(NB + 4):
                if i < NB:
                    stA(i)
                if 0 <= i - 2 < NB:
                    stB1(i - 2)
                if 0 <= i - 3 < NB:
                    stB2(i - 3)
                if 0 <= i - 4 < NB:
                    stC(i - 4)
            S.barrier()
        elif self.cfg.get("osample", True):
            odd_sample(self, HG, QT, OTA, KN, VN, rot)
        for s in range(2):
            slot = self.load_slab(dr["w_sbo"][s])

            def consume(j, ti, ps, pskey, s=s):
                oc = s * 4 + j
                self.tt(self.X[:, oc, t0:t0 + n], self.X[:, oc, t0:t0 + n], ps, ALU.add, [pskey, ("X", oc, g)], [("X", oc, g)])
            rk = [("ota", k, 0) for k in range(8)] + [("ota", k, 1) for k in range(8)]
            self.proj_fm(slot, 512, None, [(rk, lambda k: OTA[:, k, 0:n], n)], consume)
    self.ps_n = 8
    S.barrier()


KB.odd_phase = odd_phase


def odd_sample(self, HG, QT, OTA, KN, VN, rot):
    dr = self.dr
    nc = self.nc
    NPG = 64
    QB = self.arena([128, 8, 128], BF16)
    E = [self.arena([128, 512], BF16) for _ in range(2)]
    SP = [self.arena([128, 512], BF16) for _ in range(2)]
    CS = [self.arena([128, 512], F32) for _ in range(2)]
    WT = [self.arena([128, 512], F32) for _ in range(2)]
    AT = [self.arena([128, 512], BF16) for _ in range(2)]
    ATTT = [self.arena([128, 4, 128], BF16) for _ in range(2)]
    KTP = [self.arena([128, 8, 128], BF16) for _ in range(8)]
    VP = [self.arena([128, 1024], BF16) for _ in range(8)]
    AM = self.arena([128, 1024], BF16)
    CAR = self.arena([128, 4], F32)
    IDX = self.arena([128, 256], I32)
    PTB = self.arena([128, 256], I32)
    self.dma(PTB[:, :], dr["pt"].partition_broadcast(128), [], [("ptb",)], "ptb")
    self.ts(IDX[:, :], PTB[:, :], 128.0, self.PIDX[:, 0:1], ALU.mult, ALU.add, [("ptb",), ("const",)], [("idx",)])
    self.memset(QB[:, :, :], 0.0, [("qb",)])
    items = []
    for sq in range(4):
        ch = [("new", None)] + [("pg", pgp) for pgp in range(15, -1, -1)]
        for ci, (kind, pgp) in enumerate(ch):
            items.append((sq, ci, kind, pgp, ci == len(ch) - 1))
    NI = len(items)
    kslots = {}
    vslots = {}
    E3 = E + [self.arena([128, 512], BF16)]
    SP3 = SP + [self.arena([128, 512], BF16)]
    AT3 = AT + [self.arena([128, 512], BF16)]
    pgk = [0]
    pgv = [0]

    def gather(dst, poolname, col, key, semname):
        self.S.op("pool", (lambda: nc.gpsimd.indirect_dma_start(
            out=dst, out_offset=None, in_=dr[poolname][:, :],
            in_offset=bass.IndirectOffsetOnAxis(ap=IDX[:, col:col + 1], axis=0))), [("idx",)], [key], dsem=semname)

    def stA(i):
        sq, ci, kind, pgp, lastc = items[i]
        b = i % 3
        nk = 8 if kind == "new" else 512
        if ci == 0:
            for c in range(8):
                for hh in range(2):
                    h = 2 * c + hh
                    self.copy(QB[hh * 64:(hh + 1) * 64, c, h * 8:(h + 1) * 8], QT[hh * 64:(hh + 1) * 64, c, sq * 8:(sq + 1) * 8],
                              [("qt", c)], [("qb",)], eng="pool")
        pz, psz = rot()
        if kind == "new":
            for c in range(8):
                self.mm(psz[:, 0:8], QB[:, c, :], KN[:, c, sq * 8:(sq + 1) * 8], c == 0, c == 7,
                        [("qb",), ("kt", c, len(TILES) - 1)], [("ps", pz)])
        else:
            for jj in range(4):
                j = pgp * 4 + jj
                sl = pgk[0] % 8
                pgk[0] += 1
                gather(KTP[sl].rearrange("p a b -> p (a b)"), "poolk", sq * NPG + j, ("ktp", sl), f"ktp{sl}")
                for c in range(8):
                    self.mm(psz[:, jj * 128:(jj + 1) * 128], QB[:, c, :], KTP[sl][:, c, :], c == 0, c == 7,
                            [("qb",), ("ktp", sl)], [("ps", pz)])
        self.act(E3[b][:, 0:nk], psz[:, 0:nk], AF.Exp, [("ps", pz)], [("e", b)], bias=self.BIASQ[:, 0:1], scale=0.125)
        if kind == "new":
            self.tt(E3[b][:, 0:8], E3[b][:, 0:8], self.MASK8[:, :], ALU.mult, [("e", b), ("const",)], [("e", b)])
        self.act(SP3[b][:, 0:nk], E3[b][:, 0:nk], AF.Ln, [("e", b)], [("sp", b)], bias=self.ONEC[:, 0:1])

    def stB(i):
        sq, ci, kind, pgp, lastc = items[i]
        b, b2 = i % 3, i % 2
        nk = 8 if kind == "new" else 512
        if kind != "new":
            sl_list = []
            for jj in range(4):
                j = pgp * 4 + jj
                sl = pgv[0] % 8
                pgv[0] += 1
                sl_list.append(sl)
                gather(VP[sl][:, :], "poolv", sq * NPG + j, ("vp", sl), f"vp{sl}")
            vslots[i] = sl_list
        if ci == 0:
            self.memset(CAR[:, 0:1], 0.0, [("car",)], eng="dve")
        self.S.op("dve", (lambda: nc.vector.tensor_tensor_scan(
            CS[b2][:, 0:nk], self.ONEC[:, 0:1].to_broadcast([128, nk]), SP3[b][:, 0:nk], 0.0, ALU.mult, ALU.add)),
            [("sp", b), ("const",)], [("cs", b2)])
        self.tt(CAR[:, 1:2], CAR[:, 0:1], CS[b2][:, nk - 1:nk], ALU.add, [("car",), ("cs", b2)], [("car1",)])
        self.ts(CAR[:, 2:3], CAR[:, 1:2], -1.0, None, ALU.mult, None, [("car1",)], [("car2",)])
        self.act(WT[b2][:, 0:nk], CS[b2][:, 0:nk], AF.Exp, [("cs", b2), ("car2",)], [("wt", b2)], bias=CAR[:, 2:3])
        self.copy(CAR[:, 0:1], CAR[:, 1:2], [("car1",), ("car2",)], [("car",)])
        self.tt(AT3[b][:, 0:nk], E3[b][:, 0:nk], WT[b2][:, 0:nk], ALU.mult, [("e", b), ("wt", b2)], [("at", b)])

    def stC(i):
        sq, ci, kind, pgp, lastc = items[i]
        b, b2 = i % 3, i % 2
        fa, fb = (4, 5) if sq % 2 == 0 else (6, 7)
        psfa, psfb = self.PS[fa], self.PS[fb]
        pt_, pst = rot()
        pstb = pst[:].bitcast(BF16)
        nblk = 1 if kind == "new" else 4
        kk = 8 if kind == "new" else 128
        for jj in range(nblk):
            self.S.op("pe", (lambda jj=jj: nc.tensor.transpose(
                pstb[0:kk, jj * 128:(jj + 1) * 128], AT3[b][:, jj * 128:jj * 128 + kk], self.IDENT[:, :])),
                [("at", b), ("const",)], [("ps", pt_)])
        self.copy(ATTT[b2][0:kk, 0:nblk, :], pstb[0:kk, 0:nblk * 128].rearrange("p (a b) -> p a b", b=128),
                  [("ps", pt_)], [("attt", b2)], eng="dve")
        for jj in range(nblk):
            last = lastc and (jj == nblk - 1)
            first = (ci == 0) and (jj == 0)
            if kind == "new":
                rv = lambda half: VN[0:8, sq, half * 512:(half + 1) * 512]
                rk = [("vn", sq, 0), ("vn", sq, 1)]
            else:
                sl = vslots[i][jj]
                rv = lambda half, sl=sl: VP[sl][:, half * 512:(half + 1) * 512]
                rk = [("vp", sl)]
            self.mm(psfa[:, :], ATTT[b2][0:kk, jj, :], rv(0), first, last, [("attt", b2)] + rk, [("ps", fa)])
            self.mm(psfb[:, :], ATTT[b2][0:kk, jj, :], rv(1), first, last, [("attt", b2)] + rk, [("ps", fb)])
        if lastc:
            self.tt(AM[:, 0:512], psfa[:, :], self.BLKM[:, 0:512], ALU.mult, [("ps", fa), ("const",)], [("am", 0)])
            self.tt(AM[:, 512:1024], psfb[:, :], self.BLKM[:, 512:1024], ALU.mult, [("ps", fb), ("const",)], [("am", 1)])
            po_, pso = rot()
            for c in range(8):
                self.mm(pso[:, c * 8:(c + 1) * 8], AM[:, c * 128:(c + 1) * 128], self.SEL[:, :], True, True,
                        [("am", c // 4), ("const",)], [("ps", po_)])
            self.copy(OTA[:, :, sq * 8:(sq + 1) * 8], pso[:, 0:64].rearrange("p (a b) -> p a b", b=8), [("ps", po_)],
                      [("ota", k, 0) for k in range(8)] + [("ota", k, 1) for k in range(8)], eng="act")
    for i in range(NI + 2):
        if i < NI:
            stA(i)
        if 0 <= i - 1 < NI:
            stB(i - 1)
        if 0 <= i - 2 < NI:
            stC(i - 2)


def neumann_TT(self, Mt, Nt, TT, nch, C, mk, nk, tk, tag, TTb=None, xw_ttb=(), xw_tt=()):
    tbk = ("ttb", tag)
    if TTb is None:
        TTb = TT
        tbk = tk
        ident = self.IDF
    else:
        ident = self.IDENT
    for c in range(nch):
        self.tt(TTb[0:C, c, 0:C], Mt[0:C, c, 0:C], ident[0:C, 0:C], ALU.add, [mk], [tbk] + list(xw_ttb), eng="pool")
    p = 1
    while 2 * p < C:
        last = 4 * p >= C
        pn, psn = self.psum()
        pm, psm = self.psum()
        for c in range(nch):
            self.mm(psn[0:C, c * C:(c + 1) * C], Mt[0:C, c, 0:C], Nt[0:C, c, 0:C], True, True, [mk, nk], [("ps", pn)])
            if not last:
                self.mm(psm[0:C, c * C:(c + 1) * C], Nt[0:C, c, 0:C], Mt[0:C, c, 0:C], True, True, [mk, nk], [("ps", pm)])
        self.copy(Nt[0:C, 0:nch, 0:C], psn[0:C, 0:nch * C].rearrange("p (a b) -> p a b", b=C), [("ps", pn)], [nk], eng="act")
        if not last:
            self.copy(Mt[0:C, 0:nch, 0:C], psm[0:C, 0:nch * C].rearrange("p (a b) -> p a b", b=C), [("ps", pm)], [mk], eng="dve")
        pt_, pst = self.psum()
        for c in range(nch):
            self.mm(pst[0:C, c * C:(c + 1) * C], Nt[0:C, c, 0:C], TTb[0:C, c, 0:C], True, True, [nk, tbk], [("ps", pt_)])
        if last:
            self.tt(TT[0:C, 0:nch, 0:C], TTb[0:C, 0:nch, 0:C], pst[0:C, 0:nch * C].rearrange("p (a b) -> p a b", b=C), ALU.add,
                    [("ps", pt_), tbk], [tk] + list(xw_tt))
        else:
            self.tt(TTb[0:C, 0:nch, 0:C], TTb[0:C, 0:nch, 0:C], pst[0:C, 0:nch * C].rearrange("p (a b) -> p a b", b=C), ALU.add,
                    [("ps", pt_), tbk], [tbk])
        p *= 2
        yield


KB.neumann_TT = neumann_TT


def transpose_f32(self, out_ps, in_ap, kpart, reads, pkey):
    nc = self.nc
    return self.S.op("pe", lambda: nc.tensor.transpose(out_ps, in_ap, self.IDF[0:kpart, 0:kpart]), reads, [pkey])


KB.transpose_f32 = transpose_f32


def even_phase(self):
    dr = self.dr
    S = self.S
    nc = self.nc
    self.arena_reset()
    A = self.arena
    HG = A([128, 8, 512], BF16)
    OA = A([128, 8, 512], BF16)
    tmp = self.norm_tmp(sq=OA, sqkeys=[("oa", k) for k in range(8)])
    SG = A([128, 4, 128], F32)
    HISTC = A([128, 12, 3], F32)
    PR = A([128, 4, 64], F32)
    HISTR = A([128, 16], F32)
    BGW = A([128, 8, 8], BF16)
    DTB = A([128, 8], F32)
    CONVW = A([128, 12, 4], F32)
    GNRM = A([128, 1], F32)
    mark = self.ar_off
    self.dma(BGW[:, :, :], dr["w_bg"], [], [("bgw",)], "bgw", eng="pool", max_dma_last_dim=4096)
    self.dma(DTB[:, :], dr["dtb"], [], [("dtb",)], "evp")
    self.dma(CONVW[:, :, :], dr["convw"], [], [("convw",)], "evp")
    self.dma(GNRM[:, :], dr["gnrm"], [], [("gnrm",)], "evp")
    self.act(DTB[:, 4:8], DTB[:, 4:8], AF.Exp, [("dtb",)], [("dtb",)])
    self.ts(DTB[:, 4:8], DTB[:, 4:8], -1.0, None, ALU.mult, None, [("dtb",)], [("dtb",)])
    self.memset(SG[:, :, :], 0.0, [("sg", h) for h in range(4)])
    self.memset(HISTC[:, :, :], 0.0, [("histc",)])
    ntl = len(TILES)
    for g, (t0, n) in enumerate(TILES):
        prompt = g < ntl - 1
        nch, C = (4, 128) if prompt else (4, 8)
        nseg, ntok = (1, 512) if prompt else (4, 8)
        S.barrier()
        self.ar_off = mark
        xk = [("X", k, g) for k in range(8)]
        hk = [("hg", k) for k in range(8)]
        self.rmsnorm(lambda k: self.X[:, k, t0:t0 + n], xk, self.G["mix"][0], lambda k: HG[:, k, 0:n], hk, n, tmp)
        BG = A([128, 4, 8], F32)
        LB = A([128, 4, 4], F32)
        GG = A([128, 4, 4], F32)
        GCL = A([128, 4, 8], F32)
        SM = A([128, 4, 16], F32)
        pi, ps = self.psum()
        for c in range(nch):
            for k in range(8):
                self.mm(ps[0:C, c * 8:(c + 1) * 8], HG[:, k, c * C:(c + 1) * C], BGW[:, k, :], k == 0, k == 7,
                        [("hg", k), ("bgw",)], [("ps", pi)])
        psv = ps[0:C, 0:nch * 8].rearrange("p (a b) -> p a b", b=8)
        self.copy(BG[0:C, 0:nch, :], psv, [("ps", pi)], [("bg",)], eng="dve")
        self.act(LB[0:C, 0:nch, :], BG[0:C, 0:nch, 0:4], AF.Exp, [("bg",)], [("lb",)], scale=-1.0)
        self.act(LB[0:C, 0:nch, :], LB[0:C, 0:nch, :], AF.Ln, [("lb",)], [("lb",)], bias=self.ONEC[0:C, 0:1])
        self.ts(LB[0:C, 0:nch, :], LB[0:C, 0:nch, :], -1.0, None, ALU.mult, None, [("lb",)], [("lb",)])
        for c in range(nch):
            self.tt(GG[0:C, c, :], BG[0:C, c, 4:8], DTB[0:C, 0:4], ALU.add, [("bg",), ("dtb",)], [("gg",)])
        self.act(GG[0:C, 0:nch, :], GG[0:C, 0:nch, :], AF.Exp, [("gg",)], [("gg",)])
        self.act(GG[0:C, 0:nch, :], GG[0:C, 0:nch, :], AF.Ln, [("gg",)], [("gg",)], bias=self.ONEC[0:C, 0:1])
        for c in range(nch):
            self.tt(GG[0:C, c, :], GG[0:C, c, :], DTB[0:C, 4:8], ALU.mult, [("gg",), ("dtb",)], [("gg",)])
        pi, ps = self.psum()
        for c in range(nch):
            self.mm(ps[0:C, c * 8:c * 8 + 4], self.TRIF[0:C, 0:C], GG[0:C, c, :], True, True, [("gg",)], [("ps", pi)])
            self.mm(ps[0:C, c * 8 + 4:c * 8 + 8], self.ONESF[0:C, 0:C], GG[0:C, c, :], True, True, [("gg",)], [("ps", pi)])
        self.copy(GCL[0:C, 0:nch, :], ps[0:C, 0:nch * 8].rearrange("p (a b) -> p a b", b=8), [("ps", pi)], [("gcl",)], eng="dve")
        self.act(SM[0:C, 0:nch, 0:4], LB[0:C, 0:nch, :], AF.Exp, [("lb",)], [("sm",)])
        self.tt(SM[0:C, 0:nch, 12:16], GCL[0:C, 0:nch, 0:4], LB[0:C, 0:nch, :], ALU.add, [("gcl",), ("lb",), ("sm",)], [("sm",)])
        self.act(SM[0:C, 0:nch, 4:8], SM[0:C, 0:nch, 12:16], AF.Exp, [("sm",)], [("sm",)])
        self.tt(SM[0:C, 0:nch, 12:16], GCL[0:C, 0:nch, 4:8], GCL[0:C, 0:nch, 0:4], ALU.subtract, [("gcl",), ("sm",)], [("sm",)])
        self.act(SM[0:C, 0:nch, 8:12], SM[0:C, 0:nch, 12:16], AF.Exp, [("sm",)], [("sm",)])
        mark_g = self.ar_off
        if cfg_get(self, "gdn", True):
            for h in range(4):
                gdn_head(self, h, g, t0, n, nch, C, nseg, ntok, HG, OA, SG, HISTC, CONVW, GNRM, LB, GG, SM, tmp)
        else:
            for h in range(4):
                self.memset(OA[:, h, 0:n], 0.0, [("oa", h)], eng="pool")
        if cfg_get(self, "rwkv", True):
            rwkv_part(self, g, t0, n, nch, C, nseg, ntok, HG, OA, PR, HISTR, tmp, mark2=mark_g)
        else:
            for h in range(4, 8):
                self.memset(OA[:, h, 0:n], 0.0, [("oa", h)], eng="pool")
        ok = [("oa", k) for k in range(8)]
        for s in range(2):
            slot = self.load_slab(dr["w_evo"][s])

            def consume(j, ti, ps, pskey, s=s):
                oc = s * 4 + j
                self.tt(self.X[:, oc, t0:t0 + n], self.X[:, oc, t0:t0 + n], ps, ALU.add, [pskey, ("X", oc, g)], [("X", oc, g)])
            self.proj_fm(slot, 512, None, [(ok, lambda k: OA[:, k, 0:n], n)], consume)
    S.barrier()


KB.even_phase = even_phase


def cfg_get(self, k, d):
    return self.cfg.get(k, d)


def gdn_head(self, h, g, t0, n, nch, C, nseg, ntok, HG, OA, SG, HISTC, CONVW, GNRM, LB, GG, SM, tmp):
    dr = self.dr
    nc = self.nc
    A = self.arena
    ntl = len(TILES)
    prompt = g < ntl - 1
    first_alloc = not hasattr(self, "_gdnbuf") or self._gdnbuf[0] != g
    if first_alloc:
        b = {}
        b["U"] = [A([128, nseg, 3 + ntok], F32) for _ in range(3)]
        b["CV"] = [A([128, 512], F32) for _ in range(3)]
        b["ZS"] = A([128, 512], F32)
        b["SQ"] = A([128, 512], BF16)
        b["KBG"] = A([128, 4, 128], F32)
        b["KE"] = A([128, 4, 128], F32)
        b["VT"] = A([128, 4, 128], F32)
        b["YG"] = A([128, 4, 128], F32)
        b["YGB"] = A([128, 4, 128], F32)
        b["EX"] = [A([128, 4, 128], F32) for _ in range(3)]
        b["GAMB"] = A([128, 512], F32)
        b["MT"] = A([128, 4, 128], F32)
        b["NT"] = A([128, 4, 128], F32)
        b["AQ"] = A([128, 4, 128], F32)
        b["TT"] = A([128, 4, 128], F32)
        b["TBT"] = A([128, 4, 128], F32)
        b["GNT"] = A([128, 512], F32)
        b["QG"] = A([128, 512], F32)
        b["US"] = A([128, 128], F32)
        b["OT"] = A([128, 512], F32)
        b["SS"] = A([128, 128], F32)
        self._gdnbuf = (g, b)
    b = self._gdnbuf[1]
    U, CV, ZS, SQ = b["U"], b["CV"], b["ZS"], b["SQ"]
    KBG, KE, VT, YG, YGB, EX, GAMB = b["KBG"], b["KE"], b["VT"], b["YG"], b["YGB"], b["EX"], b["GAMB"]
    MT, NT_, AQ, TT, TBT, GNT, QG, US, OT, SS = b["MT"], b["NT"], b["AQ"], b["TT"], b["TBT"], b["GNT"], b["QG"], b["US"], b["OT"], b["SS"]
    r1, r2 = tmp["r1"], tmp["r2"]
    for j in range(3):
        if prompt:
            self.copy(U[j][:, 0, 0:3], HISTC[:, j * 4 + h, :], [("histc",)], [("u", j)], eng="pool")
        else:
            self.dma(U[j][:, :, 0:3], dr["convT"][:, j * 512 + h * 128: j * 512 + (h + 1) * 128, :].rearrange("s p t -> p s t"),
                     [], [("u", j)], "uh")
    slot = self.load_slab(dr["w_gdn"][h])
    hk = [("hg", k) for k in range(8)]

    def consume(j, ti, ps, pskey):
        if j < 3:
            self.copy(U[j][:, :, 3:3 + ntok], ps.rearrange("p (s t) -> p s t", s=nseg), [pskey], [("u", j)], eng="act")
        else:
            self.act(ZS[:, 0:n], ps, AF.Silu, [pskey], [("zs",)])
    self.proj_fm(slot, 512, None, [(hk, lambda k: HG[:, k, 0:n], n)], consume)
    for j in range(3):
        if prompt:
            self.copy(HISTC[:, j * 4 + h, :], U[j][:, 0, ntok:ntok + 3], [("u", j)], [("histc",)], eng="pool")
            if g == ntl - 2:
                self.dma(dr["o_convT"][j * 512 + h * 128: j * 512 + (h + 1) * 128, :], U[j][:, 0, ntok:ntok + 3], [("u", j)], [], "oc")
        else:
            self.dma(dr["o_convTs"][:, j * 512 + h * 128: j * 512 + (h + 1) * 128, :].rearrange("s p t -> p s t"),
                     U[j][:, :, ntok:ntok + 3], [("u", j)], [], "oc")
    for j in range(3):
        cv = CV[j][:, 0:n].rearrange("p (s t) -> p s t", s=nseg)
        wc = CONVW[:, j * 4 + h, :]
        self.ts(cv, U[j][:, :, 0:ntok], wc[:, 0:1], None, ALU.mult, None, [("u", j), ("convw",)], [("cv", j)])
        for tp in range(1, 4):
            self.stt(cv, U[j][:, :, tp:tp + ntok], wc[:, tp:tp + 1], cv, ALU.mult, ALU.add, [("u", j), ("cv", j), ("convw",)], [("cv", j)])
        self.act(CV[j][:, 0:n], CV[j][:, 0:n], AF.Silu, [("cv", j)], [("cv", j)])
    for j in range(2):
        self.act(SQ[:, 0:n], CV[j][:, 0:n], AF.Square, [("cv", j)], [("sq",)])
        pi, ps = self.psum()
        self.mm(ps[:, 0:n], self.ONES[:, :], SQ[:, 0:n], True, True, [("sq",)], [("ps", pi)])
        self.act(r1[:, 0:n], ps[:, 0:n], AF.Ln, [("ps", pi)], [("r1", tmp["id"])], bias=self.EPSC[:, 0:1])
        self.act(r2[:, 0:n], r1[:, 0:n], AF.Exp, [("r1", tmp["id"])], [("r2", tmp["id"])], scale=-0.5)
        if j == 0:
            self.stt(CV[0][:, 0:n], CV[0][:, 0:n], 128.0 ** -0.5, r2[:, 0:n], ALU.mult, ALU.mult, [("cv", 0), ("r2", tmp["id"])], [("cv", 0)])
        else:
            self.tt(CV[1][:, 0:n], CV[1][:, 0:n], r2[:, 0:n], ALU.mult, [("cv", 1), ("r2", tmp["id"])], [("cv", 1)])
    QN, KN, VV = CV[0], CV[1], CV[2]
    pa, psa = self.psum()
    pb, psb = self.psum()
    for c in range(nch):
        self.transpose_f32(psa[0:C, c * 128:(c + 1) * 128], KN[:, c * C:(c + 1) * C], 128, [("cv", 1)], ("ps", pa))
        self.transpose_f32(psb[0:C, c * 128:(c + 1) * 128], VV[:, c * C:(c + 1) * C], 128, [("cv", 2)], ("ps", pb))
    for c in range(nch):
        self.ts(KBG[0:C, c, :], psa[0:C, c * 128:(c + 1) * 128], SM[0:C, c, 4 + h:5 + h], None, ALU.mult, None,
                [("ps", pa), ("sm",)], [("kbg",)])
        self.ts(KE[0:C, c, :], psa[0:C, c * 128:(c + 1) * 128], SM[0:C, c, 8 + h:9 + h], None, ALU.mult, None,
                [("ps", pa), ("sm",)], [("ke",)])
    self.copy(VT[0:C, 0:nch, :], psb[0:C, 0:nch * 128].rearrange("p (a b) -> p a b", b=128), [("ps", pb)], [("vt",)], eng="act")
    for c in range(nch):
        self.ts(YG[0:C, c, 0:C], self.TRIF[0:C, 0:C], GG[0:C, c, h:h + 1], None, ALU.mult, None, [("gg",)], [("yg",)], eng="pool")
        self.stt(YGB[0:C, c, 0:C], self.IDF[0:C, 0:C], LB[0:C, c, h:h + 1], YG[0:C, c, 0:C], ALU.mult, ALU.add,
                 [("lb",), ("yg",)], [("ygb",)])
    specs = [
        ("ones", "ygb", "yg", "neg", self.MSU),
        ("ygb", "ones", "neg", "yg", self.MSL),
        ("ones", "yg", "yg", "neg", self.MIU),
    ]
    for e, (l1, r1_, l2, r2_, msk) in enumerate(specs):
        pi, ps = self.psum()
        for c in range(nch):
            def opnd(nm):
                if nm == "ones":
                    return self.ONESF[0:C, 0:C]
                if nm == "neg":
                    return self.NEGF[0:C, 0:C]
                if nm == "yg":
                    return YG[0:C, c, 0:C]
                return YGB[0:C, c, 0:C]
            o = ps[0:C, c * C:(c + 1) * C]
            self.mm(o, opnd(l1), opnd(r1_), True, False, [("yg",), ("ygb",)], [("ps", pi)])
            self.mm(o, opnd(l2), opnd(r2_), False, False, [("yg",), ("ygb",)], [("ps", pi)])
            self.mm(o, self.IDENT[0:C, 0:C], msk[0:C, 0:C], False, True, [], [("ps", pi)])
        self.act(EX[e][0:C, 0:nch, 0:C], ps[0:C, 0:nch * C].rearrange("p (a b) -> p a b", b=C), AF.Exp, [("ps", pi)], [("ex", e)])
    pi, ps = self.psum()
    for c in range(nch):
        self.mm(ps[:, c * C:(c + 1) * C], self.ONESF[0:C, :], YG[0:C, c, 0:C], True, True, [("yg",)], [("ps", pi)])
    self.act(GAMB[:, 0:n], ps[:, 0:n], AF.Exp, [("ps", pi)], [("gamb",)])
    pk_, psk = self.psum()
    pq_, psq = self.psum()
    for c in range(nch):
        self.mm(psk[0:C, c * C:(c + 1) * C], KN[:, c * C:(c + 1) * C], KN[:, c * C:(c + 1) * C], True, True, [("cv", 1)], [("ps", pk_)])
        self.mm(psq[0:C, c * C:(c + 1) * C], KN[:, c * C:(c + 1) * C], QN[:, c * C:(c + 1) * C], True, True, [("cv", 0), ("cv", 1)], [("ps", pq_)])
    v3 = lambda p_: p_[0:C, 0:nch * C].rearrange("p (a b) -> p a b", b=C)
    self.stt(MT[0:C, 0:nch, 0:C], v3(psk), -1.0, EX[0][0:C, 0:nch, 0:C], ALU.mult, ALU.mult, [("ps", pk_), ("ex", 0)], [("mt",)])
    self.stt(NT_[0:C, 0:nch, 0:C], v3(psk), -1.0, EX[1][0:C, 0:nch, 0:C], ALU.mult, ALU.mult, [("ps", pk_), ("ex", 1)], [("nt",)])
    self.tt(AQ[0:C, 0:nch, 0:C], v3(psq), EX[2][0:C, 0:nch, 0:C], ALU.mult, [("ps", pq_), ("ex", 2)], [("aq",)])
    for _ in self.neumann_TT(MT, NT_, TT, nch, C, ("mt",), ("nt",), ("tt",), "g"):
        pass
    for c in range(nch):
        self.ts(TBT[0:C, c, 0:C], TT[0:C, c, 0:C], SM[0:C, c, h:h + 1], None, ALU.mult, None, [("tt",), ("sm",)], [("tbt",)], eng="pool")
    pi, ps = self.psum()
    for c in range(nch):
        self.mm(ps[:, c * C:(c + 1) * C], KBG[0:C, c, :], TT[0:C, c, 0:C], True, True, [("kbg",), ("tt",)], [("ps", pi)])
    self.act(GNT[:, 0:n], ps[:, 0:n], AF.Identity, [("ps", pi)], [("gnt",)], scale=-1.0)
    self.tt(QG[:, 0:n], QN[:, 0:n], GAMB[:, 0:n], ALU.mult, [("cv", 0), ("gamb",)], [("qg",)], eng="pool")
    for c in range(nch):
        cs = slice(c * C, (c + 1) * C)
        if prompt:
            Sst, skey = SG[:, h, :], ("sg", h)
        else:
            Sst, skey = SS[:, :], ("ss",)
            self.dma(SS[:, :], dr["gdn_s0"][c, h], [], [("ss",)], "ss")
        pu, psu = self.psum()
        self.mm(psu[0:C, 0:128], TBT[0:C, c, 0:C], VT[0:C, c, :], True, False, [("tbt",), ("vt",)], [("ps", pu)])
        self.mm(psu[0:C, 0:128], GNT[:, cs], Sst, False, True, [("gnt",), skey], [("ps", pu)])
        self.copy(US[0:C, :], psu[0:C, 0:128], [("ps", pu)], [("us",)], eng="act")
        po, pso = self.psum()
        self.mm(pso[:, 0:C], Sst, QG[:, cs], True, False, [skey, ("qg",)], [("ps", po)])
        self.mm(pso[:, 0:C], US[0:C, :], AQ[0:C, c, 0:C], False, True, [("us",), ("aq",)], [("ps", po)])
        pc_, psc = self.psum()
        self.mm(psc[:, 0:128], KE[0:C, c, :], US[0:C, :], True, True, [("ke",), ("us",)], [("ps", pc_)])
        self.stt(Sst, Sst, GAMB[:, (c + 1) * C - 1:(c + 1) * C], psc[:, 0:128], ALU.mult, ALU.add,
                 [skey, ("gamb",), ("ps", pc_)], [skey])
        self.copy(OT[:, cs], pso[:, 0:C], [("ps", po)], [("ot",)], eng="act")
        if not prompt:
            self.dma(dr["o_gdn_s"][c, h], SS[:, :], [("ss",)], [], "sso")
    if prompt and g == ntl - 2:
        self.dma(dr["o_gdn_p"][h], SG[:, h, :], [("sg", h)], [], "sso")
    self.act(SQ[:, 0:n], OT[:, 0:n], AF.Square, [("ot",)], [("sq",)])
    pi, ps = self.psum()
    self.mm(ps[:, 0:n], self.ONES[:, :], SQ[:, 0:n], True, True, [("sq",)], [("ps", pi)])
    self.act(r1[:, 0:n], ps[:, 0:n], AF.Ln, [("ps", pi)], [("r1", tmp["id"])], bias=self.EPSC[:, 0:1], scale=1.0 / 128.0)
    self.act(r2[:, 0:n], r1[:, 0:n], AF.Exp, [("r1", tmp["id"])], [("r2", tmp["id"])], scale=-0.5)
    self.stt(OT[:, 0:n], OT[:, 0:n], GNRM[:, 0:1], r2[:, 0:n], ALU.mult, ALU.mult, [("ot",), ("r2", tmp["id"]), ("gnrm",)], [("ot",)])
    self.tt(OA[:, h, 0:n], OT[:, 0:n], ZS[:, 0:n], ALU.mult, [("ot",), ("zs",)], [("oa", h)])


def rwkv_part(self, g, t0, n, nch, C, nseg, ntok, HG, OA, PR, HISTR, tmp, mark2):
    dr = self.dr
    nc = self.nc
    A = self.arena
    S = self.S
    ntl = len(TILES)
    prompt = g < ntl - 1
    S.barrier()
    self.ar_off = mark2
    r1, r2 = tmp["r1"], tmp["r2"]
    r1k, r2k = ("r1", tmp["id"]), ("r2", tmp["id"])
    RWP = A([128, 42], F32)
    W2A = A([128, 4, 128], BF16)
    G2 = A([128, 4, 128], BF16)
    self.dma(RWP[:, :], dr["rwp"], [], [("rwp",)], "rwp")
    self.dma(W2A[:, :, :], dr["w2a"], [], [("w2a",)], "w2a", eng="pool", max_dma_last_dim=4096)
    self.dma(G2[:, :, :], dr["g2"], [], [("w2a",)], "w2a", eng="pool", max_dma_last_dim=4096)
    MU = lambda ci: RWP[:, ci:ci + 1]
    PC = lambda which, p: RWP[:, 14 + which * 4 + p: 15 + which * 4 + p]
    RL = [A([128, nseg, 1 + ntok], F32) for _ in range(3)]
    XR = [A([128, 512], F32) for _ in range(3)]
    DT = A([128, 512], F32)
    LWA = A([128, 512], BF16)
    LG = A([128, 512], BF16)
    hk = [("hg", k) for k in range(8)]
    v3 = lambda ap: ap.rearrange("p (s t) -> p s t", s=nseg)

    def load_hist(buf, bkey, ci, feat0):
        if prompt:
            if g == 0:
                self.memset(buf[:, 0, 0:1], 0.0, [bkey], eng="pool")
            else:
                self.copy(buf[:, 0, 0:1], HISTR[:, ci:ci + 1], [("histr",)], [bkey], eng="pool")
        else:
            self.dma(buf[:, :, 0:1], dr["shiftT"][feat0:feat0 + 128, :].rearrange("p (s o) -> p s o", o=1), [], [bkey], "uh",
                     allow_slow_non_contiguous=True)

    def save_hist(buf, bkey, ci, feat0):
        if prompt:
            self.copy(HISTR[:, ci:ci + 1], buf[:, 0, ntok:ntok + 1], [bkey], [("histr",)], eng="pool")
            if g == ntl - 2:
                self.dma(dr["o_shiftT"][feat0:feat0 + 128, :], buf[:, 0, ntok:ntok + 1], [bkey], [], "oc", allow_slow_non_contiguous=True)
        else:
            self.dma(dr["o_shiftTs"][feat0:feat0 + 128, :].rearrange("p (s o) -> p s o", o=1), buf[:, :, ntok:ntok + 1], [bkey], [], "oc",
                     allow_slow_non_contiguous=True)

    def shift_mix(buf, bkey, ci, out, okey):
        self.tt(v3(DT[:, 0:n]), buf[:, :, 0:ntok], buf[:, :, 1:1 + ntok], ALU.subtract, [bkey], [("dt",)])
        self.stt(v3(out), v3(DT[:, 0:n]), MU(ci), buf[:, :, 1:1 + ntok], ALU.mult, ALU.add, [("dt",), bkey, ("rwp",)], [okey])

    for j in range(2):
        load_hist(RL[j], ("rl", j), 12 + j, 1536 + j * 128)
    slot = self.load_slab(dr["w_rwl"])

    def consume(j, ti, ps, pskey):
        self.copy(RL[j][:, :, 1:1 + ntok], v3(ps), [pskey], [("rl", j)], eng="act")
    self.proj_fm(slot, 256, None, [(hk, lambda k: HG[:, k, 0:n], n)], consume)
    for j in range(2):
        save_hist(RL[j], ("rl", j), 12 + j, 1536 + j * 128)
        shift_mix(RL[j], ("rl", j), 12 + j, XR[j][:, 0:n], ("xr", j))
    self.act(LWA[0:64, 0:n], XR[0][0:64, 0:n], AF.Tanh, [("xr", 0)], [("lwa",)])
    self.copy(LWA[64:128, 0:n], XR[0][64:128, 0:n], [("xr", 0)], [("lwa",)], eng="pool")
    self.act(LG[:, 0:n], XR[1][:, 0:n], AF.Sigmoid, [("xr", 1)], [("lg",)])
    LW = A([128, 512], F32)
    AA = A([128, 512], F32)
    GT_ = A([128, 512], F32)
    KK = A([128, 512], F32)
    BB = A([128, 512], F32)
    CW = A([128, 512], F32)
    EW = [A([128, 512], F32) for _ in range(2)]
    KKW, RW, KI, BI, KEF, NBE = [A([128, 512], F32) for _ in range(6)]
    SQ = A([128, 512], BF16)
    VTr, KET, NBET, KKWT = [A([128, 4, 128], F32) for _ in range(4)]
    TT, AKKN, ARKT, NARBT, TAT = [A([128, 4, 128], F32) for _ in range(5)]
    MT, NT_, TTB = [A([128, 4, 128], BF16) for _ in range(3)]
    GTT = A([128, 512], F32)
    US = [A([128, 64], F32) for _ in range(2)]
    YT = A([128, 512], F32)
    PS_ = A([128, 64], F32)
    for p in range(4):
        feats = [p * 128, 512 + p * 128, 1024 + p * 128]
        for j in range(3):
            load_hist(RL[j], ("rl", j), p * 3 + j, feats[j])
        slot = self.load_slab(dr["w_rwp"][p])

        def consume(j, ti, ps, pskey):
            self.copy(RL[j][:, :, 1:1 + ntok], v3(ps), [pskey], [("rl", j)], eng="act")
        self.proj_fm(slot, 384, None, [(hk, lambda k: HG[:, k, 0:n], n)], consume)
        for j in range(3):
            save_hist(RL[j], ("rl", j), p * 3 + j, feats[j])
            shift_mix(RL[j], ("rl", j), p * 3 + j, XR[j][:, 0:n], ("xr", j))
        R_, K_, V_ = XR[0], XR[1], XR[2]
        pw_, psw = self.psum()
        self.mm(psw[:, 0:n], W2A[0:64, p, :], LWA[0:64, 0:n], True, True, [("w2a",), ("lwa",)], [("ps", pw_)])
        pa_, psa = self.psum()
        self.mm(psa[:, 0:n], W2A[64:128, p, :], LWA[64:128, 0:n], True, True, [("w2a",), ("lwa",)], [("ps", pa_)])
        pg_, psg = self.psum()
        self.mm(psg[:, 0:n], G2[:, p, :], LG[:, 0:n], True, True, [("w2a",), ("lg",)], [("ps", pg_)])
        self.act(LW[:, 0:n], psw[:, 0:n], AF.Sigmoid, [("ps", pw_), ("rwp",)], [("lw",)], bias=PC(0, p))
        self.ts(LW[:, 0:n], LW[:, 0:n], -float(np.exp(-0.5)), None, ALU.mult, None, [("lw",)], [("lw",)])
        self.act(AA[:, 0:n], psa[:, 0:n], AF.Sigmoid, [("ps", pa_), ("rwp",)], [("aa",)], bias=PC(1, p))
        self.copy(GT_[:, 0:n], psg[:, 0:n], [("ps", pg_)], [("gt",)], eng="act")
        self.ts(KK[:, 0:n], K_[:, 0:n], PC(2, p), None, ALU.mult, None, [("xr", 1), ("rwp",)], [("kk",)])
        self.act(SQ[:, 0:n], KK[:, 0:n], AF.Square, [("kk",)], [("sq",)])
        pi, ps = self.psum()
        self.mm(ps[:, 0:n], self.BLK64[:, :], SQ[:, 0:n], True, True, [("sq",)], [("ps", pi)])
        self.act(r1[:, 0:n], ps[:, 0:n], AF.Ln, [("ps", pi)], [r1k], bias=self.EPSC[:, 0:1])
        self.act(r2[:, 0:n], r1[:, 0:n], AF.Exp, [r1k], [r2k], scale=-0.5)
        self.tt(KK[:, 0:n], KK[:, 0:n], r2[:, 0:n], ALU.mult, [("kk",), r2k], [("kk",)])
        self.ts(DT[:, 0:n], AA[:, 0:n], -1.0, PC(3, p), ALU.add, ALU.mult, [("aa",), ("rwp",)], [("dt",)])
        self.stt(K_[:, 0:n], DT[:, 0:n], 1.0, K_[:, 0:n], ALU.add, ALU.mult, [("dt",), ("xr", 1)], [("xr", 1)])
        self.tt(BB[:, 0:n], KK[:, 0:n], AA[:, 0:n], ALU.mult, [("kk",), ("aa",)], [("bb",)])
        for c in range(nch):
            cs = slice(c * C, (c + 1) * C)
            self.S.op("dve", (lambda cs=cs: nc.vector.tensor_tensor_scan(
                CW[:, cs], self.ONEC[:, 0:1].to_broadcast([128, C]), LW[:, cs], 0.0, ALU.mult, ALU.add)),
                [("lw",)], [("cw",)])
        self.act(EW[0][:, 0:n], CW[:, 0:n], AF.Exp, [("cw",)], [("ew", 0)])
        self.tt(RW[:, 0:n], R_[:, 0:n], EW[0][:, 0:n], ALU.mult, [("xr", 0), ("ew", 0)], [("rw",)])
        self.tt(DT[:, 0:n], CW[:, 0:n], LW[:, 0:n], ALU.subtract, [("cw",), ("lw",)], [("dt",)])
        self.act(EW[1][:, 0:n], DT[:, 0:n], AF.Exp, [("dt",)], [("ew", 1)])
        self.tt(KKW[:, 0:n], KK[:, 0:n], EW[1][:, 0:n], ALU.mult, [("kk",), ("ew", 1)], [("kkw",)])
        self.act(EW[1][:, 0:n], CW[:, 0:n], AF.Exp, [("cw",), ("kkw",)], [("ew", 1)], scale=-1.0)
        self.tt(KI[:, 0:n], K_[:, 0:n], EW[1][:, 0:n], ALU.mult, [("xr", 1), ("ew", 1)], [("ki",)])
        self.tt(BI[:, 0:n], BB[:, 0:n], EW[1][:, 0:n], ALU.mult, [("bb",), ("ew", 1)], [("bi",)])
        for c in range(nch):
            cs = slice(c * C, (c + 1) * C)
            self.act(DT[:, cs], CW[:, cs], AF.Exp, [("cw",), ("dt",)], [("dt",)], bias=CW[:, (c + 1) * C - 1:(c + 1) * C], scale=-1.0)
        self.tt(KEF[:, 0:n], K_[:, 0:n], DT[:, 0:n], ALU.mult, [("xr", 1), ("dt",)], [("kef",)])
        self.stt(NBE[:, 0:n], BB[:, 0:n], -1.0, DT[:, 0:n], ALU.mult, ALU.mult, [("bb",), ("dt",)], [("nbe",)])
        for src, skey, dst, dkey in ((V_, ("xr", 2), VTr, ("vtr",)), (KEF, ("kef",), KET, ("ket",)),
                                     (NBE, ("nbe",), NBET, ("nbet",)), (KKW, ("kkw",), KKWT, ("kkwt",))):
            pi, ps = self.psum()
            for c in range(nch):
                self.transpose_f32(ps[0:C, c * 128:(c + 1) * 128], src[:, c * C:(c + 1) * C], 128, [skey], ("ps", pi))
            self.copy(dst[0:C, 0:nch, :], ps[0:C, 0:nch * 128].rearrange("p (a b) -> p a b", b=128), [("ps", pi)], [dkey], eng="act")
        pv3 = lambda p_: p_[0:C, 0:nch * C].rearrange("p (a b) -> p a b", b=C)
        v4 = lambda ap: ap.rearrange("p (a b) -> p a b", a=4)
        hb = [dict(MT=MT, NT=NT_, TTB=TTB, TT=TT, AKKN=AKKN, ARKT=ARKT, NARBT=NARBT, TAT=TAT, base={}),
              dict(MT=v4(KEF[:, 0:256].bitcast(BF16)), NT=v4(KEF[:, 256:512].bitcast(BF16)), TTB=v4(NBE[:, 0:256].bitcast(BF16)),
                   TT=v4(LW[:, :]), AKKN=v4(AA[:, :]), ARKT=v4(BB[:, :]), NARBT=v4(KK[:, :]), TAT=v4(CW[:, :]),
                   base={"mt": ("kef",), "nt": ("kef",), "ttb": ("nbe",), "tt": ("lw",), "akkn": ("aa",), "arkt": ("bb",),
                         "narbt": ("kk",), "tat": ("cw",)})]

        def head_gen(hh):
            B = hb[hh]
            bp = hh * 64
            hs = slice(bp, bp + 64)
            K = lambda nm: (nm, hh)
            used = set()

            def wk(nm):
                ks = [K(nm)]
                if nm in B["base"] and nm not in used:
                    used.add(nm)
                    ks.append(B["base"][nm])
                return ks
            specs = [(KI, ("ki",), RW, ("rw",), "ARKT", "arkt", self.XIU), (BI, ("bi",), KKW, ("kkw",), "MT", "mt", self.XSU),
                     (BI, ("bi",), RW, ("rw",), "NARBT", "narbt", self.XIUN), (KKW, ("kkw",), BI, ("bi",), "NT", "nt", self.XSLN),
                     (KKW, ("kkw",), KI, ("ki",), "AKKN", "akkn", self.XSLN)]
            for (l, lk, r, rk, dn, dk, msk) in specs:
                dst = B[dn]
                pi, ps = self.psum()
                for c in range(nch):
                    cs = slice(c * C, (c + 1) * C)
                    self.mm(ps[0:C, c * C:(c + 1) * C], l[hs, cs], r[hs, cs], True, True, [lk, rk], [("ps", pi)])
                for c in range(nch):
                    self.tt(dst[0:C, c, 0:C], ps[0:C, c * C:(c + 1) * C], msk[0:C, 0:C], ALU.mult, [("ps", pi)], wk(dk))
                yield
            xb = [B["base"]["ttb"]] if "ttb" in B["base"] else []
            xt = [B["base"]["tt"]] if "tt" in B["base"] else []
            for _ in self.neumann_TT(B["MT"], B["NT"], B["TT"], nch, C, K("mt"), K("nt"), K("tt"), ("r", hh), TTb=B["TTB"],
                                     xw_ttb=xb, xw_tt=xt):
                yield
            pi, ps = self.psum()
            for c in range(nch):
                self.mm(ps[0:C, c * C:(c + 1) * C], B["AKKN"][0:C, c, 0:C], B["TT"][0:C, c, 0:C], True, True, [K("akkn"), K("tt")], [("ps", pi)])
            self.act(B["TAT"][0:C, 0:nch, 0:C], pv3(ps), AF.Identity, [("ps", pi)], wk("tat"), scale=-1.0)
            pi, ps = self.psum()
            for c in range(nch):
                self.mm(ps[hs, c * C:(c + 1) * C], KKWT[0:C, c, hs], B["TT"][0:C, c, 0:C], True, True, [("kkwt",), K("tt")], [("ps", pi)])
            self.copy(GTT[hs, 0:n], ps[hs, 0:n], [("ps", pi)], [("gtt", hh)], eng="act")
            yield
            for c in range(nch):
                cs = slice(c * C, (c + 1) * C)
                if prompt:
                    P_, pkey = PR[hs, p, :], ("pr", p, hh)
                    if g == 0 and c == 0:
                        self.memset(PR[hs, p, :], 0.0, [pkey], eng="pool")
                else:
                    P_, pkey = PS_[hs, :], ("pss", hh)
                    self.dma(PS_[hs, :], dr["rwkv_s0T"][c, p, hs, :], [], [pkey], f"pss{hh}")
                u = US[hh]
                pu, psu = self.psum()
                self.mm(psu[0:C, 0:64], B["TAT"][0:C, c, 0:C], VTr[0:C, c, hs], True, False, [K("tat"), ("vtr",)], [("ps", pu)])
                self.mm(psu[0:C, 0:64], GTT[hs, cs], P_, False, True, [("gtt", hh), pkey], [("ps", pu)])
                self.copy(u[0:C, :], psu[0:C, 0:64], [("ps", pu)], [("us", hh)], eng="act")
                yield
                py, psy = self.psum()
                self.mm(psy[hs, 0:C], P_, RW[hs, cs], True, False, [pkey, ("rw",)], [("ps", py)])
                self.mm(psy[hs, 0:C], VTr[0:C, c, hs], B["ARKT"][0:C, c, 0:C], False, False, [("vtr",), K("arkt")], [("ps", py)])
                self.mm(psy[hs, 0:C], u[0:C, :], B["NARBT"][0:C, c, 0:C], False, True, [("us", hh), K("narbt")], [("ps", py)])
                pc_, psc = self.psum()
                self.mm(psc[hs, 0:64], KET[0:C, c, hs], VTr[0:C, c, hs], True, False, [("ket",), ("vtr",)], [("ps", pc_)])
                self.mm(psc[hs, 0:64], NBET[0:C, c, hs], u[0:C, :], False, True, [("nbet",), ("us", hh)], [("ps", pc_)])
                self.stt(P_, P_, EW[0][hs, (c + 1) * C - 1:(c + 1) * C], psc[hs, 0:64], ALU.mult, ALU.add,
                         [pkey, ("ew", 0), ("ps", pc_)], [pkey])
                self.copy(YT[hs, cs], psy[hs, 0:C], [("ps", py)], [("yt", hh)], eng="act")
                if not prompt:
                    self.dma(dr["o_rwkvT_s"][c, p, hs, :], PS_[hs, :], [pkey], [], f"pso{hh}")
                yield
            if prompt and g == ntl - 2:
                self.dma(dr["o_rwkvT_p"][p, hs, :], PR[hs, p, :], [("pr", p, hh)], [], "sso")
        gens = [head_gen(0), head_gen(1)]
        while gens:
            for gi in list(gens):
                try:
                    next(gi)
                except StopIteration:
                    gens.remove(gi)
        ytk = [("yt", 0), ("yt", 1)]
        self.copy(SQ[:, 0:n], YT[:, 0:n], ytk, [("sq",)], eng="pool")
        pi, ps = self.psum()
        self.mm(ps[:, 0:n], self.BLK64[:, :], SQ[:, 0:n], True, True, [("sq",)], [("ps", pi)])
        self.stt(YT[:, 0:n], ps[:, 0:n], -1.0 / 64.0, YT[:, 0:n], ALU.mult, ALU.add, [("ps", pi)] + ytk, ytk)
        self.act(SQ[:, 0:n], YT[:, 0:n], AF.Square, ytk, [("sq",)])
        pi, ps = self.psum()
        self.mm(ps[:, 0:n], self.BLK64[:, :], SQ[:, 0:n], True, True, [("sq",)], [("ps", pi)])
        self.act(r1[:, 0:n], ps[:, 0:n], AF.Ln, [("ps", pi)], [r1k], bias=self.GNEPS[:, 0:1], scale=1.0 / 64.0)
        self.act(r2[:, 0:n], r1[:, 0:n], AF.Exp, [r1k], [r2k], scale=-0.5)
        self.stt(YT[:, 0:n], YT[:, 0:n], PC(5, p), r2[:, 0:n], ALU.mult, ALU.mult, ytk + [r2k, ("rwp",)], ytk)
        self.stt(DT[:, 0:n], R_[:, 0:n], PC(4, p), K_[:, 0:n], ALU.mult, ALU.mult, [("xr", 0), ("xr", 1), ("rwp",)], [("dt",)])
        self.copy(SQ[:, 0:n], DT[:, 0:n], [("dt",)], [("sq",)], eng="pool")
        pi, ps = self.psum()
        self.mm(ps[:, 0:n], self.BLK64[:, :], SQ[:, 0:n], True, True, [("sq",)], [("ps", pi)])
        self.tt(DT[:, 0:n], ps[:, 0:n], V_[:, 0:n], ALU.mult, [("ps", pi), ("xr", 2)], [("dt",)])
        self.stt(YT[:, 0:n], YT[:, 0:n], PC(6, p), DT[:, 0:n], ALU.add, ALU.add, ytk + [("dt",), ("rwp",)], ytk)
        self.tt(OA[:, 4 + p, 0:n], YT[:, 0:n], GT_[:, 0:n], ALU.mult, ytk + [("gt",)], [("oa", 4 + p)])


_NC_CACHE = {}


def kernel(**inp):
    inp = {k: np.asarray(v) for k, v in inp.items()}
    ncores = 8
    if "nc" not in _NC_CACHE:
        _NC_CACHE["nc"] = build({})
    nc = _NC_CACHE["nc"]
    set_np(2048)
    shared = prep_shared(inp)
    pk, pv = prep_pools(inp)
    shared["poolk"], shared["poolv"] = pk, pv
    in_maps = []
    for c in range(ncores):
        m = prep_inputs(inp, c)
        m.update(shared)
        in_maps.append(m)
    res = run_bass_kernel_spmd(nc, in_maps, core_ids=list(range(ncores)))
    R = res.results
    f = np.float32

    def st(fn):
        return np.stack([fn(R[c]) for c in range(ncores)]).astype(f)

    def cat(fn):
        return np.concatenate([fn(R[c]) for c in range(ncores)], axis=0).astype(f)
    y_p = st(lambda r: r["o_yT"][:, :NP].T)
    y_s = cat(lambda r: r["o_yT"][:, NP:].T.reshape(4, 8, D))
    gdn_p = st(lambda r: r["o_gdn_p"])[None]
    gdn_s = cat(lambda r: r["o_gdn_s"])[None]
    conv_p = st(lambda r: r["o_convT"].T)[None]
    conv_s = cat(lambda r: r["o_convTs"].transpose(0, 2, 1))[None]
    rw_p = st(lambda r: r["o_rwkvT_p"].reshape(4, 2, 64, 64).transpose(0, 1, 3, 2).reshape(8, 64, 64))[None]
    rw_s = cat(lambda r: r["o_rwkvT_s"].reshape(4, 4, 2, 64, 64).transpose(0, 1, 2, 4, 3).reshape(4, 8, 64, 64))[None]
    sh_p = st(lambda r: r["o_shiftT"][:, 0])[None]
    sh_s = cat(lambda r: r["o_shiftTs"].T)[None]
    sbk_p = st(lambda r: r["o_sbkT"][:, :NP].T.reshape(NP, 16, 64))[None]
    sbk_s = cat(lambda r: r["o_sbkT"][:, NP:].T.reshape(4, 8, 16, 64))[None]
    sbv_p = st(lambda r: r["o_sbv"][:NP].reshape(NP, 16, 64))[None]
    sbv_s = cat(lambda r: r["o_sbv"][NP:].reshape(4, 8, 16, 64))[None]
    mk_p = np.stack([R[c]["o_memkT"].transpose(0, 2, 1).reshape(2, 256, 4, 256) for c in range(ncores)], axis=1).astype(f)
    mv_p = np.stack([R[c]["o_memv"].reshape(2, 256, 4, 256) for c in range(ncores)], axis=1).astype(f)
    return (y_p, y_s, gdn_p, gdn_s, conv_p, conv_s, rw_p, rw_s, sh_p, sh_s, sbk_p, sbk_s, sbv_p, sbv_s, mk_p, mv_p)
```

```python
import contextlib
import numpy as np
import concourse.bass as bass
import concourse.mybir as mybir
from concourse.bass_utils import run_bass_kernel_spmd

F32 = mybir.dt.float32
BF16 = mybir.dt.bfloat16
I32 = mybir.dt.int32
AF = mybir.ActivationFunctionType
ALU = mybir.AluOpType
AX = mybir.AxisListType

D = 1024
KC = 8
NP = 2048
NS = 32
NT = NP + NS
DFF = 2816
EPS = 1e-6

ENGS = ("pe", "act", "dve", "pool", "sp")
SEM_WRAP = 30000


class DSem:
    def __init__(self, name):
        self.name = name
        self.count = 0
        self.handle = None
        self.last = None


class Op:
    __slots__ = ("eng", "fn", "deps", "sig", "dsem", "dcount")

    def __init__(self, eng, fn):
        self.eng = eng
        self.fn = fn
        self.deps = []
        self.sig = None
        self.dsem = None
        self.dcount = None


class Sched:
    def __init__(self, nc):
        self.nc = nc
        self.streams = {e: [] for e in ENGS}
        self.last_w = {}
        self.readers = {}
        self.dsems = {}

    def dsem(self, name):
        if name not in self.dsems:
            self.dsems[name] = DSem(name)
        return self.dsems[name]

    def _dep(self, o, p):
        if p is None or p is o:
            return
        if p.dsem is not None:
            o.deps.append((p, p.dsem.count))
        elif not (p.eng == "pe" and o.eng == "pe"):
            o.deps.append((p, None))

    def op(self, eng, fn, reads=(), writes=(), dsem=None):
        o = Op(eng, fn)
        for k in reads:
            self._dep(o, self.last_w.get(k))
        for k in writes:
            self._dep(o, self.last_w.get(k))
            for r in self.readers.get(k, ()):
                self._dep(o, r)
        if dsem is not None:
            if isinstance(dsem, str):
                dsem = self.dsem(dsem)
            o.dsem = dsem
            dsem.count += 1
            o.dcount = dsem.count
            dsem.last = o
        for k in reads:
            self.readers.setdefault(k, []).append(o)
        for k in writes:
            self.last_w[k] = o
            self.readers[k] = []
        self.streams[eng].append(o)
        return o

    def barrier(self):
        lasts = [s[-1] for s in self.streams.values() if s]
        dl = [d.last for d in self.dsems.values() if d.last is not None]
        nc = self.nc
        nops = {"pe": lambda: nc.tensor.nop(), "act": lambda: nc.scalar.nop(), "dve": lambda: nc.vector.nop(),
                "pool": lambda: nc.gpsimd.nop(), "sp": lambda: nc.sync.nop()}
        for e in ENGS:
            o = Op(e, nops[e])
            for p in lasts:
                if p.dsem is None and not (p.eng == "pe" and e == "pe"):
                    o.deps.append((p, None))
            for p in dl:
                o.deps.append((p, p.dsem.count))
            self.streams[e].append(o)
        self.last_w = {}
        self.readers = {}

    def emit(self, stack):
        nc = self.nc
        for e in ENGS:
            for o in self.streams[e]:
                for (p, dc) in o.deps:
                    if p.dsem is None:
                        p.sig = 0
        nsig = {}
        for e in ENGS:
            c = 0
            for o in self.streams[e]:
                if o.dsem is None and o.sig is not None:
                    c += 1
                    o.sig = c
            nsig[e] = c
        esems = {}
        for e in ENGS:
            n = max(1, (nsig[e] + SEM_WRAP - 1) // SEM_WRAP)
            esems[e] = [stack.enter_context(nc.semaphore(f"s_{e}{i}")) for i in range(n)]
        for ds in self.dsems.values():
            ds.handle = stack.enter_context(nc.semaphore(f"d_{ds.name}"))
        engobj = {"pe": nc.tensor, "act": nc.scalar, "dve": nc.vector, "pool": nc.gpsimd, "sp": nc.sync}

        def emit_stream(e):
            eo = engobj[e]
            waited_e = {}
            waited_d = {}
            for o in self.streams[e]:
                need_e = {}
                need_d = {}
                for (p, dc) in o.deps:
                    if p.dsem is not None:
                        if need_d.get(p.dsem.name, 0) < dc:
                            need_d[p.dsem.name] = dc
                    else:
                        if need_e.get(p.eng, 0) < p.sig:
                            need_e[p.eng] = p.sig
                for pe_, sig in need_e.items():
                    if waited_e.get(pe_, 0) >= sig:
                        continue
                    si = (sig - 1) // SEM_WRAP
                    eo.wait_ge(esems[pe_][si], (sig - 1) % SEM_WRAP + 1)
                    waited_e[pe_] = sig
                for dn, dc in need_d.items():
                    if waited_d.get(dn, 0) >= dc:
                        continue
                    eo.wait_ge(self.dsems[dn].handle, dc * 16)
                    waited_d[dn] = dc
                ins = o.fn()
                if o.dsem is not None:
                    ins.then_inc(o.dsem.handle, 16)
                elif o.sig is not None:
                    si = (o.sig - 1) // SEM_WRAP
                    ins.then_inc(esems[e][si], 1)
            if e == "sp":
                for ds in self.dsems.values():
                    if ds.count and waited_d.get(ds.name, 0) < ds.count:
                        eo.wait_ge(ds.handle, ds.count * 16)

        block = stack.enter_context(nc.Block())

        @block.sync
        def _(eng):
            emit_stream("sp")

        @block.scalar
        def _(eng):
            emit_stream("act")

        @block.vector
        def _(eng):
            emit_stream("dve")

        @block.gpsimd
        def _(eng):
            emit_stream("pool")

        @block.tensor
        def _(eng):
            emit_stream("pe")


def w_slabs(w, col_ranges):
    K = w.shape[0]
    kc = K // 128
    out = np.zeros((len(col_ranges), 128, kc, 512), np.float32)
    wr = w.reshape(kc, 128, w.shape[1]).transpose(1, 0, 2)
    for i, (c0, c1) in enumerate(col_ranges):
        out[i, :, :, :c1 - c0] = wr[:, :, c0:c1]
    return out


def col_param(v, nch):
    return np.ascontiguousarray(np.asarray(v, np.float32).reshape(nch, 128).T)


TILES = [(0, 512), (512, 512), (1024, 512), (1536, 512), (2048, 32)]


class KB:
    def __init__(self, nc, cfg):
        self.nc = nc
        self.cfg = cfg
        self.S = Sched(nc)
        self.stack = contextlib.ExitStack()
        self.dr = {}
        self.ps_i = 0
        self.ps_n = 8
        self.w_i = 0
        self.uid = 0

    def din(self, name, shape, dtype=F32):
        self.dr[name] = self.nc.dram_tensor(name, list(shape), dtype, kind="ExternalInput").ap()
        return self.dr[name]

    def dout(self, name, shape, dtype=F32):
        self.dr[name] = self.nc.dram_tensor(name, list(shape), dtype, kind="ExternalOutput").ap()
        return self.dr[name]

    def sb(self, name, shape, dtype, stack=None):
        return (stack or self.stack).enter_context(self.nc.sbuf_tensor(name, list(shape), dtype))

    def psum(self):
        i = self.ps_i % self.ps_n
        self.ps_i = (i + 1) % self.ps_n
        return i, self.PS[i]

    def mm(self, out, lhsT, rhs, start, stop, reads, writes, skip=False):
        nc = self.nc
        if skip:
            return self.S.op("pe", lambda: nc.tensor.matmul(out, lhsT, rhs, start=start, stop=stop, skip_group_check=True), reads, writes)
        return self.S.op("pe", lambda: nc.tensor.matmul(out, lhsT, rhs, start=start, stop=stop), reads, writes)

    def act(self, out, in_, func, reads, writes, bias=None, scale=None, accum_out=None):
        nc = self.nc
        kw = {}
        if bias is not None:
            kw["bias"] = bias
        if scale is not None:
            kw["scale"] = scale
        if accum_out is not None:
            kw["accum_out"] = accum_out
        return self.S.op("act", lambda: nc.scalar.activation(out, in_, func, **kw), reads, writes)

    def tt(self, out, in0, in1, op, reads, writes, eng="dve"):
        nc = self.nc
        e = nc.vector if eng == "dve" else nc.gpsimd
        return self.S.op(eng, lambda: e.tensor_tensor(out, in0, in1, op), reads, writes)

    def ts(self, out, in0, s1, s2, op0, op1, reads, writes, eng="dve"):
        nc = self.nc
        e = nc.vector if eng == "dve" else nc.gpsimd
        if op1 is None:
            return self.S.op(eng, lambda: e.tensor_scalar(out, in0, s1, None, op0), reads, writes)
        return self.S.op(eng, lambda: e.tensor_scalar(out, in0, s1, s2, op0, op1), reads, writes)

    def stt(self, out, in0, scalar, in1, op0, op1, reads, writes):
        nc = self.nc
        return self.S.op("dve", lambda: nc.vector.scalar_tensor_tensor(out, in0, scalar, in1, op0, op1), reads, writes)

    def copy(self, out, in_, reads, writes, eng="dve"):
        nc = self.nc
        if eng == "act":
            return self.S.op("act", lambda: nc.scalar.activation(out, in_, AF.Identity), reads, writes)
        e = nc.vector if eng == "dve" else nc.gpsimd
        return self.S.op(eng, lambda: e.tensor_copy(out, in_), reads, writes)

    def memset(self, ap, val, writes, eng="pool"):
        nc = self.nc
        e = nc.vector if eng == "dve" else nc.gpsimd
        return self.S.op(eng, lambda: e.memset(ap, val), (), writes)

    def dma(self, out, in_, reads, writes, dsem, eng="sp", **kw):
        nc = self.nc
        e = nc.sync if eng == "sp" else nc.gpsimd
        return self.S.op(eng, lambda: e.dma_start(out=out, in_=in_, **kw), reads, writes, dsem=dsem)

    def load_slab(self, dram_slab, kc=8):
        slot = self.w_i
        self.w_i = (self.w_i + 1) % len(self.WS)
        w = self.WS[slot]
        self.dma(w[:, 0:kc, :], dram_slab, (), [("w", slot)], f"w{slot}", eng="pool", max_dma_last_dim=4096)
        return slot

    def proj_fm(self, slot, ncols, H, tiles, consume, kc=8):
        w = self.WS[slot]
        for ti, (hkeys, hfn, n) in enumerate(tiles):
            for j in range(ncols // 128):
                pi, ps = self.psum()
                for k in range(kc):
                    self.mm(ps[:, 0:n], w[:, k, j * 128:(j + 1) * 128], hfn(k), k == 0, k == kc - 1,
                            [("w", slot)] + list(hkeys), [("ps", pi)])
                consume(j, ti, ps[:, 0:n], ("ps", pi))

    def arena_reset(self):
        self.ar_off = 0

    def arena(self, shape, dtype):
        esz = 2 if dtype == BF16 else 4
        n = 1
        for d in shape[1:]:
            n *= d
        nbytes = (n * esz + 31) // 32 * 32
        off = self.ar_off
        assert off + nbytes <= self.ARENA_BYTES, f"arena overflow {off + nbytes} > {self.ARENA_BYTES}"
        self.ar_off += nbytes
        v = self.ARENA[:, off // 4: (off + nbytes) // 4]
        if dtype == BF16:
            v = v.bitcast(BF16)
        elif dtype == I32:
            v = v.bitcast(I32)
        v = v[:, 0:n]
        if len(shape) == 3:
            v = v.rearrange("p (a b) -> p a b", a=shape[1])
        elif len(shape) == 4:
            v = v.rearrange("p (a b c) -> p a b c", a=shape[1], b=shape[2])
        return v

    def key(self, base):
        self.uid += 1
        return (base, self.uid)

    def rmsnorm(self, src_fn, src_keys, gcol, out_fn, out_keys, n, tmp):
        pi, ps = self.psum()
        sq, r1, r2 = tmp["sq"], tmp["r1"], tmp["r2"]
        sqk = tmp.get("sqkeys") or [("sq", tmp["id"], k) for k in range(KC)]
        for k in range(KC):
            self.act(sq[:, k, 0:n], src_fn(k), AF.Square, [src_keys[k]], [sqk[k]])
        for k in range(KC):
            self.mm(ps[:, 0:n], self.ONES[:, :], sq[:, k, 0:n], k == 0, k == KC - 1,
                    [sqk[k], ("ones",)], [("ps", pi)])
        self.act(r1[:, 0:n], ps[:, 0:n], AF.Ln, [("ps", pi)], [("r1", tmp["id"])], bias=self.EPSC[:, 0:1], scale=1.0 / D)
        self.act(r2[:, 0:n], r1[:, 0:n], AF.Exp, [("r1", tmp["id"])], [("r2", tmp["id"])], scale=-0.5)
        for k in range(KC):
            self.stt(out_fn(k), src_fn(k), gcol[:, k:k + 1], r2[:, 0:n], ALU.mult, ALU.mult,
                     [src_keys[k], ("r2", tmp["id"]), ("par",)], [out_keys[k]])

    def norm_tmp(self, sq=None, sqkeys=None):
        self.uid += 1
        return {"sq": sq if sq is not None else self.arena([128, 8, 512], BF16), "sqkeys": sqkeys,
                "r1": self.arena([128, 512], F32), "r2": self.arena([128, 512], F32), "id": self.uid}

    def mem_phase(self, L):
        dr = self.dr
        self.arena_reset()
        tmp = self.norm_tmp()
        HG = self.arena([128, 8, 512], BF16)
        QT = self.arena([128, 8, 512], BF16)
        OT = self.arena([128, 8, 512], BF16)
        PT = self.arena([128, 2, 512], BF16)
        RD1 = self.arena([128, 512], F32)
        RD2 = self.arena([128, 512], F32)
        MKT = self.arena([128, 8, 256], BF16)
        MV = self.arena([128, 2, 1024], BF16)
        SMK = [self.arena([128, 8, 256], BF16) for _ in range(2)]
        SMV = [self.arena([128, 2, 1024], BF16) for _ in range(2)]
        MEMX = self.arena([128, 8, 256], F32)
        MN = self.arena([128, 8, 256], BF16)
        STG = [self.arena([128, 512], F32) for _ in range(2)]
        stg_i = [0]
        ws_base = self.WS
        self.WS = list(ws_base) + [self.arena([128, 8, 512], BF16) for _ in range(2)]
        self.w_i = 0

        def stage_out(ps_ap, pskey, n, dst, bf_dst, bf_key):
            i = stg_i[0]
            stg_i[0] ^= 1
            self.copy(STG[i][:, 0:n], ps_ap, [pskey], [("stg", i)], eng="dve")
            self.copy(bf_dst, STG[i][:, 0:n], [("stg", i)], [bf_key], eng="pool")
            self.dma(dst, STG[i][:, 0:n], [("stg", i)], [], f"stg{i}")

        self.dma(MEMX[:, :, :], dr["memT"].rearrange("(kc p) m -> p kc m", p=128), [], [("memx",)], "memx")
        self.rmsnorm(lambda k: MEMX[:, k, :], [("memx",)] * 8, self.G["memtok"][L],
                     lambda k: MN[:, k, :], [("mn", k) for k in range(8)], 256, tmp)
        mnkeys = [("mn", k) for k in range(8)]
        sub = self.cfg.get("sub", 3)
        for s in range(2):
            if sub == 10:
                continue
            slot = self.load_slab(dr["w_mk"][L, s])

            def consume(j, ti, ps, pskey, s=s):
                oc = s * 4 + j
                stage_out(ps, pskey, 256, dr["o_memkT"][L, oc * 128:(oc + 1) * 128, :], MKT[:, oc, :], ("mkt", oc))
            self.proj_fm(slot, 512, None, [(mnkeys, lambda k: MN[:, k, :], 256)], consume)
        for s in range(2):
            if sub in (10, 11):
                continue
            slot = self.load_slab(dr["w_mv"][L, s])
            w = self.WS[slot]
            for mb in range(2):
                pi, ps = self.psum()
                for k in range(8):
                    self.mm(ps[:, :], MN[:, k, mb * 128:(mb + 1) * 128], w[:, k, :], k == 0, k == 7,
                            [("w", slot), ("mn", k)], [("ps", pi)])
                stage_out(ps[:, :], ("ps", pi), 512, dr["o_memv"][L, mb * 128:(mb + 1) * 128, s * 512:(s + 1) * 512],
                          MV[:, mb, s * 512:(s + 1) * 512], ("mv", mb, s))

        def attend(kt, kv, kvkeys, c0, n):
            for hd in range(4):
                for mb in range(2):
                    pi, ps = self.psum()
                    for dc in range(2):
                        self.mm(ps[:, 0:n], kt[:, hd * 2 + dc, mb * 128:(mb + 1) * 128], QT[:, hd * 2 + dc, c0:c0 + n],
                                dc == 0, dc == 1, kvkeys + [("qt", hd * 2 + dc)], [("ps", pi)])
                    self.act(PT[:, mb, 0:n], ps[:, 0:n], AF.Exp, [("ps", pi)], [("pt", mb)], scale=1.0 / 16.0)
                pi, ps = self.psum()
                for mb in range(2):
                    self.mm(ps[:, 0:n], self.ONES[:, :], PT[:, mb, 0:n], mb == 0, mb == 1, [("pt", mb)], [("ps", pi)])
                self.act(RD1[:, 0:n], ps[:, 0:n], AF.Ln, [("ps", pi)], [("rd1",)])
                self.act(RD2[:, 0:n], RD1[:, 0:n], AF.Exp, [("rd1",)], [("rd2",)], scale=-1.0)
                for dc in range(2):
                    pi, ps = self.psum()
                    for mb in range(2):
                        self.mm(ps[:, 0:n], kv[:, mb, hd * 256 + dc * 128: hd * 256 + (dc + 1) * 128], PT[:, mb, 0:n],
                                mb == 0, mb == 1, kvkeys + [("pt", mb)], [("ps", pi)])
                    self.tt(OT[:, hd * 2 + dc, c0:c0 + n], ps[:, 0:n], RD2[:, 0:n], ALU.mult,
                            [("ps", pi), ("rd2",)], [("ot", hd * 2 + dc)])

        pkv = [("mkt", oc) for oc in range(8)] + [("mv", mb, s) for mb in range(2) for s in range(2)]
        mq_slots = mo_slots = None
        for g, (t0, n) in enumerate(TILES):
            if sub in (1, 10, 11) or (sub == 2 and g == len(TILES) - 1):
                continue
            xk = [("X", k, g) for k in range(8)]
            hk = [("hg", k) for k in range(8)]
            self.rmsnorm(lambda k: self.X[:, k, t0:t0 + n], xk, self.G["mem"][L],
                         lambda k: HG[:, k, 0:n], hk, n, tmp)
            if mq_slots is None:
                mq_slots = [self.load_slab(dr["w_mq"][L, s]) for s in range(2)]
                mo_slots = [self.load_slab(dr["w_mo"][L, s]) for s in range(2)]
            for s in range(2):
                slot = mq_slots[s]

                def consume(j, ti, ps, pskey, s=s):
                    oc = s * 4 + j
                    self.copy(QT[:, oc, 0:n], ps, [pskey], [("qt", oc)], eng="act")
                self.proj_fm(slot, 512, None, [(hk, lambda k: HG[:, k, 0:n], n)], consume)
            if g < len(TILES) - 1:
                attend(MKT, MV, pkv, 0, n)
            else:
                for sq in range(4):
                    b = sq % 2
                    self.dma(SMK[b][:, :, :], dr["cmkT"][L, sq].rearrange("(kc p) m -> p kc m", p=128),
                             [], [("smk", b)], f"smk{b}", eng="pool", max_dma_last_dim=1024)
                    self.dma(SMV[b][:, :, :], dr["cmv"][L, sq].rearrange("(mb p) f -> p mb f", p=128),
                             [], [("smv", b)], f"smv{b}", eng="pool", max_dma_last_dim=4096)
                    attend(SMK[b], SMV[b], [("smk", b), ("smv", b)], sq * 8, 8)
            ok = [("ot", k) for k in range(8)]
            for s in range(2):
                slot = mo_slots[s]

                def consume(j, ti, ps, pskey, s=s):
                    oc = s * 4 + j
                    self.tt(self.X[:, oc, t0:t0 + n], self.X[:, oc, t0:t0 + n], ps, ALU.add,
                            [pskey, ("X", oc, g)], [("X", oc, g)])
                self.proj_fm(slot, 512, None, [(ok, lambda k: OT[:, k, 0:n], n)], consume)
        self.S.barrier()
        self.WS = ws_base
        self.w_i = 0

    def ffn_phase(self, L):
        dr = self.dr
        self.arena_reset()
        tmp = self.norm_tmp()
        HA = self.arena([128, 8, NT], BF16)
        ACTB = [self.arena([128, 4, 512], BF16) for _ in range(2)]
        SG = [self.arena([128, 512], F32) for _ in range(2)]
        ws_base = self.WS
        self.WS = list(ws_base) + [self.arena([128, 8, 512], BF16) for _ in range(3)]
        self.w_i = 0
        for g, (t0, n) in enumerate(TILES):
            self.rmsnorm(lambda k: self.X[:, k, t0:t0 + n], [("X", k, g) for k in range(8)], self.G["ffn"][L],
                         lambda k: HA[:, k, t0:t0 + n], [("ha", k, g) for k in range(8)], n, tmp)
        cnt = 0
        for hg in range(6):
            nch = 4 if hg < 5 else 2
            sg_ = self.load_slab(dr["w_ffi"][L, hg])
            su_ = self.load_slab(dr["w_ffi"][L, 6 + hg])
            so_ = self.load_slab(dr["w_ffo"][L, hg].rearrange("p a (b c) -> p (a b) c", c=512))
            wg, wu = self.WS[sg_], self.WS[su_]
            wo = self.WS[so_].rearrange("p (a b) c -> p a (b c)", b=2)
            for g, (t0, n) in enumerate(TILES):
                ab = ACTB[cnt % 2]
                abk = cnt % 2
                cnt += 1
                hk = [("ha", k, g) for k in range(8)]
                for j in range(nch):
                    pg, psg = self.psum()
                    for k in range(8):
                        self.mm(psg[:, 0:n], wg[:, k, j * 128:(j + 1) * 128], HA[:, k, t0:t0 + n], k == 0, k == 7,
                                [("w", sg_), hk[k]], [("ps", pg)])
                    pu, psu = self.psum()
                    for k in range(8):
                        self.mm(psu[:, 0:n], wu[:, k, j * 128:(j + 1) * 128], HA[:, k, t0:t0 + n], k == 0, k == 7,
                                [("w", su_), hk[k]], [("ps", pu)])
                    si = j % 2
                    self.act(SG[si][:, 0:n], psg[:, 0:n], AF.Silu, [("ps", pg)], [("sg", si)])
                    self.tt(ab[:, j, 0:n], SG[si][:, 0:n], psu[:, 0:n], ALU.mult, [("sg", si), ("ps", pu)], [("ab", abk, j)])
                for oc in range(8):
                    po, pso = self.psum()
                    for j in range(nch):
                        self.mm(pso[:, 0:n], wo[:, j, oc * 128:(oc + 1) * 128], ab[:, j, 0:n], j == 0, j == nch - 1,
                                [("w", so_), ("ab", abk, j)], [("ps", po)])
                    self.tt(self.X[:, oc, t0:t0 + n], self.X[:, oc, t0:t0 + n], pso[:, 0:n], ALU.add,
                            [("ps", po), ("X", oc, g)], [("X", oc, g)])
        self.S.barrier()
        self.WS = ws_base
        self.w_i = 0

    def final_phase(self):
        dr = self.dr
        self.arena_reset()
        tmp = self.norm_tmp()
        YS = [self.arena([128, 8, 512], F32) for _ in range(2)]
        for g, (t0, n) in enumerate(TILES):
            b = g % 2
            self.rmsnorm(lambda k: self.X[:, k, t0:t0 + n], [("X", k, g) for k in range(8)], self.G["final"],
                         lambda k: YS[b][:, k, 0:n], [("ys", b, k) for k in range(8)], n, tmp)
            self.dma(dr["o_yT"][:, t0:t0 + n].rearrange("(kc p) t -> p kc t", p=128), YS[b][:, :, 0:n],
                     [("ys", b, k) for k in range(8)], [], f"ys{b}")
        self.S.barrier()


ARENA_BYTES = 110592


def set_np(n):
    global NP, NT, TILES
    NP = n
    NT = NP + NS
    TILES = [(i * 512, 512) for i in range(NP // 512)] + [(NP, NS)]


def build(cfg):
    set_np(cfg.get("np", 2048))
    nc = bass.Bass("TRN2", target_bir_lowering=False)
    kb = KB(nc, cfg)
    S = kb.S
    din, dout = kb.din, kb.dout
    din("xT", [D, NT])
    din("memT", [D, 256])
    din("cmkT", [2, 4, D, 256])
    din("cmv", [2, 4, 256, D])
    din("gcols", [8, 128, 8])
    din("gfinal", [128, 8])
    din("w_mq", [2, 2, 128, 8, 512])
    din("w_mk", [2, 2, 128, 8, 512])
    din("w_mv", [2, 2, 128, 8, 512])
    din("w_mo", [2, 2, 128, 8, 512])
    din("w_ffi", [2, 12, 128, 8, 512])
    din("w_ffo", [2, 6, 128, 4, 1024])
    din("cstb", [128, 2320])
    din("cstf", [128, 21])
    din("w_sbi", [6, 128, 8, 512])
    din("w_sbo", [2, 128, 8, 512])
    if cfg.get("odd", True):
        din("poolk", [2560 * 128, 1024])
        din("poolv", [2560 * 128, 1024])
        din("pt", [256], I32)
    din("cstg", [128, 4 * 128])
    din("cstm", [128, 8 * 128])
    din("w_bg", [128, 8, 8])
    din("dtb", [128, 8])
    din("convw", [128, 12, 4])
    din("gnrm", [128, 1])
    din("w_gdn", [4, 128, 8, 512])
    din("w_evo", [2, 128, 8, 512])
    din("rwp", [128, 42])
    din("w2a", [128, 4, 128])
    din("g2", [128, 4, 128])
    din("w_rwl", [128, 8, 512])
    din("w_rwp", [4, 128, 8, 512])
    din("shiftT", [1792, 4])
    din("rwkv_s0T", [4, 4, 128, 64])
    dout("o_shiftT", [1792, 1])
    dout("o_shiftTs", [1792, 4])
    dout("o_rwkvT_p", [4, 128, 64])
    dout("o_rwkvT_s", [4, 4, 128, 64])
    din("convT", [4, 1536, 3])
    din("gdn_s0", [4, 4, 128, 128])
    dout("o_convT", [1536, 3])
    dout("o_convTs", [4, 1536, 3])
    dout("o_gdn_p", [4, 128, 128])
    dout("o_gdn_s", [4, 4, 128, 128])
    dout("o_sbkT", [D, NT])
    dout("o_sbv", [NT, D])
    dout("o_yT", [D, NT])
    dout("o_memkT", [2, D, 256])
    dout("o_memv", [2, 256, D])
    dr = kb.dr
    with kb.stack:
        st = kb.stack
        kb.X = kb.sb("X", [128, 8, NT], F32)
        kb.ONES = kb.sb("ONES", [128, 128], BF16)
        CB = kb.sb("CB", [128, 2320], BF16)
        CF = kb.sb("CF", [128, 21], F32)
        CG = kb.sb("CG", [128, 4 * 128], F32)
        CM = kb.sb("CM", [128, 8 * 128], BF16)
        kb.IDF, kb.TRIF, kb.ONESF, kb.NEGF = CG[:, 0:128], CG[:, 128:256], CG[:, 256:384], CG[:, 384:512]
        kb.MSU, kb.MIU, kb.MSL = CM[:, 0:128], CM[:, 128:256], CM[:, 256:384]
        kb.XSU, kb.XIU, kb.XIUN, kb.XSLN, kb.BLK64 = CM[:, 384:512], CM[:, 512:640], CM[:, 640:768], CM[:, 768:896], CM[:, 896:1024]
        kb.IDENT, kb.TRIU, kb.TRIL, kb.NEGM = CB[:, 0:128], CB[:, 128:256], CB[:, 256:384], CB[:, 384:1280]
        kb.BLKM, kb.SEL, kb.MASK8 = CB[:, 1280:2304], CB[:, 2304:2312], CB[:, 2312:2320]
        kb.PIDX, kb.ONEC, kb.EPSC, kb.BIASQ, kb.BIASH, kb.GNEPS = CF[:, 0:1], CF[:, 1:2], CF[:, 2:3], CF[:, 3:4], CF[:, 4:20], CF[:, 20:21]
        GC = kb.sb("GC", [128, 9, 8], F32)
        kb.WS = [kb.sb(f"WS{i}", [128, 8, 512], BF16) for i in range(3)]
        kb.PS = [st.enter_context(nc.psum_tensor(f"PS{i}", [128, 512], F32)) for i in range(8)]
        kb.ARENA = kb.sb("ARENA", [128, ARENA_BYTES // 4], F32)
        kb.ARENA_BYTES = ARENA_BYTES
        kb.G = {"mix": [GC[:, 0, :], GC[:, 1, :]], "mem": [GC[:, 2, :], GC[:, 3, :]],
                "memtok": [GC[:, 4, :], GC[:, 5, :]], "ffn": [GC[:, 6, :], GC[:, 7, :]], "final": GC[:, 8, :]}
        kb.memset(kb.ONES[:, :], 1.0, [("ones",)])
        kb.dma(CB[:, :], dr["cstb"], [], [("const",)], "const", eng="pool", max_dma_last_dim=4096)
        kb.dma(CF[:, :], dr["cstf"], [], [("constf",), ("par",)], "constf")
        kb.dma(CG[:, :], dr["cstg"], [], [("constf",)], "constf")
        kb.dma(CM[:, :], dr["cstm"], [], [("const",)], "const", eng="pool", max_dma_last_dim=4096)
        for i in range(8):
            kb.dma(GC[:, i, :], dr["gcols"][i], [], [("par",)], "par")
        kb.dma(GC[:, 8, :], dr["gfinal"], [], [("par",)], "par")
        for k in range(8):
            kb.dma(kb.X[:, k, :], dr["xT"][k * 128:(k + 1) * 128, :], [], [("X", k, g) for g in range(len(TILES))], "xin")
        S.barrier()
        dbg = cfg.get("dbg", "all")
        if dbg == "A":
            for k in range(8):
                kb.dma(dr["o_yT"][k * 128:(k + 1) * 128, :], kb.X[:, k, :], [("X", k, g) for g in range(len(TILES))], [], "yo")
        elif dbg == "B":
            kb.final_phase()
        elif dbg == "C":
            kb.mem_phase(0)
            kb.final_phase()
        elif dbg == "D":
            kb.ffn_phase(0)
            kb.final_phase()
        elif dbg == "E":
            kb.even_phase()
            kb.final_phase()
        elif dbg == "O":
            kb.odd_phase()
            kb.final_phase()
        else:
            for L in range(2):
                if L == 0 and cfg.get("even", True):
                    kb.even_phase()
                if L == 1 and cfg.get("odd", True):
                    kb.odd_phase()
                kb.mem_phase(L)
                kb.ffn_phase(L)
            kb.final_phase()
        S.emit(st)
    return nc


def prep_inputs(inp, c):
    f = np.float32
    m = {}
    xs = inp["x_sample"][4 * c:4 * c + 4].reshape(NS, D)
    m["xT"] = np.ascontiguousarray(np.concatenate([inp["x_prompt"][c][:NP], xs], axis=0).T.astype(f))
    m["memT"] = np.ascontiguousarray(inp["mem_prompt"][c].T)
    m["cmkT"] = np.ascontiguousarray(inp["cache_mem_k"][:, 4 * c:4 * c + 4].reshape(2, 4, 256, D).transpose(0, 1, 3, 2))
    m["cmv"] = np.ascontiguousarray(inp["cache_mem_v"][:, 4 * c:4 * c + 4].reshape(2, 4, 256, D))
    m["convT"] = np.ascontiguousarray(inp["state_gdn_conv"][0, 4 * c:4 * c + 4].transpose(0, 2, 1))
    m["gdn_s0"] = np.ascontiguousarray(inp["state_gdn"][0, 4 * c:4 * c + 4])
    m["shiftT"] = np.ascontiguousarray(inp["state_rwkv_shift"][0, 4 * c:4 * c + 4].T)
    sr = inp["state_rwkv"][0, 4 * c:4 * c + 4]
    m["rwkv_s0T"] = np.ascontiguousarray(sr.transpose(0, 1, 3, 2).reshape(4, 4, 128, 64))
    if "page_table" in inp:
        m["pt"] = np.ascontiguousarray(inp["page_table"][4 * c:4 * c + 4].reshape(256).astype(np.int32))
    return m


def prep_pools(inp):
    pk = inp["cache_sb_k"][0].reshape(2560, 128, 8, 128)
    pk = np.ascontiguousarray(pk.transpose(0, 3, 2, 1)).reshape(2560 * 128, 1024)
    pv = np.ascontiguousarray(inp["cache_sb_v"][0]).reshape(2560 * 128, 1024)
    return pk, pv


def consts(sb_bias):
    cb = np.zeros((128, 2320), np.float32)
    k = np.arange(128)
    cb[:, 0:128] = np.eye(128)
    cb[:, 128:256] = (k[:, None] > k[None, :])
    cb[:, 256:384] = (k[:, None] <= k[None, :])
    c = np.arange(896)
    cb[:, 384:1280] = np.where(k[:, None] < c[None, :] - 384, 0.0, -1600.0)
    hq = k // 8
    hp = np.arange(1024) // 64
    cb[:, 1280:2304] = (hq[:, None] == hp[None, :])
    cb[:, 2304:2312] = ((k % 8)[:, None] == np.arange(8)[None, :])
    cb[:, 2312:2320] = (np.arange(8)[None, :] < (k % 8)[:, None])
    cf = np.zeros((128, 21), np.float32)
    cf[:, 20] = 64e-5
    cf[:, 0] = k
    cf[:, 1] = 1.0
    cf[:, 2] = EPS
    cf[:, 3] = np.repeat(sb_bias, 8)
    cf[:, 4:20] = np.broadcast_to(sb_bias[None, :], (128, 16))
    return cb, cf


def consts2():
    k = np.arange(128)
    cg = np.zeros((128, 512), np.float32)
    cg[:, 0:128] = np.eye(128)
    cg[:, 128:256] = (k[:, None] <= k[None, :])
    cg[:, 256:384] = 1.0
    cg[:, 384:512] = -1.0
    cm = np.zeros((128, 1024), np.float32)
    r, c = k[:, None], k[None, :]
    NEG = -10000.0
    cm[:, 0:128] = np.where(c > r, 0.0, NEG)
    cm[:, 128:256] = np.where(c >= r, 0.0, NEG)
    cm[:, 256:384] = np.where(r > c, 0.0, NEG)
    cm[:, 384:512] = np.where(c > r, -1.0, 0.0)
    cm[:, 512:640] = np.where(c >= r, 1.0, 0.0)
    cm[:, 640:768] = np.where(c >= r, -1.0, 0.0)
    cm[:, 768:896] = np.where(r > c, -1.0, 0.0)
    cm[:, 896:1024] = ((r // 64) == (c // 64))
    return cg, cm


def prep_shared(inp):
    m = {}
    m["cstb"], m["cstf"] = consts(np.asarray(inp["sb_bias"][0], np.float32))
    m["cstg"], m["cstm"] = consts2()
    wi = inp["ev_w_in"][0]
    m["w_bg"] = np.ascontiguousarray(wi[:, 2048:2056].reshape(8, 128, 8).transpose(1, 0, 2))
    dtb = np.zeros((128, 8), np.float32)
    dtb[:, 0:4] = inp["gdn_dt_bias"][0][None, :]
    dtb[:, 4:8] = inp["gdn_a_log"][0][None, :]
    m["dtb"] = dtb
    cw = inp["gdn_conv_w"][0]
    m["convw"] = np.ascontiguousarray(cw.reshape(4, 12, 128).transpose(2, 1, 0))
    m["gnrm"] = np.ascontiguousarray(inp["gdn_norm"][0].reshape(128, 1))
    gcols = []
    for h in range(4):
        cols = np.concatenate([np.arange(j * 512 + h * 128, j * 512 + (h + 1) * 128) for j in range(4)])
        gcols.append(wi[:, cols])
    m["w_gdn"] = np.stack([w_slabs(g, [(0, 512)])[0] for g in gcols])
    m["w_evo"] = w_slabs(inp["ev_w_out"][0], [(0, 512), (512, 1024)])
    RB = 2056
    wr = wi[:, RB:RB + 1792]
    m["w_rwl"] = w_slabs(wr, [(1536, 1792)])[0]
    pc = []
    for p in range(4):
        cols = np.concatenate([np.arange(j * 512 + p * 128, j * 512 + (p + 1) * 128) for j in range(3)])
        pc.append(w_slabs(wr[:, cols], [(0, 384)])[0])
    m["w_rwp"] = np.stack(pc)
    rwp = np.zeros((128, 42), np.float32)
    mu = inp["rwkv_mu"][0]
    for p in range(4):
        for j in range(3):
            rwp[:, p * 3 + j] = mu[j * 512 + p * 128: j * 512 + (p + 1) * 128]
    rwp[:, 12] = mu[1536:1664]
    rwp[:, 13] = mu[1664:1792]
    for wi_, key in enumerate(["rwkv_w0", "rwkv_a0", "rwkv_k_k", "rwkv_k_a", "rwkv_r_k", "rwkv_gn_g", "rwkv_gn_b"]):
        rwp[:, 14 + wi_ * 4: 18 + wi_ * 4] = col_param(inp[key][0], 4)
    m["rwp"] = rwp
    w2a = np.zeros((128, 4, 128), np.float32)
    w2a[0:64] = inp["rwkv_w2"][0].reshape(64, 4, 128)
    w2a[64:128] = inp["rwkv_a2"][0].reshape(64, 4, 128)
    m["w2a"] = w2a
    m["g2"] = np.ascontiguousarray(inp["rwkv_g2"][0].reshape(128, 4, 128))
    m["w_sbi"] = w_slabs(inp["sb_w_in"][0], [(i * 512, (i + 1) * 512) for i in range(6)])
    m["w_sbo"] = w_slabs(inp["sb_w_out"][0], [(0, 512), (512, 1024)])
    g = [inp["norm_mix"][0], inp["norm_mix"][1], inp["norm_mem"][0], inp["norm_mem"][1],
         inp["norm_memtok"][0], inp["norm_memtok"][1], inp["norm_ffn"][0], inp["norm_ffn"][1]]
    m["gcols"] = np.stack([col_param(v, 8) for v in g])
    m["gfinal"] = col_param(inp["norm_final"], 8)
    r2 = [(0, 512), (512, 1024)]
    for nm, key in (("w_mq", "mem_w_q"), ("w_mk", "mem_w_k"), ("w_mv", "mem_w_v"), ("w_mo", "mem_w_o")):
        m[nm] = np.stack([w_slabs(inp[key][L], r2) for L in range(2)])
    rg = [(i * 512, min((i + 1) * 512, DFF)) for i in range(6)]
    rr = rg + [(DFF + a, DFF + b) for a, b in rg]
    m["w_ffi"] = np.stack([w_slabs(inp["ffn_w_in"][L], rr) for L in range(2)])
    wo = np.zeros((2, 6, 128, 4, 1024), np.float32)
    for L in range(2):
        w = inp["ffn_w_out"][L].reshape(22, 128, D)
        for hg in range(6):
            nch = 4 if hg < 5 else 2
            wo[L, hg, :, :nch, :] = w[hg * 4:hg * 4 + nch].transpose(1, 0, 2)
    m["w_ffo"] = wo
    return m


def odd_phase(self):
    dr = self.dr
    S = self.S
    nc = self.nc
    self.arena_reset()
    self.ps_n = 4
    HG = self.arena([128, 8, 512], BF16)
    QT = self.arena([128, 8, 512], BF16)
    OTA = self.arena([128, 8, 512], BF16)
    tmp = self.norm_tmp(sq=QT, sqkeys=[("qt", k) for k in range(8)])
    STG = [self.arena([128, 512], F32) for _ in range(2)]
    mark = self.ar_off
    KT = self.arena([128, 8, NP], BF16)
    nkb = NP // 128
    V = self.arena([128, nkb, 1024], BF16)
    EB = [self.arena([128, 512], BF16) for _ in range(2)]
    SPB = [self.arena([128, 512], BF16) for _ in range(2)]
    TBh = [self.arena([128, 512], BF16) for _ in range(4)]
    HGT = [HG[:, k, :] for k in range(8)]
    ATT = [self.arena([128, 512], BF16) for _ in range(2)]
    stg_i = [0]

    def stage_out(ps_ap, pskey, n, dst, bf_dst, bf_key, npart=128):
        i = stg_i[0]
        stg_i[0] ^= 1
        self.copy(STG[i][0:npart, 0:n], ps_ap, [pskey], [("stg", i)], eng="dve")
        self.copy(bf_dst, STG[i][0:npart, 0:n], [("stg", i)], [bf_key], eng="pool")
        self.dma(dst, STG[i][0:npart, 0:n], [("stg", i)], [], f"stg{i}")

    rot = self.psum
    blk = 0
    for g, (t0, n) in enumerate(TILES):
        prompt = g < len(TILES) - 1
        if not prompt:
            S.barrier()
            self.ar_off = mark
        xk = [("X", k, g) for k in range(8)]
        hk = [("hg", k) for k in range(8)]
        self.rmsnorm(lambda k: self.X[:, k, t0:t0 + n], xk, self.G["mix"][1], lambda k: HG[:, k, 0:n], hk, n, tmp)
        for s in range(2):
            slot = self.load_slab(dr["w_sbi"][s])

            def consume(j, ti, ps, pskey, s=s):
                self.copy(QT[:, s * 4 + j, 0:n], ps, [pskey], [("qt", s * 4 + j)], eng="act")
            self.proj_fm(slot, 512, None, [(hk, lambda k: HG[:, k, 0:n], n)], consume)
        if not prompt:
            KN = self.arena([128, 8, 32], BF16)
            VN = self.arena([8, 4, 1024], BF16) if False else self.arena([128, 4, 1024], BF16)
        for s in range(2):
            slot = self.load_slab(dr["w_sbi"][2 + s])

            def consume(j, ti, ps, pskey, s=s):
                oc = s * 4 + j
                dst = KT[:, oc, t0:t0 + n] if prompt else KN[:, oc, 0:n]
                stage_out(ps, pskey, n, dr["o_sbkT"][oc * 128:(oc + 1) * 128, t0:t0 + n], dst, ("kt", oc, g))
            self.proj_fm(slot, 512, None, [(hk, lambda k: HG[:, k, 0:n], n)], consume)
        for s in range(2):
            slot = self.load_slab(dr["w_sbi"][4 + s])
            w = self.WS[slot]
            if prompt:
                for tb in range(4):
                    pi, ps = self.psum()
                    for k in range(8):
                        self.mm(ps[:, :], HG[:, k, tb * 128:(tb + 1) * 128], w[:, k, :], k == 0, k == 7,
                                [("w", slot), ("hg", k)], [("ps", pi)])
                    kb = g * 4 + tb
                    stage_out(ps[:, :], ("ps", pi), 512, dr["o_sbv"][t0 + tb * 128:t0 + (tb + 1) * 128, s * 512:(s + 1) * 512],
                              V[:, kb, s * 512:(s + 1) * 512], ("v", kb, s))
            else:
                for sq in range(4):
                    pi, ps = self.psum()
                    for k in range(8):
                        self.mm(ps[0:8, :], HG[:, k, sq * 8:(sq + 1) * 8], w[:, k, :], k == 0, k == 7,
                                [("w", slot), ("hg", k)], [("ps", pi)])
                    stage_out(ps[0:8, :], ("ps", pi), 512, dr["o_sbv"][t0 + sq * 8:t0 + (sq + 1) * 8, s * 512:(s + 1) * 512],
                              VN[0:8, sq, s * 512:(s + 1) * 512], ("vn", sq, s), npart=8)
        if prompt:
            nkv = (g + 1) * 4
            S.barrier()
            blocks = [(2 * c + hh, kb) for c in range(8) for kb in range(nkv - 1, -1, -1) for hh in range(2)]
            NB = len(blocks)
            DE = 6
            EBp = [HGT[i] for i in range(6)]
            SPp = [HGT[6], HGT[7]] + EB + SPB
            TBp = [TBh[0], TBh[1], TBh[2]]
            ATp = [TBh[3], ATT[0], ATT[1]]
            zb = {}

            def stA(i):
                h, kb = blocks[i]
                c, po = h // 2, (h % 2) * 64
                pz, psz = rot()
                diag = kb >= g * 4
                self.mm(psz[:, :], KT[po:po + 64, c, kb * 128:(kb + 1) * 128], QT[po:po + 64, c, 0:512], True, not diag,
                        [("kt", c, kb // 4), ("qt", c)], [("ps", pz)])
                if diag:
                    off = 384 - 128 * (kb - g * 4)
                    self.mm(psz[:, :], self.IDENT[:, :], self.NEGM[:, off:off + 512], False, True, [("const",)], [("ps", pz)])
                e, sp = i % DE, i % DE
                self.act(EBp[e][:, :], psz[:, :], AF.Exp, [("ps", pz)], [("eb", e)], bias=self.BIASH[:, h:h + 1], scale=0.125)
                self.act(SPp[sp][:, :], EBp[e][:, :], AF.Ln, [("eb", e)], [("spb", sp)], bias=self.ONEC[:, 0:1])

            def banks(h):
                ch = h % 2
                return 4 + ch * 2, self.PS[4 + ch * 2], 5 + ch * 2, self.PS[5 + ch * 2]

            def stB1(i):
                h, kb = blocks[i]
                pc, psc, pob, pso = banks(h)
                sp, tb = i % DE, i % 3
                self.mm(psc[:, :], self.TRIU[:, :], SPp[sp][:, :], kb == nkv - 1, True, [("spb", sp), ("const",)], [("ps", pc)], skip=True)
                self.tt(TBp[tb][:, :], SPp[sp][:, :], psc[:, :], ALU.add, [("spb", sp), ("ps", pc)], [("tb", tb)])

            def stB2(i):
                h, kb = blocks[i]
                pc, psc, pob, pso = banks(h)
                sp, tb, e, at = i % DE, i % 3, i % DE, i % 3
                self.mm(psc[:, :], self.TRIL[:, :], SPp[sp][:, :], False, True, [("spb", sp), ("const",)], [("ps", pc)], skip=True)
                self.act(TBp[tb][:, :], TBp[tb][:, :], AF.Exp, [("tb", tb)], [("tb", tb)], scale=-1.0)
                self.tt(ATp[at][:, :], EBp[e][:, :], TBp[tb][:, :], ALU.mult, [("eb", e), ("tb", tb)], [("att", at)])

            def stC(i):
                h, kb = blocks[i]
                c, po = h // 2, (h % 2) * 64
                pc, psc, pob, pso = banks(h)
                oap = pso[po:po + 64, :]
                at = i % 3
                self.mm(oap, V[:, kb, h * 64:(h + 1) * 64], ATp[at][:, :], kb == nkv - 1, kb == 0,
                        [("v", kb, h // 8), ("att", at)], [("ps", pob)])
                if kb == 0:
                    self.copy(OTA[po:po + 64, c, 0:512], oap, [("ps", pob)], [("ota", c, h % 2)], eng="act")
            for i in range(NB + 4):
                if i < NB:
                    stA(i)
                if 0 <= i - 2 < NB:
                    stB1(i - 2)
                if 0 <= i - 3 < NB:
                    stB2(i - 3)
                if 0 <= i - 4 < NB:
                    stC(i - 4)
            S.barrier()
        elif self.cfg.get("osample", True):
            odd_sample(self, HG, QT, OTA, KN, VN, rot)
        for s in range(2):
            slot = self.load_slab(dr["w_sbo"][s])

            def consume(j, ti, ps, pskey, s=s):
                oc = s * 4 + j
                self.tt(self.X[:, oc, t0:t0 + n], self.X[:, oc, t0:t0 + n], ps, ALU.add, [pskey, ("X", oc, g)], [("X", oc, g)])
            rk = [("ota", k, 0) for k in range(8)] + [("ota", k, 1) for k in range(8)]
            self.proj_fm(slot, 512, None, [(rk, lambda k: OTA[:, k, 0:n], n)], consume)
    self.ps_n = 8
    S.barrier()


KB.odd_phase = odd_phase


def odd_sample(self, HG, QT, OTA, KN, VN, rot):
    dr = self.dr
    nc = self.nc
    NPG = 64
    QB = self.arena([128, 8, 128], BF16)
    E = [self.arena([128, 512], BF16) for _ in range(2)]
    SP = [self.arena([128, 512], BF16) for _ in range(2)]
    CS = [self.arena([128, 512], F32) for _ in range(2)]
    WT = [self.arena([128, 512], F32) for _ in range(2)]
    AT = [self.arena([128, 512], BF16) for _ in range(2)]
    ATTT = [self.arena([128, 4, 128], BF16) for _ in range(2)]
    KTP = [self.arena([128, 8, 128], BF16) for _ in range(8)]
    VP = [self.arena([128, 1024], BF16) for _ in range(8)]
    AM = self.arena([128, 1024], BF16)
    CAR = self.arena([128, 4], F32)
    IDX = self.arena([128, 256], I32)
    PTB = self.arena([128, 256], I32)
    self.dma(PTB[:, :], dr["pt"].partition_broadcast(128), [], [("ptb",)], "ptb")
    self.ts(IDX[:, :], PTB[:, :], 128.0, self.PIDX[:, 0:1], ALU.mult, ALU.add, [("ptb",), ("const",)], [("idx",)])
    self.memset(QB[:, :, :], 0.0, [("qb",)])
    items = []
    for sq in range(4):
        ch = [("new", None)] + [("pg", pgp) for pgp in range(15, -1, -1)]
        for ci, (kind, pgp) in enumerate(ch):
            items.append((sq, ci, kind, pgp, ci == len(ch) - 1))
    NI = len(items)
    kslots = {}
    vslots = {}
    E3 = E + [self.arena([128, 512], BF16)]
    SP3 = SP + [self.arena([128, 512], BF16)]
    AT3 = AT + [self.arena([128, 512], BF16)]
    pgk = [0]
    pgv = [0]

    def gather(dst, poolname, col, key, semname):
        self.S.op("pool", (lambda: nc.gpsimd.indirect_dma_start(
            out=dst, out_offset=None, in_=dr[poolname][:, :],
            in_offset=bass.IndirectOffsetOnAxis(ap=IDX[:, col:col + 1], axis=0))), [("idx",)], [key], dsem=semname)

    def stA(i):
        sq, ci, kind, pgp, lastc = items[i]
        b = i % 3
        nk = 8 if kind == "new" else 512
        if ci == 0:
            for c in range(8):
                for hh in range(2):
                    h = 2 * c + hh
                    self.copy(QB[hh * 64:(hh + 1) * 64, c, h * 8:(h + 1) * 8], QT[hh * 64:(hh + 1) * 64, c, sq * 8:(sq + 1) * 8],
                              [("qt", c)], [("qb",)], eng="pool")
        pz, psz = rot()
        if kind == "new":
            for c in range(8):
                self.mm(psz[:, 0:8], QB[:, c, :], KN[:, c, sq * 8:(sq + 1) * 8], c == 0, c == 7,
                        [("qb",), ("kt", c, len(TILES) - 1)], [("ps", pz)])
        else:
            for jj in range(4):
                j = pgp * 4 + jj
                sl = pgk[0] % 8
                pgk[0] += 1
                gather(KTP[sl].rearrange("p a b -> p (a b)"), "poolk", sq * NPG + j, ("ktp", sl), f"ktp{sl}")
                for c in range(8):
                    self.mm(psz[:, jj * 128:(jj + 1) * 128], QB[:, c, :], KTP[sl][:, c, :], c == 0, c == 7,
                            [("qb",), ("ktp", sl)], [("ps", pz)])
        self.act(E3[b][:, 0:nk], psz[:, 0:nk], AF.Exp, [("ps", pz)], [("e", b)], bias=self.BIASQ[:, 0:1], scale=0.125)
        if kind == "new":
            self.tt(E3[b][:, 0:8], E3[b][:, 0:8], self.MASK8[:, :], ALU.mult, [("e", b), ("const",)], [("e", b)])
        self.act(SP3[b][:, 0:nk], E3[b][:, 0:nk], AF.Ln, [("e", b)], [("sp", b)], bias=self.ONEC[:, 0:1])

    def stB(i):
        sq, ci, kind, pgp, lastc = items[i]
        b, b2 = i % 3, i % 2
        nk = 8 if kind == "new" else 512
        if kind != "new":
            sl_list = []
            for jj in range(4):
                j = pgp * 4 + jj
                sl = pgv[0] % 8
                pgv[0] += 1
                sl_list.append(sl)
                gather(VP[sl][:, :], "poolv", sq * NPG + j, ("vp", sl), f"vp{sl}")
            vslots[i] = sl_list
        if ci == 0:
            self.memset(CAR[:, 0:1], 0.0, [("car",)], eng="dve")
        self.S.op("dve", (lambda: nc.vector.tensor_tensor_scan(
            CS[b2][:, 0:nk], self.ONEC[:, 0:1].to_broadcast([128, nk]), SP3[b][:, 0:nk], 0.0, ALU.mult, ALU.add)),
            [("sp", b), ("const",)], [("cs", b2)])
        self.tt(CAR[:, 1:2], CAR[:, 0:1], CS[b2][:, nk - 1:nk], ALU.add, [("car",), ("cs", b2)], [("car1",)])
        self.ts(CAR[:, 2:3], CAR[:, 1:2], -1.0, None, ALU.mult, None, [("car1",)], [("car2",)])
        self.act(WT[b2][:, 0:nk], CS[b2][:, 0:nk], AF.Exp, [("cs", b2), ("car2",)], [("wt", b2)], bias=CAR[:, 2:3])
        self.copy(CAR[:, 0:1], CAR[:, 1:2], [("car1",), ("car2",)], [("car",)])
        self.tt(AT3[b][:, 0:nk], E3[b][:, 0:nk], WT[b2][:, 0:nk], ALU.mult, [("e", b), ("wt", b2)], [("at", b)])

    def stC(i):
        sq, ci, kind, pgp, lastc = items[i]
        b, b2 = i % 3, i % 2
        fa, fb = (4, 5) if sq % 2 == 0 else (6, 7)
        psfa, psfb = self.PS[fa], self.PS[fb]
        pt_, pst = rot()
        pstb = pst[:].bitcast(BF16)
        nblk = 1 if kind == "new" else 4
        kk = 8 if kind == "new" else 128
        for jj in range(nblk):
            self.S.op("pe", (lambda jj=jj: nc.tensor.transpose(
                pstb[0:kk, jj * 128:(jj + 1) * 128], AT3[b][:, jj * 128:jj * 128 + kk], self.IDENT[:, :])),
                [("at", b), ("const",)], [("ps", pt_)])
        self.copy(ATTT[b2][0:kk, 0:nblk, :], pstb[0:kk, 0:nblk * 128].rearrange("p (a b) -> p a b", b=128),
                  [("ps", pt_)], [("attt", b2)], eng="dve")
        for jj in range(nblk):
            last = lastc and (jj == nblk - 1)
            first = (ci == 0) and (jj == 0)
            if kind == "new":
                rv = lambda half: VN[0:8, sq, half * 512:(half + 1) * 512]
                rk = [("vn", sq, 0), ("vn", sq, 1)]
            else:
                sl = vslots[i][jj]
                rv = lambda half, sl=sl: VP[sl][:, half * 512:(half + 1) * 512]
                rk = [("vp", sl)]
            self.mm(psfa[:, :], ATTT[b2][0:kk, jj, :], rv(0), first, last, [("attt", b2)] + rk, [("ps", fa)])
            self.mm(psfb[:, :], ATTT[b2][0:kk, jj, :], rv(1), first, last, [("attt", b2)] + rk, [("ps", fb)])
        if lastc:
            self.tt(AM[:, 0:512], psfa[:, :], self.BLKM[:, 0:512], ALU.mult, [("ps", fa), ("const",)], [("am", 0)])
            self.tt(AM[:, 512:1024], psfb[:, :], self.BLKM[:, 512:1024], ALU.mult, [("ps", fb), ("const",)], [("am", 1)])
            po_, pso = rot()
            for c in range(8):
                self.mm(pso[:, c * 8:(c + 1) * 8], AM[:, c * 128:(c + 1) * 128], self.SEL[:, :], True, True,
                        [("am", c // 4), ("const",)], [("ps", po_)])
            self.copy(OTA[:, :, sq * 8:(sq + 1) * 8], pso[:, 0:64].rearrange("p (a b) -> p a b", b=8), [("ps", po_)],
                      [("ota", k, 0) for k in range(8)] + [("ota", k, 1) for k in range(8)], eng="act")
    for i in range(NI + 2):
        if i < NI:
            stA(i)
        if 0 <= i - 1 < NI:
            stB(i - 1)
        if 0 <= i - 2 < NI:
            stC(i - 2)


def neumann_TT(self, Mt, Nt, TT, nch, C, mk, nk, tk, tag, TTb=None, xw_ttb=(), xw_tt=()):
    tbk = ("ttb", tag)
    if TTb is None:
        TTb = TT
        tbk = tk
        ident = self.IDF
    else:
        ident = self.IDENT
    for c in range(nch):
        self.tt(TTb[0:C, c, 0:C], Mt[0:C, c, 0:C], ident[0:C, 0:C], ALU.add, [mk], [tbk] + list(xw_ttb), eng="pool")
    p = 1
    while 2 * p < C:
        last = 4 * p >= C
        pn, psn = self.psum()
        pm, psm = self.psum()
        for c in range(nch):
            self.mm(psn[0:C, c * C:(c + 1) * C], Mt[0:C, c, 0:C], Nt[0:C, c, 0:C], True, True, [mk, nk], [("ps", pn)])
            if not last:
                self.mm(psm[0:C, c * C:(c + 1) * C], Nt[0:C, c, 0:C], Mt[0:C, c, 0:C], True, True, [mk, nk], [("ps", pm)])
        self.copy(Nt[0:C, 0:nch, 0:C], psn[0:C, 0:nch * C].rearrange("p (a b) -> p a b", b=C), [("ps", pn)], [nk], eng="act")
        if not last:
            self.copy(Mt[0:C, 0:nch, 0:C], psm[0:C, 0:nch * C].rearrange("p (a b) -> p a b", b=C), [("ps", pm)], [mk], eng="dve")
        pt_, pst = self.psum()
        for c in range(nch):
            self.mm(pst[0:C, c * C:(c + 1) * C], Nt[0:C, c, 0:C], TTb[0:C, c, 0:C], True, True, [nk, tbk], [("ps", pt_)])
        if last:
            self.tt(TT[0:C, 0:nch, 0:C], TTb[0:C, 0:nch, 0:C], pst[0:C, 0:nch * C].rearrange("p (a b) -> p a b", b=C), ALU.add,
                    [("ps", pt_), tbk], [tk] + list(xw_tt))
        else:
            self.tt(TTb[0:C, 0:nch, 0:C], TTb[0:C, 0:nch, 0:C], pst[0:C, 0:nch * C].rearrange("p (a b) -> p a b", b=C), ALU.add,
                    [("ps", pt_), tbk], [tbk])
        p *= 2
        yield


KB.neumann_TT = neumann_TT


def transpose_f32(self, out_ps, in_ap, kpart, reads, pkey):
    nc = self.nc
    return self.S.op("pe", lambda: nc.tensor.transpose(out_ps, in_ap, self.IDF[0:kpart, 0:kpart]), reads, [pkey])


KB.transpose_f32 = transpose_f32


def even_phase(self):
    dr = self.dr
    S = self.S
    nc = self.nc
    self.arena_reset()
    A = self.arena
    HG = A([128, 8, 512], BF16)
    OA = A([128, 8, 512], BF16)
    tmp = self.norm_tmp(sq=OA, sqkeys=[("oa", k) for k in range(8)])
    SG = A([128, 4, 128], F32)
    HISTC = A([128, 12, 3], F32)
    PR = A([128, 4, 64], F32)
    HISTR = A([128, 16], F32)
    BGW = A([128, 8, 8], BF16)
    DTB = A([128, 8], F32)
    CONVW = A([128, 12, 4], F32)
    GNRM = A([128, 1], F32)
    mark = self.ar_off
    self.dma(BGW[:, :, :], dr["w_bg"], [], [("bgw",)], "bgw", eng="pool", max_dma_last_dim=4096)
    self.dma(DTB[:, :], dr["dtb"], [], [("dtb",)], "evp")
    self.dma(CONVW[:, :, :], dr["convw"], [], [("convw",)], "evp")
    self.dma(GNRM[:, :], dr["gnrm"], [], [("gnrm",)], "evp")
    self.act(DTB[:, 4:8], DTB[:, 4:8], AF.Exp, [("dtb",)], [("dtb",)])
    self.ts(DTB[:, 4:8], DTB[:, 4:8], -1.0, None, ALU.mult, None, [("dtb",)], [("dtb",)])
    self.memset(SG[:, :, :], 0.0, [("sg", h) for h in range(4)])
    self.memset(HISTC[:, :, :], 0.0, [("histc",)])
    ntl = len(TILES)
    for g, (t0, n) in enumerate(TILES):
        prompt = g < ntl - 1
        nch, C = (4, 128) if prompt else (4, 8)
        nseg, ntok = (1, 512) if prompt else (4, 8)
        S.barrier()
        self.ar_off = mark
        xk = [("X", k, g) for k in range(8)]
        hk = [("hg", k) for k in range(8)]
        self.rmsnorm(lambda k: self.X[:, k, t0:t0 + n], xk, self.G["mix"][0], lambda k: HG[:, k, 0:n], hk, n, tmp)
        BG = A([128, 4, 8], F32)
        LB = A([128, 4, 4], F32)
        GG = A([128, 4, 4], F32)
        GCL = A([128, 4, 8], F32)
        SM = A([128, 4, 16], F32)
        pi, ps = self.psum()
        for c in range(nch):
            for k in range(8):
                self.mm(ps[0:C, c * 8:(c + 1) * 8], HG[:, k, c * C:(c + 1) * C], BGW[:, k, :], k == 0, k == 7,
                        [("hg", k), ("bgw",)], [("ps", pi)])
        psv = ps[0:C, 0:nch * 8].rearrange("p (a b) -> p a b", b=8)
        self.copy(BG[0:C, 0:nch, :], psv, [("ps", pi)], [("bg",)], eng="dve")
        self.act(LB[0:C, 0:nch, :], BG[0:C, 0:nch, 0:4], AF.Exp, [("bg",)], [("lb",)], scale=-1.0)
        self.act(LB[0:C, 0:nch, :], LB[0:C, 0:nch, :], AF.Ln, [("lb",)], [("lb",)], bias=self.ONEC[0:C, 0:1])
        self.ts(LB[0:C, 0:nch, :], LB[0:C, 0:nch, :], -1.0, None, ALU.mult, None, [("lb",)], [("lb",)])
        for c in range(nch):
            self.tt(GG[0:C, c, :], BG[0:C, c, 4:8], DTB[0:C, 0:4], ALU.add, [("bg",), ("dtb",)], [("gg",)])
        self.act(GG[0:C, 0:nch, :], GG[0:C, 0:nch, :], AF.Exp, [("gg",)], [("gg",)])
        self.act(GG[0:C, 0:nch, :], GG[0:C, 0:nch, :], AF.Ln, [("gg",)], [("gg",)], bias=self.ONEC[0:C, 0:1])
        for c in range(nch):
            self.tt(GG[0:C, c, :], GG[0:C, c, :], DTB[0:C, 4:8], ALU.mult, [("gg",), ("dtb",)], [("gg",)])
        pi, ps = self.psum()
        for c in range(nch):
            self.mm(ps[0:C, c * 8:c * 8 + 4], self.TRIF[0:C, 0:C], GG[0:C, c, :], True, True, [("gg",)], [("ps", pi)])
            self.mm(ps[0:C, c * 8 + 4:c * 8 + 8], self.ONESF[0:C, 0:C], GG[0:C, c, :], True, True, [("gg",)], [("ps", pi)])
        self.copy(GCL[0:C, 0:nch, :], ps[0:C, 0:nch * 8].rearrange("p (a b) -> p a b", b=8), [("ps", pi)], [("gcl",)], eng="dve")
        self.act(SM[0:C, 0:nch, 0:4], LB[0:C, 0:nch, :], AF.Exp, [("lb",)], [("sm",)])
        self.tt(SM[0:C, 0:nch, 12:16], GCL[0:C, 0:nch, 0:4], LB[0:C, 0:nch, :], ALU.add, [("gcl",), ("lb",), ("sm",)], [("sm",)])
        self.act(SM[0:C, 0:nch, 4:8], SM[0:C, 0:nch, 12:16], AF.Exp, [("sm",)], [("sm",)])
        self.tt(SM[0:C, 0:nch, 12:16], GCL[0:C, 0:nch, 4:8], GCL[0:C, 0:nch, 0:4], ALU.subtract, [("gcl",), ("sm",)], [("sm",)])
        self.act(SM[0:C, 0:nch, 8:12], SM[0:C, 0:nch, 12:16], AF.Exp, [("sm",)], [("sm",)])
        mark_g = self.ar_off
        if cfg_get(self, "gdn", True):
            for h in range(4):
                gdn_head(self, h, g, t0, n, nch, C, nseg, ntok, HG, OA, SG, HISTC, CONVW, GNRM, LB, GG, SM, tmp)
        else:
            for h in range(4):
                self.memset(OA[:, h, 0:n], 0.0, [("oa", h)], eng="pool")
        if cfg_get(self, "rwkv", True):
            rwkv_part(self, g, t0, n, nch, C, nseg, ntok, HG, OA, PR, HISTR, tmp, mark2=mark_g)
        else:
            for h in range(4, 8):
                self.memset(OA[:, h, 0:n], 0.0, [("oa", h)], eng="pool")
        ok = [("oa", k) for k in range(8)]
        for s in range(2):
            slot = self.load_slab(dr["w_evo"][s])

            def consume(j, ti, ps, pskey, s=s):
                oc = s * 4 + j
                self.tt(self.X[:, oc, t0:t0 + n], self.X[:, oc, t0:t0 + n], ps, ALU.add, [pskey, ("X", oc, g)], [("X", oc, g)])
            self.proj_fm(slot, 512, None, [(ok, lambda k: OA[:, k, 0:n], n)], consume)
    S.barrier()


KB.even_phase = even_phase


def cfg_get(self, k, d):
    return self.cfg.get(k, d)


def gdn_head(self, h, g, t0, n, nch, C, nseg, ntok, HG, OA, SG, HISTC, CONVW, GNRM, LB, GG, SM, tmp):
    dr = self.dr
    nc = self.nc
    A = self.arena
    ntl = len(TILES)
    prompt = g < ntl - 1
    first_alloc = not hasattr(self, "_gdnbuf") or self._gdnbuf[0] != g
    if first_alloc:
        b = {}
        b["U"] = [A([128, nseg, 3 + ntok], F32) for _ in range(3)]
        b["CV"] = [A([128, 512], F32) for _ in range(3)]
        b["ZS"] = A([128, 512], F32)
        b["SQ"] = A([128, 512], BF16)
        b["KBG"] = A([128, 4, 128], F32)
        b["KE"] = A([128, 4, 128], F32)
        b["VT"] = A([128, 4, 128], F32)
        b["YG"] = A([128, 4, 128], F32)
        b["YGB"] = A([128, 4, 128], F32)
        b["EX"] = [A([128, 4, 128], F32) for _ in range(3)]
        b["GAMB"] = A([128, 512], F32)
        b["MT"] = A([128, 4, 128], F32)
        b["NT"] = A([128, 4, 128], F32)
        b["AQ"] = A([128, 4, 128], F32)
        b["TT"] = A([128, 4, 128], F32)
        b["TBT"] = A([128, 4, 128], F32)
        b["GNT"] = A([128, 512], F32)
        b["QG"] = A([128, 512], F32)
        b["US"] = A([128, 128], F32)
        b["OT"] = A([128, 512], F32)
        b["SS"] = A([128, 128], F32)
        self._gdnbuf = (g, b)
    b = self._gdnbuf[1]
    U, CV, ZS, SQ = b["U"], b["CV"], b["ZS"], b["SQ"]
    KBG, KE, VT, YG, YGB, EX, GAMB = b["KBG"], b["KE"], b["VT"], b["YG"], b["YGB"], b["EX"], b["GAMB"]
    MT, NT_, AQ, TT, TBT, GNT, QG, US, OT, SS = b["MT"], b["NT"], b["AQ"], b["TT"], b["TBT"], b["GNT"], b["QG"], b["US"], b["OT"], b["SS"]
    r1, r2 = tmp["r1"], tmp["r2"]
    for j in range(3):
        if prompt:
            self.copy(U[j][:, 0, 0:3], HISTC[:, j * 4 + h, :], [("histc",)], [("u", j)], eng="pool")
        else:
            self.dma(U[j][:, :, 0:3], dr["convT"][:, j * 512 + h * 128: j * 512 + (h + 1) * 128, :].rearrange("s p t -> p s t"),
                     [], [("u", j)], "uh")
    slot = self.load_slab(dr["w_gdn"][h])
    hk = [("hg", k) for k in range(8)]

    def consume(j, ti, ps, pskey):
        if j < 3:
            self.copy(U[j][:, :, 3:3 + ntok], ps.rearrange("p (s t) -> p s t", s=nseg), [pskey], [("u", j)], eng="act")
        else:
            self.act(ZS[:, 0:n], ps, AF.Silu, [pskey], [("zs",)])
    self.proj_fm(slot, 512, None, [(hk, lambda k: HG[:, k, 0:n], n)], consume)
    for j in range(3):
        if prompt:
            self.copy(HISTC[:, j * 4 + h, :], U[j][:, 0, ntok:ntok + 3], [("u", j)], [("histc",)], eng="pool")
            if g == ntl - 2:
                self.dma(dr["o_convT"][j * 512 + h * 128: j * 512 + (h + 1) * 128, :], U[j][:, 0, ntok:ntok + 3], [("u", j)], [], "oc")
        else:
            self.dma(dr["o_convTs"][:, j * 512 + h * 128: j * 512 + (h + 1) * 128, :].rearrange("s p t -> p s t"),
                     U[j][:, :, ntok:ntok + 3], [("u", j)], [], "oc")
    for j in range(3):
        cv = CV[j][:, 0:n].rearrange("p (s t) -> p s t", s=nseg)
        wc = CONVW[:, j * 4 + h, :]
        self.ts(cv, U[j][:, :, 0:ntok], wc[:, 0:1], None, ALU.mult, None, [("u", j), ("convw",)], [("cv", j)])
        for tp in range(1, 4):
            self.stt(cv, U[j][:, :, tp:tp + ntok], wc[:, tp:tp + 1], cv, ALU.mult, ALU.add, [("u", j), ("cv", j), ("convw",)], [("cv", j)])
        self.act(CV[j][:, 0:n], CV[j][:, 0:n], AF.Silu, [("cv", j)], [("cv", j)])
    for j in range(2):
        self.act(SQ[:, 0:n], CV[j][:, 0:n], AF.Square, [("cv", j)], [("sq",)])
        pi, ps = self.psum()
        self.mm(ps[:, 0:n], self.ONES[:, :], SQ[:, 0:n], True, True, [("sq",)], [("ps", pi)])
        self.act(r1[:, 0:n], ps[:, 0:n], AF.Ln, [("ps", pi)], [("r1", tmp["id"])], bias=self.EPSC[:, 0:1])
        self.act(r2[:, 0:n], r1[:, 0:n], AF.Exp, [("r1", tmp["id"])], [("r2", tmp["id"])], scale=-0.5)
        if j == 0:
            self.stt(CV[0][:, 0:n], CV[0][:, 0:n], 128.0 ** -0.5, r2[:, 0:n], ALU.mult, ALU.mult, [("cv", 0), ("r2", tmp["id"])], [("cv", 0)])
        else:
            self.tt(CV[1][:, 0:n], CV[1][:, 0:n], r2[:, 0:n], ALU.mult, [("cv", 1), ("r2", tmp["id"])], [("cv", 1)])
    QN, KN, VV = CV[0], CV[1], CV[2]
    pa, psa = self.psum()
    pb, psb = self.psum()
    for c in range(nch):
        self.transpose_f32(psa[0:C, c * 128:(c + 1) * 128], KN[:, c * C:(c + 1) * C], 128, [("cv", 1)], ("ps", pa))
        self.transpose_f32(psb[0:C, c * 128:(c + 1) * 128], VV[:, c * C:(c + 1) * C], 128, [("cv", 2)], ("ps", pb))
    for c in range(nch):
        self.ts(KBG[0:C, c, :], psa[0:C, c * 128:(c + 1) * 128], SM[0:C, c, 4 + h:5 + h], None, ALU.mult, None,
                [("ps", pa), ("sm",)], [("kbg",)])
        self.ts(KE[0:C, c, :], psa[0:C, c * 128:(c + 1) * 128], SM[0:C, c, 8 + h:9 + h], None, ALU.mult, None,
                [("ps", pa), ("sm",)], [("ke",)])
    self.copy(VT[0:C, 0:nch, :], psb[0:C, 0:nch * 128].rearrange("p (a b) -> p a b", b=128), [("ps", pb)], [("vt",)], eng="act")
    for c in range(nch):
        self.ts(YG[0:C, c, 0:C], self.TRIF[0:C, 0:C], GG[0:C, c, h:h + 1], None, ALU.mult, None, [("gg",)], [("yg",)], eng="pool")
        self.stt(YGB[0:C, c, 0:C], self.IDF[0:C, 0:C], LB[0:C, c, h:h + 1], YG[0:C, c, 0:C], ALU.mult, ALU.add,
                 [("lb",), ("yg",)], [("ygb",)])
    specs = [
        ("ones", "ygb", "yg", "neg", self.MSU),
        ("ygb", "ones", "neg", "yg", self.MSL),
        ("ones", "yg", "yg", "neg", self.MIU),
    ]
    for e, (l1, r1_, l2, r2_, msk) in enumerate(specs):
        pi, ps = self.psum()
        for c in range(nch):
            def opnd(nm):
                if nm == "ones":
                    return self.ONESF[0:C, 0:C]
                if nm == "neg":
                    return self.NEGF[0:C, 0:C]
                if nm == "yg":
                    return YG[0:C, c, 0:C]
                return YGB[0:C, c, 0:C]
            o = ps[0:C, c * C:(c + 1) * C]
            self.mm(o, opnd(l1), opnd(r1_), True, False, [("yg",), ("ygb",)], [("ps", pi)])
            self.mm(o, opnd(l2), opnd(r2_), False, False, [("yg",), ("ygb",)], [("ps", pi)])
            self.mm(o, self.IDENT[0:C, 0:C], msk[0:C, 0:C], False, True, [], [("ps", pi)])
        self.act(EX[e][0:C, 0:nch, 0:C], ps[0:C, 0:nch * C].rearrange("p (a b) -> p a b", b=C), AF.Exp, [("ps", pi)], [("ex", e)])
    pi, ps = self.psum()
    for c in range(nch):
        self.mm(ps[:, c * C:(c + 1) * C], self.ONESF[0:C, :], YG[0:C, c, 0:C], True, True, [("yg",)], [("ps", pi)])
    self.act(GAMB[:, 0:n], ps[:, 0:n], AF.Exp, [("ps", pi)], [("gamb",)])
    pk_, psk = self.psum()
    pq_, psq = self.psum()
    for c in range(nch):
        self.mm(psk[0:C, c * C:(c + 1) * C], KN[:, c * C:(c + 1) * C], KN[:, c * C:(c + 1) * C], True, True, [("cv", 1)], [("ps", pk_)])
        self.mm(psq[0:C, c * C:(c + 1) * C], KN[:, c * C:(c + 1) * C], QN[:, c * C:(c + 1) * C], True, True, [("cv", 0), ("cv", 1)], [("ps", pq_)])
    v3 = lambda p_: p_[0:C, 0:nch * C].rearrange("p (a b) -> p a b", b=C)
    self.stt(MT[0:C, 0:nch, 0:C], v3(psk), -1.0, EX[0][0:C, 0:nch, 0:C], ALU.mult, ALU.mult, [("ps", pk_), ("ex", 0)], [("mt",)])
    self.stt(NT_[0:C, 0:nch, 0:C], v3(psk), -1.0, EX[1][0:C, 0:nch, 0:C], ALU.mult, ALU.mult, [("ps", pk_), ("ex", 1)], [("nt",)])
    self.tt(AQ[0:C, 0:nch, 0:C], v3(psq), EX[2][0:C, 0:nch, 0:C], ALU.mult, [("ps", pq_), ("ex", 2)], [("aq",)])
    for _ in self.neumann_TT(MT, NT_, TT, nch, C, ("mt",), ("nt",), ("tt",), "g"):
        pass
    for c in range(nch):
        self.ts(TBT[0:C, c, 0:C], TT[0:C, c, 0:C], SM[0:C, c, h:h + 1], None, ALU.mult, None, [("tt",), ("sm",)], [("tbt",)], eng="pool")
    pi, ps = self.psum()
    for c in range(nch):
        self.mm(ps[:, c * C:(c + 1) * C], KBG[0:C, c, :], TT[0:C, c, 0:C], True, True, [("kbg",), ("tt",)], [("ps", pi)])
    self.act(GNT[:, 0:n], ps[:, 0:n], AF.Identity, [("ps", pi)], [("gnt",)], scale=-1.0)
    self.tt(QG[:, 0:n], QN[:, 0:n], GAMB[:, 0:n], ALU.mult, [("cv", 0), ("gamb",)], [("qg",)], eng="pool")
    for c in range(nch):
        cs = slice(c * C, (c + 1) * C)
        if prompt:
            Sst, skey = SG[:, h, :], ("sg", h)
        else:
            Sst, skey = SS[:, :], ("ss",)
            self.dma(SS[:, :], dr["gdn_s0"][c, h], [], [("ss",)], "ss")
        pu, psu = self.psum()
        self.mm(psu[0:C, 0:128], TBT[0:C, c, 0:C], VT[0:C, c, :], True, False, [("tbt",), ("vt",)], [("ps", pu)])
        self.mm(psu[0:C, 0:128], GNT[:, cs], Sst, False, True, [("gnt",), skey], [("ps", pu)])
        self.copy(US[0:C, :], psu[0:C, 0:128], [("ps", pu)], [("us",)], eng="act")
        po, pso = self.psum()
        self.mm(pso[:, 0:C], Sst, QG[:, cs], True, False, [skey, ("qg",)], [("ps", po)])
        self.mm(pso[:, 0:C], US[0:C, :], AQ[0:C, c, 0:C], False, True, [("us",), ("aq",)], [("ps", po)])
        pc_, psc = self.psum()
        self.mm(psc[:, 0:128], KE[0:C, c, :], US[0:C, :], True, True, [("ke",), ("us",)], [("ps", pc_)])
        self.stt(Sst, Sst, GAMB[:, (c + 1) * C - 1:(c + 1) * C], psc[:, 0:128], ALU.mult, ALU.add,
                 [skey, ("gamb",), ("ps", pc_)], [skey])
        self.copy(OT[:, cs], pso[:, 0:C], [("ps", po)], [("ot",)], eng="act")
        if not prompt:
            self.dma(dr["o_gdn_s"][c, h], SS[:, :], [("ss",)], [], "sso")
    if prompt and g == ntl - 2:
        self.dma(dr["o_gdn_p"][h], SG[:, h, :], [("sg", h)], [], "sso")
    self.act(SQ[:, 0:n], OT[:, 0:n], AF.Square, [("ot",)], [("sq",)])
    pi, ps = self.psum()
    self.mm(ps[:, 0:n], self.ONES[:, :], SQ[:, 0:n], True, True, [("sq",)], [("ps", pi)])
    self.act(r1[:, 0:n], ps[:, 0:n], AF.Ln, [("ps", pi)], [("r1", tmp["id"])], bias=self.EPSC[:, 0:1], scale=1.0 / 128.0)
    self.act(r2[:, 0:n], r1[:, 0:n], AF.Exp, [("r1", tmp["id"])], [("r2", tmp["id"])], scale=-0.5)
    self.stt(OT[:, 0:n], OT[:, 0:n], GNRM[:, 0:1], r2[:, 0:n], ALU.mult, ALU.mult, [("ot",), ("r2", tmp["id"]), ("gnrm",)], [("ot",)])
    self.tt(OA[:, h, 0:n], OT[:, 0:n], ZS[:, 0:n], ALU.mult, [("ot",), ("zs",)], [("oa", h)])


def rwkv_part(self, g, t0, n, nch, C, nseg, ntok, HG, OA, PR, HISTR, tmp, mark2):
    dr = self.dr
    nc = self.nc
    A = self.arena
    S = self.S
    ntl = len(TILES)
    prompt = g < ntl - 1
    S.barrier()
    self.ar_off = mark2
    r1, r2 = tmp["r1"], tmp["r2"]
    r1k, r2k = ("r1", tmp["id"]), ("r2", tmp["id"])
    RWP = A([128, 42], F32)
    W2A = A([128, 4, 128], BF16)
    G2 = A([128, 4, 128], BF16)
    self.dma(RWP[:, :], dr["rwp"], [], [("rwp",)], "rwp")
    self.dma(W2A[:, :, :], dr["w2a"], [], [("w2a",)], "w2a", eng="pool", max_dma_last_dim=4096)
    self.dma(G2[:, :, :], dr["g2"], [], [("w2a",)], "w2a", eng="pool", max_dma_last_dim=4096)
    MU = lambda ci: RWP[:, ci:ci + 1]
    PC = lambda which, p: RWP[:, 14 + which * 4 + p: 15 + which * 4 + p]
    RL = [A([128, nseg, 1 + ntok], F32) for _ in range(3)]
    XR = [A([128, 512], F32) for _ in range(3)]
    DT = A([128, 512], F32)
    LWA = A([128, 512], BF16)
    LG = A([128, 512], BF16)
    hk = [("hg", k) for k in range(8)]
    v3 = lambda ap: ap.rearrange("p (s t) -> p s t", s=nseg)

    def load_hist(buf, bkey, ci, feat0):
        if prompt:
            if g == 0:
                self.memset(buf[:, 0, 0:1], 0.0, [bkey], eng="pool")
            else:
                self.copy(buf[:, 0, 0:1], HISTR[:, ci:ci + 1], [("histr",)], [bkey], eng="pool")
        else:
            self.dma(buf[:, :, 0:1], dr["shiftT"][feat0:feat0 + 128, :].rearrange("p (s o) -> p s o", o=1), [], [bkey], "uh",
                     allow_slow_non_contiguous=True)

    def save_hist(buf, bkey, ci, feat0):
        if prompt:
            self.copy(HISTR[:, ci:ci + 1], buf[:, 0, ntok:ntok + 1], [bkey], [("histr",)], eng="pool")
            if g == ntl - 2:
                self.dma(dr["o_shiftT"][feat0:feat0 + 128, :], buf[:, 0, ntok:ntok + 1], [bkey], [], "oc", allow_slow_non_contiguous=True)
        else:
            self.dma(dr["o_shiftTs"][feat0:feat0 + 128, :].rearrange("p (s o) -> p s o", o=1), buf[:, :, ntok:ntok + 1], [bkey], [], "oc",
                     allow_slow_non_contiguous=True)

    def shift_mix(buf, bkey, ci, out, okey):
        self.tt(v3(DT[:, 0:n]), buf[:, :, 0:ntok], buf[:, :, 1:1 + ntok], ALU.subtract, [bkey], [("dt",)])
        self.stt(v3(out), v3(DT[:, 0:n]), MU(ci), buf[:, :, 1:1 + ntok], ALU.mult, ALU.add, [("dt",), bkey, ("rwp",)], [okey])

    for j in range(2):
        load_hist(RL[j], ("rl", j), 12 + j, 1536 + j * 128)
    slot = self.load_slab(dr["w_rwl"])

    def consume(j, ti, ps, pskey):
        self.copy(RL[j][:, :, 1:1 + ntok], v3(ps), [pskey], [("rl", j)], eng="act")
    self.proj_fm(slot, 256, None, [(hk, lambda k: HG[:, k, 0:n], n)], consume)
    for j in range(2):
        save_hist(RL[j], ("rl", j), 12 + j, 1536 + j * 128)
        shift_mix(RL[j], ("rl", j), 12 + j, XR[j][:, 0:n], ("xr", j))
    self.act(LWA[0:64, 0:n], XR[0][0:64, 0:n], AF.Tanh, [("xr", 0)], [("lwa",)])
    self.copy(LWA[64:128, 0:n], XR[0][64:128, 0:n], [("xr", 0)], [("lwa",)], eng="pool")
    self.act(LG[:, 0:n], XR[1][:, 0:n], AF.Sigmoid, [("xr", 1)], [("lg",)])
    LW = A([128, 512], F32)
    AA = A([128, 512], F32)
    GT_ = A([128, 512], F32)
    KK = A([128, 512], F32)
    BB = A([128, 512], F32)
    CW = A([128, 512], F32)
    EW = [A([128, 512], F32) for _ in range(2)]
    KKW, RW, KI, BI, KEF, NBE = [A([128, 512], F32) for _ in range(6)]
    SQ = A([128, 512], BF16)
    VTr, KET, NBET, KKWT = [A([128, 4, 128], F32) for _ in range(4)]
    TT, AKKN, ARKT, NARBT, TAT = [A([128, 4, 128], F32) for _ in range(5)]
    MT, NT_, TTB = [A([128, 4, 128], BF16) for _ in range(3)]
    GTT = A([128, 512], F32)
    US = [A([128, 64], F32) for _ in range(2)]
    YT = A([128, 512], F32)
    PS_ = A([128, 64], F32)
    for p in range(4):
        feats = [p * 128, 512 + p * 128, 1024 + p * 128]
        for j in range(3):
            load_hist(RL[j], ("rl", j), p * 3 + j, feats[j])
        slot = self.load_slab(dr["w_rwp"][p])

        def consume(j, ti, ps, pskey):
            self.copy(RL[j][:, :, 1:1 + ntok], v3(ps), [pskey], [("rl", j)], eng="act")
        self.proj_fm(slot, 384, None, [(hk, lambda k: HG[:, k, 0:n], n)], consume)
        for j in range(3):
            save_hist(RL[j], ("rl", j), p * 3 + j, feats[j])
            shift_mix(RL[j], ("rl", j), p * 3 + j, XR[j][:, 0:n], ("xr", j))
        R_, K_, V_ = XR[0], XR[1], XR[2]
        pw_, psw = self.psum()
        self.mm(psw[:, 0:n], W2A[0:64, p, :], LWA[0:64, 0:n], True, True, [("w2a",), ("lwa",)], [("ps", pw_)])
        pa_, psa = self.psum()
        self.mm(psa[:, 0:n], W2A[64:128, p, :], LWA[64:128, 0:n], True, True, [("w2a",), ("lwa",)], [("ps", pa_)])
        pg_, psg = self.psum()
        self.mm(psg[:, 0:n], G2[:, p, :], LG[:, 0:n], True, True, [("w2a",), ("lg",)], [("ps", pg_)])
        self.act(LW[:, 0:n], psw[:, 0:n], AF.Sigmoid, [("ps", pw_), ("rwp",)], [("lw",)], bias=PC(0, p))
        self.ts(LW[:, 0:n], LW[:, 0:n], -float(np.exp(-0.5)), None, ALU.mult, None, [("lw",)], [("lw",)])
        self.act(AA[:, 0:n], psa[:, 0:n], AF.Sigmoid, [("ps", pa_), ("rwp",)], [("aa",)], bias=PC(1, p))
        self.copy(GT_[:, 0:n], psg[:, 0:n], [("ps", pg_)], [("gt",)], eng="act")
        self.ts(KK[:, 0:n], K_[:, 0:n], PC(2, p), None, ALU.mult, None, [("xr", 1), ("rwp",)], [("kk",)])
        self.act(SQ[:, 0:n], KK[:, 0:n], AF.Square, [("kk",)], [("sq",)])
        pi, ps = self.psum()
        self.mm(ps[:, 0:n], self.BLK64[:, :], SQ[:, 0:n], True, True, [("sq",)], [("ps", pi)])
        self.act(r1[:, 0:n], ps[:, 0:n], AF.Ln, [("ps", pi)], [r1k], bias=self.EPSC[:, 0:1])
        self.act(r2[:, 0:n], r1[:, 0:n], AF.Exp, [r1k], [r2k], scale=-0.5)
        self.tt(KK[:, 0:n], KK[:, 0:n], r2[:, 0:n], ALU.mult, [("kk",), r2k], [("kk",)])
        self.ts(DT[:, 0:n], AA[:, 0:n], -1.0, PC(3, p), ALU.add, ALU.mult, [("aa",), ("rwp",)], [("dt",)])
        self.stt(K_[:, 0:n], DT[:, 0:n], 1.0, K_[:, 0:n], ALU.add, ALU.mult, [("dt",), ("xr", 1)], [("xr", 1)])
        self.tt(BB[:, 0:n], KK[:, 0:n], AA[:, 0:n], ALU.mult, [("kk",), ("aa",)], [("bb",)])
        for c in range(nch):
            cs = slice(c * C, (c + 1) * C)
            self.S.op("dve", (lambda cs=cs: nc.vector.tensor_tensor_scan(
                CW[:, cs], self.ONEC[:, 0:1].to_broadcast([128, C]), LW[:, cs], 0.0, ALU.mult, ALU.add)),
                [("lw",)], [("cw",)])
        self.act(EW[0][:, 0:n], CW[:, 0:n], AF.Exp, [("cw",)], [("ew", 0)])
        self.tt(RW[:, 0:n], R_[:, 0:n], EW[0][:, 0:n], ALU.mult, [("xr", 0), ("ew", 0)], [("rw",)])
        self.tt(DT[:, 0:n], CW[:, 0:n], LW[:, 0:n], ALU.subtract, [("cw",), ("lw",)], [("dt",)])
        self.act(EW[1][:, 0:n], DT[:, 0:n], AF.Exp, [("dt",)], [("ew", 1)])
        self.tt(KKW[:, 0:n], KK[:, 0:n], EW[1][:, 0:n], ALU.mult, [("kk",), ("ew", 1)], [("kkw",)])
        self.act(EW[1][:, 0:n], CW[:, 0:n], AF.Exp, [("cw",), ("kkw",)], [("ew", 1)], scale=-1.0)
        self.tt(KI[:, 0:n], K_[:, 0:n], EW[1][:, 0:n], ALU.mult, [("xr", 1), ("ew", 1)], [("ki",)])
        self.tt(BI[:, 0:n], BB[:, 0:n], EW[1][:, 0:n], ALU.mult, [("bb",), ("ew", 1)], [("bi",)])
        for c in range(nch):
            cs = slice(c * C, (c + 1) * C)
            self.act(DT[:, cs], CW[:, cs], AF.Exp, [("cw",), ("dt",)], [("dt",)], bias=CW[:, (c + 1) * C - 1:(c + 1) * C], scale=-1.0)
        self.tt(KEF[:, 0:n], K_[:, 0:n], DT[:, 0:n], ALU.mult, [("xr", 1), ("dt",)], [("kef",)])
        self.stt(NBE[:, 0:n], BB[:, 0:n], -1.0, DT[:, 0:n], ALU.mult, ALU.mult, [("bb",), ("dt",)], [("nbe",)])
        for src, skey, dst, dkey in ((V_, ("xr", 2), VTr, ("vtr",)), (KEF, ("kef",), KET, ("ket",)),
                                     (NBE, ("nbe",), NBET, ("nbet",)), (KKW, ("kkw",), KKWT, ("kkwt",))):
            pi, ps = self.psum()
            for c in range(nch):
                self.transpose_f32(ps[0:C, c * 128:(c + 1) * 128], src[:, c * C:(c + 1) * C], 128, [skey], ("ps", pi))
            self.copy(dst[0:C, 0:nch, :], ps[0:C, 0:nch * 128].rearrange("p (a b) -> p a b", b=128), [("ps", pi)], [dkey], eng="act")
        pv3 = lambda p_: p_[0:C, 0:nch * C].rearrange("p (a b) -> p a b", b=C)
        v4 = lambda ap: ap.rearrange("p (a b) -> p a b", a=4)
        hb = [dict(MT=MT, NT=NT_, TTB=TTB, TT=TT, AKKN=AKKN, ARKT=ARKT, NARBT=NARBT, TAT=TAT, base={}),
              dict(MT=v4(KEF[:, 0:256].bitcast(BF16)), NT=v4(KEF[:, 256:512].bitcast(BF16)), TTB=v4(NBE[:, 0:256].bitcast(BF16)),
                   TT=v4(LW[:, :]), AKKN=v4(AA[:, :]), ARKT=v4(BB[:, :]), NARBT=v4(KK[:, :]), TAT=v4(CW[:, :]),
                   base={"mt": ("kef",), "nt": ("kef",), "ttb": ("nbe",), "tt": ("lw",), "akkn": ("aa",), "arkt": ("bb",),
                         "narbt": ("kk",), "tat": ("cw",)})]

        def head_gen(hh):
            B = hb[hh]
            bp = hh * 64
            hs = slice(bp, bp + 64)
            K = lambda nm: (nm, hh)
            used = set()

            def wk(nm):
                ks = [K(nm)]
                if nm in B["base"] and nm not in used:
                    used.add(nm)
                    ks.append(B["base"][nm])
                return ks
            specs = [(KI, ("ki",), RW, ("rw",), "ARKT", "arkt", self.XIU), (BI, ("bi",), KKW, ("kkw",), "MT", "mt", self.XSU),
                     (BI, ("bi",), RW, ("rw",), "NARBT", "narbt", self.XIUN), (KKW, ("kkw",), BI, ("bi",), "NT", "nt", self.XSLN),
                     (KKW, ("kkw",), KI, ("ki",), "AKKN", "akkn", self.XSLN)]
            for (l, lk, r, rk, dn, dk, msk) in specs:
                dst = B[dn]
                pi, ps = self.psum()
                for c in range(nch):
                    cs = slice(c * C, (c + 1) * C)
                    self.mm(ps[0:C, c * C:(c + 1) * C], l[hs, cs], r[hs, cs], True, True, [lk, rk], [("ps", pi)])
                for c in range(nch):
                    self.tt(dst[0:C, c, 0:C], ps[0:C, c * C:(c + 1) * C], msk[0:C, 0:C], ALU.mult, [("ps", pi)], wk(dk))
                yield
            xb = [B["base"]["ttb"]] if "ttb" in B["base"] else []
            xt = [B["base"]["tt"]] if "tt" in B["base"] else []
            for _ in self.neumann_TT(B["MT"], B["NT"], B["TT"], nch, C, K("mt"), K("nt"), K("tt"), ("r", hh), TTb=B["TTB"],
                                     xw_ttb=xb, xw_tt=xt):
                yield
            pi, ps = self.psum()
            for c in range(nch):
                self.mm(ps[0:C, c * C:(c + 1) * C], B["AKKN"][0:C, c, 0:C], B["TT"][0:C, c, 0:C], True, True, [K("akkn"), K("tt")], [("ps", pi)])
            self.act(B["TAT"][0:C, 0:nch, 0:C], pv3(ps), AF.Identity, [("ps", pi)], wk("tat"), scale=-1.0)
            pi, ps = self.psum()
            for c in range(nch):
                self.mm(ps[hs, c * C:(c + 1) * C], KKWT[0:C, c, hs], B["TT"][0:C, c, 0:C], True, True, [("kkwt",), K("tt")], [("ps", pi)])
            self.copy(GTT[hs, 0:n], ps[hs, 0:n], [("ps", pi)], [("gtt", hh)], eng="act")
            yield
            for c in range(nch):
                cs = slice(c * C, (c + 1) * C)
                if prompt:
                    P_, pkey = PR[hs, p, :], ("pr", p, hh)
                    if g == 0 and c == 0:
                        self.memset(PR[hs, p, :], 0.0, [pkey], eng="pool")
                else:
                    P_, pkey = PS_[hs, :], ("pss", hh)
                    self.dma(PS_[hs, :], dr["rwkv_s0T"][c, p, hs, :], [], [pkey], f"pss{hh}")
                u = US[hh]
                pu, psu = self.psum()
                self.mm(psu[0:C, 0:64], B["TAT"][0:C, c, 0:C], VTr[0:C, c, hs], True, False, [K("tat"), ("vtr",)], [("ps", pu)])
                self.mm(psu[0:C, 0:64], GTT[hs, cs], P_, False, True, [("gtt", hh), pkey], [("ps", pu)])
                self.copy(u[0:C, :], psu[0:C, 0:64], [("ps", pu)], [("us", hh)], eng="act")
                yield
                py, psy = self.psum()
                self.mm(psy[hs, 0:C], P_, RW[hs, cs], True, False, [pkey, ("rw",)], [("ps", py)])
                self.mm(psy[hs, 0:C], VTr[0:C, c, hs], B["ARKT"][0:C, c, 0:C], False, False, [("vtr",), K("arkt")], [("ps", py)])
                self.mm(psy[hs, 0:C], u[0:C, :], B["NARBT"][0:C, c, 0:C], False, True, [("us", hh), K("narbt")], [("ps", py)])
                pc_, psc = self.psum()
                self.mm(psc[hs, 0:64], KET[0:C, c, hs], VTr[0:C, c, hs], True, False, [("ket",), ("vtr",)], [("ps", pc_)])
                self.mm(psc[hs, 0:64], NBET[0:C, c, hs], u[0:C, :], False, True, [("nbet",), ("us", hh)], [("ps", pc_)])
                self.stt(P_, P_, EW[0][hs, (c + 1) * C - 1:(c + 1) * C], psc[hs, 0:64], ALU.mult, ALU.add,
                         [pkey, ("ew", 0), ("ps", pc_)], [pkey])
                self.copy(YT[hs, cs], psy[hs, 0:C], [("ps", py)], [("yt", hh)], eng="act")
                if not prompt:
                    self.dma(dr["o_rwkvT_s"][c, p, hs, :], PS_[hs, :], [pkey], [], f"pso{hh}")
                yield
            if prompt and g == ntl - 2:
                self.dma(dr["o_rwkvT_p"][p, hs, :], PR[hs, p, :], [("pr", p, hh)], [], "sso")
        gens = [head_gen(0), head_gen(1)]
        while gens:
            for gi in list(gens):
                try:
                    next(gi)
                except StopIteration:
                    gens.remove(gi)
        ytk = [("yt", 0), ("yt", 1)]
        self.copy(SQ[:, 0:n], YT[:, 0:n], ytk, [("sq",)], eng="pool")
        pi, ps = self.psum()
        self.mm(ps[:, 0:n], self.BLK64[:, :], SQ[:, 0:n], True, True, [("sq",)], [("ps", pi)])
        self.stt(YT[:, 0:n], ps[:, 0:n], -1.0 / 64.0, YT[:, 0:n], ALU.mult, ALU.add, [("ps", pi)] + ytk, ytk)
        self.act(SQ[:, 0:n], YT[:, 0:n], AF.Square, ytk, [("sq",)])
        pi, ps = self.psum()
        self.mm(ps[:, 0:n], self.BLK64[:, :], SQ[:, 0:n], True, True, [("sq",)], [("ps", pi)])
        self.act(r1[:, 0:n], ps[:, 0:n], AF.Ln, [("ps", pi)], [r1k], bias=self.GNEPS[:, 0:1], scale=1.0 / 64.0)
        self.act(r2[:, 0:n], r1[:, 0:n], AF.Exp, [r1k], [r2k], scale=-0.5)
        self.stt(YT[:, 0:n], YT[:, 0:n], PC(5, p), r2[:, 0:n], ALU.mult, ALU.mult, ytk + [r2k, ("rwp",)], ytk)
        self.stt(DT[:, 0:n], R_[:, 0:n], PC(4, p), K_[:, 0:n], ALU.mult, ALU.mult, [("xr", 0), ("xr", 1), ("rwp",)], [("dt",)])
        self.copy(SQ[:, 0:n], DT[:, 0:n], [("dt",)], [("sq",)], eng="pool")
        pi, ps = self.psum()
        self.mm(ps[:, 0:n], self.BLK64[:, :], SQ[:, 0:n], True, True, [("sq",)], [("ps", pi)])
        self.tt(DT[:, 0:n], ps[:, 0:n], V_[:, 0:n], ALU.mult, [("ps", pi), ("xr", 2)], [("dt",)])
        self.stt(YT[:, 0:n], YT[:, 0:n], PC(6, p), DT[:, 0:n], ALU.add, ALU.add, ytk + [("dt",), ("rwp",)], ytk)
        self.tt(OA[:, 4 + p, 0:n], YT[:, 0:n], GT_[:, 0:n], ALU.mult, ytk + [("gt",)], [("oa", 4 + p)])


_NC_CACHE = {}


def kernel(**inp):
    inp = {k: np.asarray(v) for k, v in inp.items()}
    ncores = 8
    if "nc" not in _NC_CACHE:
        _NC_CACHE["nc"] = build({})
    nc = _NC_CACHE["nc"]
    set_np(2048)
    shared = prep_shared(inp)
    pk, pv = prep_pools(inp)
    shared["poolk"], shared["poolv"] = pk, pv
    in_maps = []
    for c in range(ncores):
        m = prep_inputs(inp, c)
        m.update(shared)
        in_maps.append(m)
    res = run_bass_kernel_spmd(nc, in_maps, core_ids=list(range(ncores)))
    R = res.results
    f = np.float32

    def st(fn):
        return np.stack([fn(R[c]) for c in range(ncores)]).astype(f)

    def cat(fn):
        return np.concatenate([fn(R[c]) for c in range(ncores)], axis=0).astype(f)
    y_p = st(lambda r: r["o_yT"][:, :NP].T)
    y_s = cat(lambda r: r["o_yT"][:, NP:].T.reshape(4, 8, D))
    gdn_p = st(lambda r: r["o_gdn_p"])[None]
    gdn_s = cat(lambda r: r["o_gdn_s"])[None]
    conv_p = st(lambda r: r["o_convT"].T)[None]
    conv_s = cat(lambda r: r["o_convTs"].transpose(0, 2, 1))[None]
    rw_p = st(lambda r: r["o_rwkvT_p"].reshape(4, 2, 64, 64).transpose(0, 1, 3, 2).reshape(8, 64, 64))[None]
    rw_s = cat(lambda r: r["o_rwkvT_s"].reshape(4, 4, 2, 64, 64).transpose(0, 1, 2, 4, 3).reshape(4, 8, 64, 64))[None]
    sh_p = st(lambda r: r["o_shiftT"][:, 0])[None]
    sh_s = cat(lambda r: r["o_shiftTs"].T)[None]
    sbk_p = st(lambda r: r["o_sbkT"][:, :NP].T.reshape(NP, 16, 64))[None]
    sbk_s = cat(lambda r: r["o_sbkT"][:, NP:].T.reshape(4, 8, 16, 64))[None]
    sbv_p = st(lambda r: r["o_sbv"][:NP].reshape(NP, 16, 64))[None]
    sbv_s = cat(lambda r: r["o_sbv"][NP:].reshape(4, 8, 16, 64))[None]
    mk_p = np.stack([R[c]["o_memkT"].transpose(0, 2, 1).reshape(2, 256, 4, 256) for c in range(ncores)], axis=1).astype(f)
    mv_p = np.stack([R[c]["o_memv"].reshape(2, 256, 4, 256) for c in range(ncores)], axis=1).astype(f)
    return (y_p, y_s, gdn_p, gdn_s, conv_p, conv_s, rw_p, rw_s, sh_p, sh_s, sbk_p, sbk_s, sbv_p, sbv_s, mk_p, mv_p)
```
